# Optimizing a Trainium2 kernel written in Bass

```python
import jax, jax.numpy as jnp
from jax import lax
import numpy as np

D_MODEL = 1024
BATCH = 1
SEQ = 16384
DEPTH = 4

GRID_W = 64
CTX_LEN = 256
Q_BLOCK = 128
ROPE_THETA = 10000.0
EPS = 1e-6
N_MOD = 6
N_BRANCH = 3

CONV_DIM = 512
CONV_WIDTH = 31
MLA_HEADS = 8
MLA_Q_RANK = 384
MLA_KV_RANK = 256
MLA_NOPE = 64
MLA_ROPE = 32
MLA_V = 64
MLA_QK = MLA_NOPE + MLA_ROPE
GQA_HEADS = 8
GQA_KV_HEADS = 2
GQA_HEAD_DIM = 64
GQA_GROUP = GQA_HEADS // GQA_KV_HEADS
D_FF = 2816
FFN_CONV_WIDTH = 3

MLA_SCALE = MLA_QK ** -0.5
GQA_SCALE = GQA_HEAD_DIM ** -0.5

SECTION_WIDTHS = (2 * CONV_DIM, MLA_Q_RANK, MLA_KV_RANK, MLA_ROPE,
                  GQA_HEADS * GQA_HEAD_DIM, GQA_KV_HEADS * GQA_HEAD_DIM, GQA_KV_HEADS * GQA_HEAD_DIM,
                  N_BRANCH * D_MODEL)
IN_COLS = 2 * CONV_DIM + MLA_Q_RANK + MLA_KV_RANK + MLA_ROPE + (GQA_HEADS + 2 * GQA_KV_HEADS) * GQA_HEAD_DIM + N_BRANCH * D_MODEL

kernel_name = 'hybrid_conv_mla_gqa_dit_trunk'


def rms_norm(x, g):
    xf = x.astype(jnp.float32)
    y = xf * lax.rsqrt(jnp.mean(xf * xf, axis=-1, keepdims=True) + EPS)
    return (y * g.astype(jnp.float32)).astype(x.dtype)


def layer_norm(x, g, b):
    xf = x.astype(jnp.float32)
    mu = jnp.mean(xf, axis=-1, keepdims=True)
    xc = xf - mu
    var = jnp.mean(xc * xc, axis=-1, keepdims=True)
    return (xc * lax.rsqrt(var + EPS) * g.astype(jnp.float32) + b.astype(jnp.float32)).astype(x.dtype)


def modulate(h, shift, scale):
    return h * (1.0 + scale) + shift


def adaln(cond, w_mod, b_mod):
    m = jax.nn.silu(cond) @ w_mod + b_mod
    return [t[:, None, :] for t in jnp.split(m, N_MOD, axis=-1)]


def split_sections(z):
    pts, acc = [], 0
    for w in SECTION_WIDTHS[:-1]:
        acc += w
        pts.append(acc)
    return jnp.split(z, pts, axis=-1)


def axial_rope_tables(row_idx, col_idx, rot_dim):
    n_freq = rot_dim // 4
    inv = 1.0 / (ROPE_THETA ** (jnp.arange(n_freq, dtype=jnp.float32) / n_freq))
    ang_r = row_idx.astype(jnp.float32)[:, None] * inv
    ang_c = col_idx.astype(jnp.float32)[:, None] * inv
    ang = jnp.concatenate([ang_r, ang_r, ang_c, ang_c], axis=-1)
    return jnp.cos(ang), jnp.sin(ang)


def apply_axial_rope(x, cos, sin):
    x1, x2, x3, x4 = jnp.split(x, 4, axis=-1)
    rot = jnp.concatenate([-x2, x1, -x4, x3], axis=-1)
    return (x * cos[:, None, :] + rot * sin[:, None, :]).astype(x.dtype)


def depthwise_conv(x, w, b):
    k, ch = w.shape
    y = lax.conv_general_dilated(x, w[:, None, :], window_strides=(1,), padding=[(k // 2, k // 2)],
                                 dimension_numbers=('NWC', 'WIO', 'NWC'), feature_group_count=ch)
    return y + b


def attention_block(q, k, v, scale):
    s = jnp.einsum('bqhgd,bkhd->bhgqk', q, k).astype(jnp.float32) * scale
    p = jax.nn.softmax(s, axis=-1)
    return jnp.einsum('bhgqk,bkhd->bqhgd', p.astype(v.dtype), v)


def blocked_attention(q, k, v, scale):
    b, n, hkv, g, dk = q.shape
    nb = n // Q_BLOCK
    qb = q.reshape(b, nb, Q_BLOCK, hkv, g, dk).transpose(1, 0, 2, 3, 4, 5)
    out = lax.map(lambda qi: attention_block(qi, k, v, scale), qb)
    return out.transpose(1, 0, 2, 3, 4, 5).reshape(b, n, hkv, g, -1)


def project_tokens(h, p, rope_m, rope_g):
    b, n, _ = h.shape
    z = h @ p['w_in']
    z_glu, z_qa, z_kva, z_kr, z_gq, z_gk, z_gv, z_gate = split_sections(z)
    q_m = (rms_norm(z_qa, p['g_q_a']) @ p['w_q_b']).reshape(b, n, MLA_HEADS, MLA_QK)
    kv = (rms_norm(z_kva, p['g_kv_a']) @ p['w_kv_b']).reshape(b, n, MLA_HEADS, MLA_NOPE + MLA_V)
    k_nope, v_m = kv[..., :MLA_NOPE], kv[..., MLA_NOPE:]
    k_rope = jnp.broadcast_to(z_kr[:, :, None, :], (b, n, MLA_HEADS, MLA_ROPE))
    k_m = jnp.concatenate([k_nope, k_rope], axis=-1)
    q_m = rms_norm(q_m, p['g_mla_q'])
    k_m = rms_norm(k_m, p['g_mla_k'])
    if rope_m is not None:
        cos, sin = rope_m
        q_m = jnp.concatenate([q_m[..., :MLA_NOPE], apply_axial_rope(q_m[..., MLA_NOPE:], cos, sin)], axis=-1)
        k_m = jnp.concatenate([k_m[..., :MLA_NOPE], apply_axial_rope(k_m[..., MLA_NOPE:], cos, sin)], axis=-1)
    q_g = rms_norm(z_gq.reshape(b, n, GQA_HEADS, GQA_HEAD_DIM), p['g_gqa_q'])
    k_g = rms_norm(z_gk.reshape(b, n, GQA_KV_HEADS, GQA_HEAD_DIM), p['g_gqa_k'])
    v_g = z_gv.reshape(b, n, GQA_KV_HEADS, GQA_HEAD_DIM)
    if rope_g is not None:
        cos, sin = rope_g
        q_g = apply_axial_rope(q_g, cos, sin)
        k_g = apply_axial_rope(k_g, cos, sin)
    q_m = q_m[:, :, :, None, :]
    q_g = q_g.reshape(b, n, GQA_KV_HEADS, GQA_GROUP, GQA_HEAD_DIM)
    return z_glu, z_gate, (q_m, k_m, v_m), (q_g, k_g, v_g)


def conformer_conv(z_glu, p):
    a, g = jnp.split(z_glu, 2, axis=-1)
    y = a * jax.nn.sigmoid(g)
    y = depthwise_conv(y, p['conv_dw_w'], p['conv_dw_b'])
    y = jax.nn.silu(layer_norm(y, p['conv_ln_g'], p['conv_ln_b']))
    return y @ p['w_conv_out']


def merge_branches(z_glu, z_gate, o_m, o_g, p):
    b, n, _ = z_gate.shape
    br_conv = conformer_conv(z_glu, p)
    br_mla = o_m.reshape(b, n, MLA_HEADS * MLA_V) @ p['w_mla_o']
    br_gqa = o_g.reshape(b, n, GQA_HEADS * GQA_HEAD_DIM) @ p['w_gqa_o']
    gates = jax.nn.sigmoid((z_gate + p['b_gate']).astype(jnp.float32)).astype(z_gate.dtype)
    gates = gates.reshape(b, n, N_BRANCH, D_MODEL)
    merged = gates[:, :, 0] * br_conv + gates[:, :, 1] * br_mla + gates[:, :, 2] * br_gqa
    return merged @ p['w_out']


def conv_ffn(h, p):
    u = depthwise_conv(h @ p['w_up'], p['ffn_dw_w'], p['ffn_dw_b'])
    a, g = jnp.split(u, 2, axis=-1)
    return (jax.nn.silu(g) * a) @ p['w_down']


def setup_inputs(seed: int = 0) -> dict:
    key = jax.random.key(seed)
    ks = jax.random.split(key, 30)
    L, D = DEPTH, D_MODEL

    def nrm(i, shape, scale):
        return jax.random.normal(ks[i], shape, jnp.float32) * scale

    def gain(i, shape):
        return 1.0 + 0.05 * jax.random.normal(ks[i], shape, jnp.float32)

    return {
        'x': nrm(0, (BATCH, SEQ, D), 1.0),
        'c': nrm(1, (BATCH, D), 1.0),
        'ctx': nrm(2, (BATCH, CTX_LEN, D), 1.0),
        'c_ctx': nrm(3, (D,), 1.0),
        'w_mod': nrm(4, (L, D, N_MOD * D), 0.5 * D ** -0.5),
        'b_mod': nrm(5, (L, N_MOD * D), 0.01),
        'g_norm1': gain(6, (L, D)),
        'g_norm2': gain(7, (L, D)),
        'w_in': nrm(8, (L, D, IN_COLS), D ** -0.5),
        'b_gate': nrm(9, (L, N_BRANCH * D), 0.01),
        'conv_dw_w': nrm(10, (L, CONV_WIDTH, CONV_DIM), CONV_WIDTH ** -0.5),
        'conv_dw_b': nrm(11, (L, CONV_DIM), 0.01),
        'conv_ln_g': gain(12, (L, CONV_DIM)),
        'conv_ln_b': nrm(13, (L, CONV_DIM), 0.01),
        'w_conv_out': nrm(14, (L, CONV_DIM, D), CONV_DIM ** -0.5),
        'g_q_a': gain(15, (L, MLA_Q_RANK)),
        'w_q_b': nrm(16, (L, MLA_Q_RANK, MLA_HEADS * MLA_QK), MLA_Q_RANK ** -0.5),
        'g_kv_a': gain(17, (L, MLA_KV_RANK)),
        'w_kv_b': nrm(18, (L, MLA_KV_RANK, MLA_HEADS * (MLA_NOPE + MLA_V)), MLA_KV_RANK ** -0.5),
        'g_mla_q': gain(19, (L, MLA_QK)),
        'g_mla_k': gain(20, (L, MLA_QK)),
        'w_mla_o': nrm(21, (L, MLA_HEADS * MLA_V, D), (MLA_HEADS * MLA_V) ** -0.5),
        'g_gqa_q': gain(22, (L, GQA_HEAD_DIM)),
        'g_gqa_k': gain(23, (L, GQA_HEAD_DIM)),
        'w_gqa_o': nrm(24, (L, GQA_HEADS * GQA_HEAD_DIM, D), (GQA_HEADS * GQA_HEAD_DIM) ** -0.5),
        'w_out': nrm(25, (L, D, D), D ** -0.5),
        'w_up': nrm(26, (L, D, 2 * D_FF), D ** -0.5),
        'ffn_dw_w': nrm(27, (L, FFN_CONV_WIDTH, 2 * D_FF), FFN_CONV_WIDTH ** -0.5),
        'ffn_dw_b': nrm(28, (L, 2 * D_FF), 0.01),
        'w_down': nrm(29, (L, D_FF, D), D_FF ** -0.5),
    }


def reference(x, c, ctx, c_ctx, w_mod, b_mod, g_norm1, g_norm2, w_in, b_gate, conv_dw_w, conv_dw_b,
              conv_ln_g, conv_ln_b, w_conv_out, g_q_a, w_q_b, g_kv_a, w_kv_b, g_mla_q, g_mla_k, w_mla_o,
              g_gqa_q, g_gqa_k, w_gqa_o, w_out, w_up, ffn_dw_w, ffn_dw_b, w_down):
    n = x.shape[1]
    n_rows = n // GRID_W
    row_idx = jnp.repeat(jnp.arange(n_rows, dtype=jnp.int32), GRID_W)
    col_idx = jnp.tile(jnp.arange(GRID_W, dtype=jnp.int32), n_rows)
    rope_m = axial_rope_tables(row_idx, col_idx, MLA_ROPE)
    rope_g = axial_rope_tables(row_idx, col_idx, GQA_HEAD_DIM)
    xc = ctx
    for layer in range(DEPTH):
        p = dict(w_in=w_in[layer], b_gate=b_gate[layer], conv_dw_w=conv_dw_w[layer], conv_dw_b=conv_dw_b[layer],
                 conv_ln_g=conv_ln_g[layer], conv_ln_b=conv_ln_b[layer], w_conv_out=w_conv_out[layer],
                 g_q_a=g_q_a[layer], w_q_b=w_q_b[layer], g_kv_a=g_kv_a[layer], w_kv_b=w_kv_b[layer],
                 g_mla_q=g_mla_q[layer], g_mla_k=g_mla_k[layer], w_mla_o=w_mla_o[layer],
                 g_gqa_q=g_gqa_q[layer], g_gqa_k=g_gqa_k[layer], w_gqa_o=w_gqa_o[layer], w_out=w_out[layer],
                 w_up=w_up[layer], ffn_dw_w=ffn_dw_w[layer], ffn_dw_b=ffn_dw_b[layer], w_down=w_down[layer])
        sh1, sc1, g1, sh2, sc2, g2 = adaln(c, w_mod[layer], b_mod[layer])
        csh1, csc1, cg1, csh2, csc2, cg2 = adaln(c_ctx[None, :], w_mod[layer], b_mod[layer])

        h = modulate(rms_norm(x, g_norm1[layer]), sh1, sc1)
        hc = modulate(rms_norm(xc, g_norm1[layer]), csh1, csc1)
        glu, gate, (qm, km, vm), (qg, kg, vg) = project_tokens(h, p, rope_m, rope_g)
        glu_c, gate_c, (qmc, kmc, vmc), (qgc, kgc, vgc) = project_tokens(hc, p, None, None)
        o_m = blocked_attention(qm, jnp.concatenate([kmc, km], axis=1), jnp.concatenate([vmc, vm], axis=1), MLA_SCALE)
        o_g = blocked_attention(qg, jnp.concatenate([kgc, kg], axis=1), jnp.concatenate([vgc, vg], axis=1), GQA_SCALE)
        x = x + g1 * merge_branches(glu, gate, o_m, o_g, p)
        x = x + g2 * conv_ffn(modulate(rms_norm(x, g_norm2[layer]), sh2, sc2), p)

        if layer < DEPTH - 1:
            oc_m = attention_block(qmc, kmc, vmc, MLA_SCALE)
            oc_g = attention_block(qgc, kgc, vgc, GQA_SCALE)
            xc = xc + cg1 * merge_branches(glu_c, gate_c, oc_m, oc_g, p)
            xc = xc + cg2 * conv_ffn(modulate(rms_norm(xc, g_norm2[layer]), csh2, csc2), p)
    return x
```

```python
import numpy as np
import ml_dtypes
from contextlib import ExitStack
import concourse.bass as bass
import concourse.mybir as mybir
from concourse.bass_utils import run_bass_kernel_spmd

F32 = mybir.dt.float32
BF16 = mybir.dt.bfloat16
U8 = mybir.dt.uint8
AF = mybir.ActivationFunctionType
ALU = mybir.AluOpType
NPBF = ml_dtypes.bfloat16

NCORES = 8
D = 1024
SEQ = 16384
TL = SEQ // NCORES
CTX = 256
TT = TL + CTX
NK = CTX + SEQ
DEPTH = 4
EPS = 1e-6
BLOCKS = [(0, 512, 0), (512, 512, 0), (1024, 512, 0), (1536, 512, 0), (2048, 256, 1)]
NQH = 16
NKH = 10
MLA_SCALE = 96 ** -0.5
GQA_SCALE = 64 ** -0.5
C_GLU, C_QA, C_KVA, C_KR, C_GQ, C_GK, C_GV, C_GATE = 0, 1024, 1408, 1664, 1696, 2208, 2336, 2464
IN_COLS = 5536
D_FF = 2816

ENGS = ("pe", "act", "dve", "pool", "sp")
SEM_LIM = 30000
DMA_K = 12


class T:
    __slots__ = ("ap", "name")

    def __init__(self, ap, name):
        self.ap = ap
        self.name = name


class Prog:
    def __init__(self, nc, arena, arena_bytes):
        self.nc = nc
        self.ops = []
        self.last_w = {}
        self.readers = {}
        self.arena_bytes = arena_bytes
        self.arena = arena
        self.off = 0
        self.ndma = {e: 0 for e in ENGS}
        self.last_op = {e: None for e in ENGS}
        self.peak = 0
        self.dmaq = 0

    def sb(self, name, shape, dtype, parts=128):
        esz = 4 if dtype == F32 else 2
        n = int(np.prod(shape))
        nbytes = n * esz
        start = (self.off + 63) // 64 * 64
        assert start + nbytes <= self.arena_bytes, f"SBUF arena overflow {name} {start + nbytes}"
        self.off = start + nbytes
        self.peak = max(self.peak, self.off)
        ap = self.arena[0:parts, start:start + nbytes].bitcast(dtype)
        if len(shape) == 2:
            ap = ap.rearrange("p (a b) -> p a b", b=shape[1])
        elif len(shape) == 3:
            ap = ap.rearrange("p (a b c) -> p a b c", b=shape[1], c=shape[2])
        return T(ap, name)

    def mark(self):
        return self.off

    def release(self, mark):
        self.off = mark

    def _deps(self, idx, r, w):
        deps = set()
        for k in r:
            lw = self.last_w.get(k)
            if lw is not None:
                deps.add(lw)
        for k in w:
            lw = self.last_w.get(k)
            if lw is not None:
                deps.add(lw)
            for rd in self.readers.get(k, ()):
                deps.add(rd)
        for k in r:
            self.readers.setdefault(k, []).append(idx)
        for k in w:
            self.last_w[k] = idx
            self.readers[k] = []
        deps.discard(idx)
        return deps

    def op(self, eng, fn, r=(), w=()):
        idx = len(self.ops)
        deps = self._deps(idx, r, w)
        self.ops.append(dict(eng=eng, fn=fn, deps=deps, dma=False))
        self.last_op[eng] = idx
        return idx

    def dma(self, q, out, in_, r=(), w=()):
        if q is None:
            q = ("sp", "pool")[self.dmaq % 2]
            self.dmaq += 1
        idx = len(self.ops)
        deps = self._deps(idx, r, w)
        j = self.ndma[q]
        self.ndma[q] += 1
        self.ops.append(dict(eng=q, fn=lambda e: e.dma_start(out=out, in_=in_), deps=deps, dma=True, j=j))
        self.last_op[q] = idx
        return idx

    def barrier(self):
        lasts = {e: v for e, v in self.last_op.items() if v is not None}
        dmas = []
        for q in ENGS:
            cnt = 0
            for i in range(len(self.ops) - 1, -1, -1):
                o = self.ops[i]
                if o["dma"] and o["eng"] == q:
                    dmas.append(i)
                    cnt += 1
                    if cnt >= DMA_K:
                        break
        for e in ENGS:
            idx = len(self.ops)
            deps = set(v for ee, v in lasts.items() if ee != e and not self.ops[v]["dma"]) | set(dmas)
            self.ops.append(dict(eng=e, fn=None, deps=deps, dma=False))
            self.last_op[e] = idx
        self.last_w.clear()
        self.readers.clear()

    def emit(self, block, sems):
        ops = self.ops
        signaled = set()
        for o in ops:
            for d in o["deps"]:
                if not ops[d]["dma"]:
                    if ops[d]["eng"] == "pe" and o["eng"] == "pe":
                        continue
                    signaled.add(d)
        cnt = {e: 0 for e in ENGS}
        for i, o in enumerate(ops):
            if i in signaled:
                o["sig"] = cnt[o["eng"]]
                cnt[o["eng"]] += 1
        sem_iter = iter(sems)
        eng_sems = {}
        for e in ENGS:
            n_ep = cnt[e] // SEM_LIM + 1
            eng_sems[e] = [next(sem_iter) for _ in range(n_ep)]
        dma_sems = {}
        for q in ENGS:
            if self.ndma[q]:
                dma_sems[q] = [next(sem_iter) for _ in range(DMA_K)]
        per_eng = {e: [] for e in ENGS}
        for i, o in enumerate(ops):
            per_eng[o["eng"]].append(i)

        def run(e_name, e):
            waited = {x: -1 for x in ENGS}
            dwaited = set()
            for i in per_eng[e_name]:
                o = ops[i]
                need = {}
                for d in sorted(o["deps"]):
                    po = ops[d]
                    if po["dma"]:
                        if d not in dwaited:
                            dwaited.add(d)
                            q = po["eng"]
                            j = po["j"]
                            e.wait_ge(dma_sems[q][j % DMA_K], 16 * (j // DMA_K + 1))
                    else:
                        if po["eng"] == "pe" and e_name == "pe":
                            continue
                        s = po["sig"]
                        if s > waited[po["eng"]]:
                            need[po["eng"]] = max(need.get(po["eng"], -1), s)
                for pe_, s in need.items():
                    waited[pe_] = s
                    e.wait_ge(eng_sems[pe_][s // SEM_LIM], s % SEM_LIM + 1)
                if o["dma"]:
                    j = o["j"]
                    if j >= DMA_K:
                        e.wait_ge(dma_sems[e_name][j % DMA_K], 16 * (j // DMA_K))
                    ins = o["fn"](e)
                    ins.then_inc(dma_sems[e_name][j % DMA_K], 16)
                else:
                    if o["fn"] is None:
                        ins = e.nop() if "sig" in o else None
                    else:
                        ins = o["fn"](e)
                    if "sig" in o:
                        s = o["sig"]
                        ins.then_inc(eng_sems[e_name][s // SEM_LIM], 1)

        @block.tensor
        def _(e):
            run("pe", e)

        @block.scalar
        def _(e):
            run("act", e)

        @block.vector
        def _(e):
            run("dve", e)

        @block.gpsimd
        def _(e):
            run("pool", e)

        @block.sync
        def _(e):
            run("sp", e)


class Ctx:
    def __init__(self):
        self.nc = bass.Bass("TRN2", target_bir_lowering=False)
        self.es = ExitStack()
        self.ins = {}
        self.outs = {}

    def inp(self, name, shape, dt=F32):
        t = self.nc.dram_tensor(name, list(shape), dt, kind="ExternalInput").ap()
        self.ins[name] = t
        return t

    def out(self, name, shape, dt=F32):
        t = self.nc.dram_tensor(name, list(shape), dt, kind="ExternalOutput").ap()
        self.outs[name] = t
        return t

    def scratch(self, name, shape, dt):
        return self.nc.dram_tensor(name, list(shape), dt).ap()

    def start(self, arena_kb=204):
        nc, es = self.nc, self.es
        arena = es.enter_context(nc.sbuf_tensor("arena", [128, arena_kb * 1024], U8))
        psum = es.enter_context(nc.psum_tensor("psum", [128, 4096], F32))
        self.sems = [es.enter_context(nc.semaphore(f"s{i}")) for i in range(70)]
        self.block = es.enter_context(nc.Block())
        self.P = Prog(nc, arena[:], arena_kb * 1024)
        self.psum = psum[:]
        self.banks = [T(psum[:, i * 512:(i + 1) * 512], f"bank{i}") for i in range(8)]
        self.stage = None
        self.nstage = 0
        self.eps_t = self.P.sb("eps_t", [1], F32)
        self.P.op("pool", lambda e: e.memset(self.eps_t.ap, EPS), w=[self.eps_t])
        self.ncv = 0
        return self.P

    def finish(self):
        self.P.barrier()
        self.P.emit(self.block, self.sems)
        self.es.close()
        return self.nc

    def load_f32(self, name, src, shape, parts=128):
        t = self.P.sb(name, shape, F32)
        self.P.dma(None, t.ap[0:parts], src, w=[t])
        return t

    def load_bf16(self, name, src3, kc, ncols, parts=128, dst=None, dst_ap=None):
        P = self.P
        if self.stage is None:
            self.stage = [P.sb(f"stage{i}", [2048], F32) for i in range(self.nst)]
        if dst is None:
            dst = P.sb(name, [kc, ncols], BF16)
            dst_ap = dst.ap
        for k in range(kc):
            for c0 in range(0, ncols, 2048):
                cn = min(2048, ncols - c0)
                st = self.stage[self.nstage % self.nst]
                self.nstage += 1
                P.dma(None, st.ap[0:parts, 0:cn], src3[:, k, c0:c0 + cn], w=[st])
                eng = ("pool", "act", "dve")[self.ncv % 3] if self.cv_engs is None else self.cv_engs[self.ncv % len(self.cv_engs)]
                self.ncv += 1
                o = dst_ap[0:parts, k, c0:c0 + cn]
                i_ = st.ap[0:parts, 0:cn]
                if eng == "act":
                    P.op("act", lambda e, o=o, i_=i_: e.activation(out=o, in_=i_, func=AF.Copy), r=[st], w=[dst])
                elif eng == "pool":
                    P.op("pool", lambda e, o=o, i_=i_: e.tensor_copy(out=o, in_=i_), r=[st], w=[dst])
                else:
                    P.op("dve", lambda e, o=o, i_=i_: e.tensor_copy(out=o, in_=i_), r=[st], w=[dst])
        return dst

    cv_engs = None
    nst = 3


def wview(ap2d):
    return ap2d.rearrange("(kc p) n -> p kc n", p=128)


def emit_norm_mod(C, xs, n, ones_bf, Acol, Bcol, grp, hT, tmpA, sqbuf, bank_mean, rstd):
    P = C.P
    P.op("act", lambda e: e.activation(out=sqbuf.ap[:, :, 0:n], in_=xs.ap[:, :, 0:n], func=AF.Square, scale=1.0 / 32.0),
         r=[xs], w=[sqbuf])
    for kc in range(8):
        P.op("pe", lambda e, kc=kc: e.matmul(bank_mean.ap[:, 0:n], lhsT=ones_bf.ap[:, 0:128], rhs=sqbuf.ap[:, kc, 0:n],
                                             start=(kc == 0), stop=(kc == 7)), r=[sqbuf, ones_bf], w=[bank_mean])
    P.op("act", lambda e: e.activation(out=rstd.ap[:, 0:n], in_=bank_mean.ap[:, 0:n], func=AF.Ln, bias=C.eps_t.ap[:, 0:1]), r=[bank_mean, C.eps_t], w=[rstd])
    P.op("act", lambda e: e.activation(out=rstd.ap[:, 0:n], in_=rstd.ap[:, 0:n], func=AF.Exp, scale=-0.5), r=[rstd], w=[rstd])
    for kc in range(8):
        P.op("dve", lambda e, kc=kc: e.scalar_tensor_tensor(out=tmpA.ap[:, kc, 0:n], in0=xs.ap[:, kc, 0:n],
                                                            scalar=Acol.ap[:, kc, grp:grp + 1], in1=rstd.ap[:, 0:n],
                                                            op0=ALU.mult, op1=ALU.mult), r=[xs, Acol, rstd], w=[tmpA])
        P.op("pool", lambda e, kc=kc: e.tensor_scalar(out=hT.ap[:, kc, 0:n], in0=tmpA.ap[:, kc, 0:n],
                                                      scalar1=Bcol.ap[:, kc, grp:grp + 1], scalar2=None, op0=ALU.add),
             r=[tmpA, Bcol], w=[hT])


def build_pre():
    C = Ctx()
    xT = C.inp("xT", [D, TT])
    c2 = C.inp("c2", [128, 8, 2])
    w_mod = C.inp("w_mod", [D, 6 * D])
    b_modc = C.inp("b_modc", [128, 48])
    gn1c = C.inp("gn1c", [128, 8])
    gn2c = C.inp("gn2c", [128, 8])
    w_in = C.inp("w_in", [D, IN_COLS])
    gqac = C.inp("gqac", [128, 3])
    gkvac = C.inp("gkvac", [128, 2])
    w_q_b = C.inp("w_q_b", [384, 768])
    w_kv_b = C.inp("w_kv_b", [256, 1024])
    hgain = C.inp("hgain", [96, 4])
    ropeM = C.inp("ropeM", [96, 2, TT])
    ropeG = C.inp("ropeG", [96, 2, TT])
    rotm = C.inp("rotm", [96, 2, 96], BF16)
    shiftm = C.inp("shiftm", [32, 96], BF16)
    modv_o = C.out("modv", [128, 6, 8, 2])
    hT_o = C.out("hT", [D, TT], BF16)
    qT_o = C.out("qT", [NQH, 96, TT], BF16)
    kT_o = C.out("kT", [NKH, 96, TT], BF16)
    vP_o = C.out("vP", [TT, NKH, 65], BF16)
    y_o = C.out("y", [512, TT])
    P = C.start()
    eps_t = C.eps_t
    C.cv_engs = ("pool", "act")
    banks = C.banks

    ones_bf = P.sb("ones_bf", [128], BF16)
    P.op("pool", lambda e: e.memset(ones_bf.ap, 1.0), w=[ones_bf])
    rot_t = P.sb("rot_t", [2, 96], BF16)
    P.dma(None, rot_t.ap[0:96], rotm, w=[rot_t])
    shift_t = P.sb("shift_t", [96], BF16)
    P.dma(None, shift_t.ap[0:32], shiftm, w=[shift_t])
    hgain_t = C.load_f32("hgain_t", hgain, [4], parts=96)
    gqa_t = C.load_f32("gqa_t", gqac, [3])
    gkva_t = C.load_f32("gkva_t", gkvac, [2])
    gn1_t = C.load_f32("gn1_t", gn1c, [8])
    gn2_t = C.load_f32("gn2_t", gn2c, [8])
    bmod_t = C.load_f32("bmod_t", b_modc, [48])

    c2_t = C.load_f32("c2_t", c2, [8, 2])
    sc_t = P.sb("sc_t", [8, 2], F32)
    P.op("act", lambda e: e.activation(out=sc_t.ap, in_=c2_t.ap, func=AF.Silu), r=[c2_t], w=[sc_t])
    modv = P.sb("modv", [6, 8, 2], F32)
    m0 = P.mark()
    wm = [P.sb(f"wm{i}", [8, 512], F32) for i in range(2)]
    wmv = wview(w_mod)
    for g in range(12):
        wt = wm[g % 2]
        P.dma(None, wt.ap, wmv[:, :, g * 512:(g + 1) * 512], w=[wt])
        for s in range(4):
            j = g * 4 + s
            bk = banks[j % 2]
            for kc in range(8):
                P.op("pe", lambda e, bk=bk, wt=wt, kc=kc, s=s: e.matmul(bk.ap[:, 0:2], lhsT=wt.ap[:, kc, s * 128:(s + 1) * 128],
                                                                     rhs=sc_t.ap[:, kc, :], start=(kc == 0), stop=(kc == 7)),
                     r=[wt, sc_t], w=[bk])
            P.op("dve", lambda e, bk=bk, j=j: e.tensor_scalar(out=modv.ap[:, j // 8, j % 8, :], in0=bk.ap[:, 0:2],
                                                            scalar1=bmod_t.ap[:, j:j + 1], scalar2=None, op0=ALU.add),
                 r=[bk, bmod_t], w=[modv])
    P.barrier()
    P.release(m0)
    modc = P.sb("modc", [6, 8, 2], F32)
    for (dst, src, gn) in ((0, 1, gn1_t), (3, 4, gn2_t)):
        for cnd in range(2):
            P.op("dve", lambda e, dst=dst, src=src, gn=gn, cnd=cnd: e.scalar_tensor_tensor(
                out=modc.ap[:, dst, :, cnd], in0=modv.ap[:, src, :, cnd], scalar=1.0, in1=gn.ap,
                op0=ALU.add, op1=ALU.mult), r=[modv, gn], w=[modc])
    for (dst, src) in ((1, 0), (2, 2), (4, 3), (5, 5)):
        P.op("dve", lambda e, dst=dst, src=src: e.tensor_copy(out=modc.ap[:, dst], in_=modv.ap[:, src]), r=[modv], w=[modc])
    P.dma(None, modv_o, modc.ap, r=[modc])
    import os
    STOP = int(os.environ.get('PRE_STOP', '9'))
    if STOP <= 1:
        return C.finish(), C
    Acol = T(modc.ap[:, 0], "A1")
    Bcol = T(modc.ap[:, 1], "B1")

    wiv = wview(w_in)
    w_a = C.load_bf16("w_a", wiv[:, :, 0:C_GATE], 8, C_GATE)
    wqb = C.load_bf16("wqb", wview(w_q_b), 3, 768)
    wkb = P.sb("wkb", [2, 8, 96], BF16)
    P.op("pool", lambda e: e.memset(wkb.ap, 0.0), w=[wkb])
    wkv = P.sb("wkv", [2, 8, 64], BF16)
    wkbv = wview(w_kv_b).rearrange("p kc (h t) -> p kc h t", t=128)
    stg = C.stage
    for kc in range(2):
        st = stg[C.nstage % C.nst]
        C.nstage += 1
        P.dma(None, st.ap[:, 0:1024], wview(w_kv_b)[:, kc, :], w=[st])
        sv = st.ap[:, 0:1024].rearrange("p (h t) -> p h t", t=128)
        P.op("pool", lambda e, kc=kc, sv=sv: e.tensor_copy(out=wkb.ap[:, kc, :, 0:64], in_=sv[:, :, 0:64]), r=[st], w=[wkb])
        P.op("pool", lambda e, kc=kc, sv=sv: e.tensor_copy(out=wkv.ap[:, kc, :, :], in_=sv[:, :, 64:128]), r=[st], w=[wkv])

    if STOP <= 2:
        return C.finish(), C
    xs = P.sb("xs", [8, 512], F32)
    sqb = P.sb("sqb", [8, 512], BF16)
    tmpA = P.sb("tmpA", [8, 512], F32)
    hT = P.sb("hT", [8, 512], BF16)
    rstd = P.sb("rstd", [512], F32)
    sg = [P.sb(f"sg{i}", [512], F32) for i in range(2)]
    yt = [P.sb(f"yt{i}", [512], F32) for i in range(2)]
    zq = P.sb("zq", [3, 512], F32)
    sqa = P.sb("sqa", [3, 512], BF16)
    qan = P.sb("qan", [3, 512], BF16)
    zkv = P.sb("zkv", [2, 512], F32)
    sqkv = P.sb("sqkv", [2, 512], BF16)
    kvn = P.sb("kvn", [2, 512], BF16)
    zkr = P.sb("zkr", [512], BF16)
    rstd2 = P.sb("rstd2", [512], F32)
    ropeM_t = P.sb("ropeM_t", [2, 512], F32)
    ropeG_t = P.sb("ropeG_t", [2, 512], F32)
    vt = [P.sb(f"vt{i}", [NKH, 65], BF16) for i in range(2)]
    for v_ in vt:
        P.op("pool", lambda e, v_=v_: e.memset(v_.ap, 1.0), w=[v_])
    NU = 2
    u_sq = [P.sb(f"u_sq{i}", [512], BF16) for i in range(NU)]
    u_zc = [P.sb(f"u_zc{i}", [512], F32) for i in range(NU)]
    u_rs = [P.sb(f"u_rs{i}", [512], F32) for i in range(NU)]
    u_xn = [P.sb(f"u_xn{i}", [512], BF16) for i in range(NU)]
    u_t1 = [P.sb(f"u_t1{i}", [512], F32) for i in range(NU)]
    u_t2 = [P.sb(f"u_t2{i}", [512], F32) for i in range(NU)]
    u_o = [P.sb(f"u_o{i}", [512], BF16) for i in range(NU)]
    u_og = [P.sb(f"u_og{i}", [512], BF16) for i in range(NU)]
    for o_ in u_og:
        P.op("pool", lambda e, o_=o_: e.memset(o_.ap, 0.0), w=[o_])
    ucnt = [0]

    def inproj(bank, col0, m, n):
        for kc in range(8):
            P.op("pe", lambda e, kc=kc: e.matmul(bank.ap[0:m, 0:n], lhsT=w_a.ap[:, kc, col0:col0 + m], rhs=hT.ap[:, kc, 0:n],
                                                 start=(kc == 0), stop=(kc == 7)), r=[w_a, hT], w=[bank])

    def normrope(main_fn, d, gcol, rope_t, ridx, out_dram, n):
        u = ucnt[0] % NU
        ucnt[0] += 1
        bA, bB, bC = banks[3 * u], banks[3 * u + 1], banks[3 * u + 2]
        main_fn(bA)
        sq, zc, rs_, xn, t1, t2, o_ = u_sq[u], u_zc[u], u_rs[u], u_xn[u], u_t1[u], u_t2[u], (u_o[u] if d == 96 else u_og[u])
        P.op("act", lambda e: e.activation(out=sq.ap[0:d, 0:n], in_=bA.ap[0:d, 0:n], func=AF.Square, scale=float(d) ** -0.5),
             r=[bA], w=[sq])
        P.op("act", lambda e: e.activation(out=zc.ap[0:d, 0:n], in_=bA.ap[0:d, 0:n], func=AF.Copy), r=[bA], w=[zc])
        P.op("pe", lambda e: e.matmul(bB.ap[0:96, 0:n], lhsT=ones_bf.ap[0:d, 0:96], rhs=sq.ap[0:d, 0:n], start=True, stop=True),
             r=[sq, ones_bf], w=[bB])
        P.op("act", lambda e: e.activation(out=rs_.ap[0:d, 0:n], in_=bB.ap[0:d, 0:n], func=AF.Ln, bias=eps_t.ap[0:d, 0:1]), r=[bB, eps_t], w=[rs_])
        P.op("act", lambda e: e.activation(out=rs_.ap[0:d, 0:n], in_=rs_.ap[0:d, 0:n], func=AF.Exp, scale=-0.5), r=[rs_], w=[rs_])
        P.op("dve", lambda e: e.scalar_tensor_tensor(out=xn.ap[0:d, 0:n], in0=zc.ap[0:d, 0:n], scalar=hgain_t.ap[0:d, gcol:gcol + 1],
                                                     in1=rs_.ap[0:d, 0:n], op0=ALU.mult, op1=ALU.mult),
             r=[zc, rs_, hgain_t], w=[xn])
        P.op("pe", lambda e: e.matmul(bC.ap[0:96, 0:n], lhsT=rot_t.ap[0:d, ridx, 0:96], rhs=xn.ap[0:d, 0:n], start=True, stop=True),
             r=[xn, rot_t], w=[bC])
        P.op("pool", lambda e: e.tensor_tensor(out=t1.ap[0:d, 0:n], in0=xn.ap[0:d, 0:n], in1=rope_t.ap[0:d, 0, 0:n], op=ALU.mult),
             r=[xn, rope_t], w=[t1])
        P.op("dve", lambda e: e.tensor_tensor(out=t2.ap[0:d, 0:n], in0=bC.ap[0:d, 0:n], in1=rope_t.ap[0:d, 1, 0:n], op=ALU.mult),
             r=[bC, rope_t], w=[t2])
        P.op("pool", lambda e: e.tensor_tensor(out=o_.ap[0:d, 0:n], in0=t1.ap[0:d, 0:n], in1=t2.ap[0:d, 0:n], op=ALU.add),
             r=[t1, t2], w=[o_])
        P.dma(None, out_dram, o_.ap[0:96, 0:n], r=[o_])

    def do_block(t0, n, grp):
        P.dma("sp", xs.ap[:, :, 0:n], wview(xT)[:, :, t0:t0 + n], w=[xs])
        P.dma("pool", ropeM_t.ap[0:96, :, 0:n], ropeM[:, :, t0:t0 + n], w=[ropeM_t])
        P.dma("pool", ropeG_t.ap[0:96, :, 0:n], ropeG[:, :, t0:t0 + n], w=[ropeG_t])
        emit_norm_mod(C, xs, n, ones_bf, Acol, Bcol, grp, hT, tmpA, sqb, banks[6], rstd)
        P.dma(None, wview(hT_o)[:, :, t0:t0 + n], hT.ap[:, :, 0:n], r=[hT])
        if STOP <= 3:
            return
        for j in range(4):
            bg, ba = banks[6], banks[7]
            inproj(bg, C_GLU + 512 + j * 128, 128, n)
            s_ = sg[j % 2]
            P.op("act", lambda e, s_=s_, bg=bg: e.activation(out=s_.ap[:, 0:n], in_=bg.ap[:, 0:n], func=AF.Sigmoid), r=[bg], w=[s_])
            inproj(ba, C_GLU + j * 128, 128, n)
            y_ = yt[j % 2]
            P.op("dve", lambda e, y_=y_, ba=ba, s_=s_: e.tensor_tensor(out=y_.ap[:, 0:n], in0=ba.ap[:, 0:n], in1=s_.ap[:, 0:n], op=ALU.mult),
                 r=[ba, s_], w=[y_])
            P.dma(None, y_o[j * 128:(j + 1) * 128, t0:t0 + n], y_.ap[:, 0:n], r=[y_])
        if STOP <= 4:
            return
        for (zt, sqt, outn, col0, nch, dim, gt) in ((zq, sqa, qan, C_QA, 3, 384, gqa_t), (zkv, sqkv, kvn, C_KVA, 2, 256, gkva_t)):
            for j in range(nch):
                bk = banks[6 + (j % 2)]
                inproj(bk, col0 + j * 128, 128, n)
                P.op("act", lambda e, bk=bk, j=j, sqt=sqt, dim=dim: e.activation(out=sqt.ap[:, j, 0:n], in_=bk.ap[:, 0:n], func=AF.Square,
                                                                             scale=float(dim) ** -0.5), r=[bk], w=[sqt])
                P.op("act", lambda e, bk=bk, j=j, zt=zt: e.activation(out=zt.ap[:, j, 0:n], in_=bk.ap[:, 0:n], func=AF.Copy), r=[bk], w=[zt])
            bk = banks[6]
            for j in range(nch):
                P.op("pe", lambda e, j=j, sqt=sqt, bk=bk, nch=nch: e.matmul(bk.ap[:, 0:n], lhsT=ones_bf.ap[:, 0:128], rhs=sqt.ap[:, j, 0:n],
                                                                        start=(j == 0), stop=(j == nch - 1)), r=[sqt, ones_bf], w=[bk])
            P.op("act", lambda e, bk=bk: e.activation(out=rstd2.ap[:, 0:n], in_=bk.ap[:, 0:n], func=AF.Ln, bias=eps_t.ap[:, 0:1]), r=[bk, eps_t], w=[rstd2])
            P.op("act", lambda e, bk=bk: e.activation(out=rstd2.ap[:, 0:n], in_=rstd2.ap[:, 0:n], func=AF.Exp, scale=-0.5), r=[rstd2], w=[rstd2])
            for j in range(nch):
                P.op("dve", lambda e, j=j, zt=zt, outn=outn, gt=gt: e.scalar_tensor_tensor(
                    out=outn.ap[:, j, 0:n], in0=zt.ap[:, j, 0:n], scalar=gt.ap[:, j:j + 1], in1=rstd2.ap[:, 0:n],
                    op0=ALU.mult, op1=ALU.mult), r=[zt, gt, rstd2], w=[outn])
        if STOP <= 5:
            return
        bk = banks[7]
        inproj(bk, C_KR, 32, n)
        P.op("act", lambda e, bk=bk: e.activation(out=zkr.ap[0:32, 0:n], in_=bk.ap[0:32, 0:n], func=AF.Copy), r=[bk], w=[zkr])
        for h in range(8):
            def mq(bank, h=h):
                for j in range(3):
                    P.op("pe", lambda e, j=j: e.matmul(bank.ap[0:96, 0:n], lhsT=wqb.ap[:, j, h * 96:(h + 1) * 96], rhs=qan.ap[:, j, 0:n],
                                                       start=(j == 0), stop=(j == 2)), r=[wqb, qan], w=[bank])
            normrope(mq, 96, 0, ropeM_t, 0, qT_o[h, :, t0:t0 + n], n)
        if STOP <= 6:
            return
        for h in range(8):
            def mk(bank, h=h):
                for j in range(2):
                    P.op("pe", lambda e, j=j: e.matmul(bank.ap[0:96, 0:n], lhsT=wkb.ap[:, j, h, :], rhs=kvn.ap[:, j, 0:n],
                                                       start=(j == 0), stop=False), r=[wkb, kvn], w=[bank])
                P.op("pe", lambda e: e.matmul(bank.ap[0:96, 0:n], lhsT=shift_t.ap[0:32, 0:96], rhs=zkr.ap[0:32, 0:n],
                                              start=False, stop=True), r=[shift_t, zkr], w=[bank])
            normrope(mk, 96, 1, ropeM_t, 0, kT_o[h, :, t0:t0 + n], n)
        if STOP <= 7:
            return
        for h in range(8):
            def gq(bank, h=h):
                inproj(bank, C_GQ + h * 64, 64, n)
            normrope(gq, 64, 2, ropeG_t, 1, qT_o[8 + h, :, t0:t0 + n], n)
        for h in range(2):
            def gk(bank, h=h):
                inproj(bank, C_GK + h * 64, 64, n)
            normrope(gk, 64, 3, ropeG_t, 1, kT_o[8 + h, :, t0:t0 + n], n)
        if STOP <= 8:
            return
        for ti in range(n // 128):
            v_ = vt[ti % 2]
            bm, bg = banks[6], banks[7]
            for j in range(2):
                P.op("pe", lambda e, j=j, ti=ti, bm=bm: e.matmul(bm.ap[:, 0:512], lhsT=kvn.ap[:, j, ti * 128:(ti + 1) * 128],
                                                             rhs=wkv.ap[:, j].rearrange("p h d -> p (h d)"),
                                                             start=(j == 0), stop=(j == 1)), r=[kvn, wkv], w=[bm])
            for kc in range(8):
                P.op("pe", lambda e, kc=kc, ti=ti, bg=bg: e.matmul(bg.ap[:, 0:128], lhsT=hT.ap[:, kc, ti * 128:(ti + 1) * 128],
                                                               rhs=w_a.ap[:, kc, C_GV:C_GV + 128],
                                                               start=(kc == 0), stop=(kc == 7)), r=[hT, w_a], w=[bg])
            P.op("act", lambda e, v_=v_, bm=bm: e.activation(out=v_.ap[:, 0:8, 0:64], in_=bm.ap[:, 0:512].rearrange("p (h d) -> p h d", d=64),
                                                            func=AF.Copy), r=[bm], w=[v_])
            P.op("act", lambda e, v_=v_, bg=bg: e.activation(out=v_.ap[:, 8:10, 0:64], in_=bg.ap[:, 0:128].rearrange("p (h d) -> p h d", d=64),
                                                            func=AF.Copy), r=[bg], w=[v_])
            P.dma(None, vP_o[t0 + ti * 128:t0 + (ti + 1) * 128], v_.ap, r=[v_])
    for (t0_, n_, grp_) in BLOCKS:
        do_block(t0_, n_, grp_)
    return C.finish(), C


KB = 1024


def attention_phase(C, heads, NQ, key_blocks, ones_f):
    P = C.P
    banks = C.banks
    psum_all = C.psum
    m0 = P.mark()
    NBUF = 4
    HW = min(1024, NQ)
    NH_ = NQ // HW
    SW = min(512, HW)
    NS = HW // SW
    kbuf = [P.sb(f"kbuf{i}", [KB], BF16) for i in range(NBUF)]
    vbuf = [P.sb(f"vbuf{i}", [KB // 128, 65], BF16) for i in range(NBUF)]
    qbuf = [P.sb(f"qbuf{i}", [NQ], BF16) for i in range(2)]
    pT = [P.sb(f"pT{i}", [HW], BF16) for i in range(3)]
    rs = P.sb("rs", [NQ], F32)
    bcs = P.sb("bcs", [NQ], F32)
    obuf = [P.sb(f"obuf{i}", [NQ], BF16) for i in range(2)]
    Ob = banks[0:4]
    Sb = [banks[4:6], banks[6:8]]
    nblk_total = 0
    sidx = [0]
    pidx = [0]
    dk = 96
    for hi, hd in enumerate(heads):
        qb = qbuf[hi % 2]
        P.dma("sp", qb.ap[0:dk, :], hd["q"], w=[qb])
        nchunks_total = sum(n // 128 for _, n in key_blocks)
        ci = 0
        its = []
        binfo = []
        for (k0, kn) in key_blocks:
            kb_ = kbuf[nblk_total % NBUF]
            vb_ = vbuf[nblk_total % NBUF]
            nblk_total += 1
            nch = kn // 128
            binfo.append(dict(k0=k0, kn=kn, kb=kb_, vb=vb_, nch=nch, loaded=False))
            for c in range(nch):
                first = (ci == 0)
                last = (ci == nchunks_total - 1)
                ci += 1
                for half in range(NH_):
                    its.append(dict(kb=kb_, vb=vb_, c=c, kn=kn, nch=nch, half=half, first=first, last=last, blk=len(binfo) - 1))

        def load_upto(nb):
            for b in binfo[:nb + 1]:
                if not b["loaded"]:
                    b["loaded"] = True
                    k0, kn, nch = b["k0"], b["kn"], b["nch"]
                    P.dma("sp", b["kb"].ap[0:dk, 0:kn], hd["k"][:, k0:k0 + kn], w=[b["kb"]])
                    P.dma("pool", b["vb"].ap[:, 0:nch, :], hd["v"][k0:k0 + kn, :].rearrange("(p c) e -> p c e", c=nch), w=[b["vb"]])

        def emit_S(it):
            load_upto(it["blk"] + 2)
            sb2 = Sb[sidx[0] % 2]
            it["sb2"] = sb2
            it["sbase"] = (4 + 2 * (sidx[0] % 2)) * 512
            sidx[0] += 1
            for j in range(NS):
                q0 = it["half"] * HW + j * SW
                P.op("pe", lambda e, o=sb2[j].ap[:, 0:SW], l=it["kb"].ap[0:dk, it["c"]:it["kn"]:it["nch"]], r=qb.ap[0:dk, q0:q0 + SW]:
                     e.matmul(o, lhsT=l, rhs=r, start=True, stop=True), r=[it["kb"], qb], w=[sb2[j]])

        def emit_exp(it):
            pt = pT[pidx[0] % 3]
            pidx[0] += 1
            it["pt"] = pt
            if NS == 2:
                s_ap = psum_all[:, it["sbase"]:it["sbase"] + 1024]
            else:
                s_ap = psum_all[:, it["sbase"]:it["sbase"] + SW]
            P.op("act", lambda e, o=pt.ap[:, 0:HW], i=s_ap, sc=hd["scale"]: e.activation(out=o, in_=i, func=AF.Exp, scale=sc),
                 r=list(it["sb2"][0:NS]), w=[pt])

        def emit_PV(it):
            for j in range(NS):
                ob = Ob[it["half"] * NS + j]
                P.op("pe", lambda e, o=ob.ap[0:65, 0:SW], l=it["vb"].ap[:, it["c"], :], r=it["pt"].ap[:, j * SW:(j + 1) * SW],
                     f=it["first"], la=it["last"]: e.matmul(o, lhsT=l, rhs=r, start=f, stop=la), r=[it["vb"], it["pt"]], w=[ob])

        emit_S(its[0])
        emit_exp(its[0])
        for i in range(len(its)):
            if i + 1 < len(its):
                emit_S(its[i + 1])
            emit_PV(its[i])
            if i + 1 < len(its):
                emit_exp(its[i + 1])
        nob = NH_ * NS
        for j in range(nob):
            P.op("dve", lambda e, j=j: e.reciprocal(out=rs.ap[64:65, j * SW:(j + 1) * SW], in_=Ob[j].ap[64:65, 0:SW]), r=[Ob[j]], w=[rs])
        for j in range(nob):
            sbk = banks[4 + j]
            P.op("pe", lambda e, sbk=sbk, j=j: e.matmul(sbk.ap[0:64, 0:SW], lhsT=ones_f.ap[64:65, 0:64], rhs=rs.ap[64:65, j * SW:(j + 1) * SW],
                                                    start=True, stop=True), r=[rs, ones_f], w=[sbk])
            P.op("act", lambda e, sbk=sbk, j=j: e.activation(out=bcs.ap[0:64, j * SW:(j + 1) * SW], in_=sbk.ap[0:64, 0:SW], func=AF.Copy),
                 r=[sbk], w=[bcs])
        ob_ = obuf[hi % 2]
        for j in range(nob):
            P.op("dve", lambda e, j=j, ob_=ob_: e.tensor_tensor(out=ob_.ap[0:64, j * SW:(j + 1) * SW], in0=Ob[j].ap[0:64, 0:SW],
                                                               in1=bcs.ap[0:64, j * SW:(j + 1) * SW], op=ALU.mult),
                 r=[Ob[j], bcs], w=[ob_])
        P.dma("sp", hd["o"], ob_.ap[0:64, :], r=[ob_])
    P.release(m0)


def build_attn():
    C = Ctx()
    qT = C.inp("qT", [NQH, 96, TT], BF16)
    kTf = C.inp("kTf", [NKH, 96, NK], BF16)
    vPf = C.inp("vPf", [NKH, NK, 65], BF16)
    hT_d = C.inp("hT", [D, TT], BF16)
    xT = C.inp("xT", [D, TT])
    yh = C.inp("yh", [512, TL + 30])
    yc = C.inp("yc", [512, CTX + 30])
    modv = C.inp("modv", [128, 6, 8, 2])
    w_in = C.inp("w_in", [D, IN_COLS])
    bgc = C.inp("bgc", [128, 24])
    cwc = C.inp("cwc", [128, 4, 31])
    cvec = C.inp("cvec", [128, 4, 3])
    w_conv_out = C.inp("w_conv_out", [512, D])
    w_mla_o = C.inp("w_mla_o", [512, D])
    w_gqa_o = C.inp("w_gqa_o", [512, D])
    w_out = C.inp("w_out", [D, D])
    xm_o = C.out("xmT", [D, TT])
    h2_o = C.out("h2T", [D, TT], BF16)
    oT_d = C.scratch("oT_d", [NQH * 64, TT], BF16)
    P = C.start()
    eps_t = C.eps_t
    banks = C.banks
    ones_f = P.sb("ones_f", [64], F32)
    P.op("pool", lambda e: e.memset(ones_f.ap, 1.0), w=[ones_f])
    lat_blocks = [(0, CTX)] + [(CTX + b * KB, KB) for b in range(SEQ // KB)]
    heads_l, heads_c = [], []
    for h in range(NQH):
        kvh = h if h < 8 else 8 + (h - 8) // 4
        sc = MLA_SCALE if h < 8 else GQA_SCALE
        heads_l.append(dict(q=qT[h, :, 0:TL], k=kTf[kvh], v=vPf[kvh], o=oT_d[h * 64:(h + 1) * 64, 0:TL], scale=sc))
        heads_c.append(dict(q=qT[h, :, TL:TT], k=kTf[kvh], v=vPf[kvh], o=oT_d[h * 64:(h + 1) * 64, TL:TT], scale=sc))
    attention_phase(C, heads_c, CTX, [(0, CTX)], ones_f)
    attention_phase(C, heads_l, TL, lat_blocks, ones_f)
    P.barrier()
    C.cv_engs = ("pool", "act", "dve")
    C.nst = 2
    NB = 256
    ones_bf = P.sb("ones_bf", [128], BF16)
    P.op("pool", lambda e: e.memset(ones_bf.ap, 1.0), w=[ones_bf])
    ones512 = P.sb("ones512", [128], BF16)
    P.op("pool", lambda e: e.memset(ones512.ap, 1.0 / 512.0), w=[ones512])
    modc = C.load_f32("modc", modv, [6, 8, 2])
    G1 = T(modc.ap[:, 2], "G1")
    A2 = T(modc.ap[:, 3], "A2")
    B2 = T(modc.ap[:, 4], "B2")
    bg_t = C.load_f32("bg_t", bgc, [24])
    cw_t = C.load_f32("cw_t", cwc, [4, 31])
    cv_t = C.load_f32("cv_t", cvec, [4, 3])
    wiv = wview(w_in)
    w_g = C.load_bf16("w_g", wiv[:, :, C_GATE:IN_COLS], 8, 3072)
    w_co = C.load_bf16("w_co", wview(w_conv_out), 4, D)
    w_mo = C.load_bf16("w_mo", wview(w_mla_o), 4, D)
    w_go = C.load_bf16("w_go", wview(w_gqa_o), 4, D)
    w_o = C.load_bf16("w_o", wview(w_out), 8, D)
    xs = P.sb("xs", [8, NB], F32)
    hT = P.sb("hT", [8, NB], BF16)
    ytile = P.sb("ytile", [4, NB + 30], F32)
    scrF = P.sb("scrF", [8, NB], F32)
    scrB = P.sb("scrB", [8, NB], BF16)
    mean_s = P.sb("mean_s", [NB], F32)
    var_s = P.sb("var_s", [NB], F32)
    rstd_c = P.sb("rstd_c", [NB], F32)
    ys = P.sb("ys", [4, NB], BF16)
    om = P.sb("om", [4, NB], BF16)
    og = P.sb("og", [4, NB], BF16)
    gate = [P.sb(f"gate{i}", [NB], F32) for i in range(3)]
    mtmp = [P.sb(f"mtmp{i}", [NB], F32) for i in range(2)]
    macc = P.sb("macc", [NB], F32)
    merged = P.sb("merged", [8, NB], BF16)
    xm = P.sb("xm", [8, NB], F32)
    h2 = P.sb("h2", [8, NB], BF16)
    rstd = P.sb("rstd", [NB], F32)
    oT3 = oT_d.rearrange("(b j p) t -> b p j t", b=2, p=128)
    MBLOCKS = [(i * NB, NB, 0) for i in range(TL // NB)] + [(TL, CTX, 1)]

    def do_block(t0, n, grp):
        P.dma("sp", xs.ap[:, :, 0:n], wview(xT)[:, :, t0:t0 + n], w=[xs])
        P.dma("pool", hT.ap[:, :, 0:n], wview(hT_d)[:, :, t0:t0 + n], w=[hT])
        ysrc = yh if grp == 0 else yc
        y0 = t0 if grp == 0 else 0
        P.dma("sp", ytile.ap[:, :, 0:n + 30], wview(ysrc)[:, :, y0:y0 + n + 30], w=[ytile])
        P.dma("pool", om.ap[:, :, 0:n], oT3[0][:, :, t0:t0 + n], w=[om])
        P.dma("pool", og.ap[:, :, 0:n], oT3[1][:, :, t0:t0 + n], w=[og])
        for ch in range(4):
            P.op("dve", lambda e, ch=ch: e.tensor_scalar(out=scrF.ap[:, ch, 0:n], in0=ytile.ap[:, ch, 0:n], scalar1=cw_t.ap[:, ch, 0:1],
                                                         scalar2=cv_t.ap[:, ch, 0:1], op0=ALU.mult, op1=ALU.add),
                 r=[ytile, cw_t, cv_t], w=[scrF])
            for k in range(1, 31):
                P.op("dve", lambda e, ch=ch, k=k: e.scalar_tensor_tensor(out=scrF.ap[:, ch, 0:n], in0=ytile.ap[:, ch, k:k + n],
                                                                         scalar=cw_t.ap[:, ch, k:k + 1], in1=scrF.ap[:, ch, 0:n],
                                                                         op0=ALU.mult, op1=ALU.add), r=[ytile, cw_t, scrF], w=[scrF])
        P.op("act", lambda e: e.activation(out=scrB.ap[:, 0:4, 0:n], in_=scrF.ap[:, 0:4, 0:n], func=AF.Copy), r=[scrF], w=[scrB])
        P.op("act", lambda e: e.activation(out=scrB.ap[:, 4:8, 0:n], in_=scrF.ap[:, 0:4, 0:n], func=AF.Square, scale=512.0 ** -0.5),
             r=[scrF], w=[scrB])
        bm, bv = banks[0], banks[1]
        for ch in range(4):
            P.op("pe", lambda e, ch=ch: e.matmul(bm.ap[:, 0:n], lhsT=ones512.ap[:, 0:128], rhs=scrB.ap[:, ch, 0:n], start=(ch == 0), stop=(ch == 3)),
                 r=[scrB, ones512], w=[bm])
        for ch in range(4):
            P.op("pe", lambda e, ch=ch: e.matmul(bv.ap[:, 0:n], lhsT=ones_bf.ap[:, 0:128], rhs=scrB.ap[:, 4 + ch, 0:n], start=(ch == 0), stop=(ch == 3)),
                 r=[scrB, ones_bf], w=[bv])
        P.op("act", lambda e: e.activation(out=mean_s.ap[:, 0:n], in_=bm.ap[:, 0:n], func=AF.Copy), r=[bm], w=[mean_s])
        P.op("dve", lambda e: e.tensor_tensor(out=var_s.ap[:, 0:n], in0=mean_s.ap[:, 0:n], in1=mean_s.ap[:, 0:n], op=ALU.mult), r=[mean_s], w=[var_s])
        P.op("dve", lambda e: e.tensor_tensor(out=var_s.ap[:, 0:n], in0=bv.ap[:, 0:n], in1=var_s.ap[:, 0:n], op=ALU.subtract), r=[bv, var_s], w=[var_s])
        P.op("act", lambda e: e.activation(out=rstd_c.ap[:, 0:n], in_=var_s.ap[:, 0:n], func=AF.Ln, bias=eps_t.ap[:, 0:1]), r=[var_s, eps_t], w=[rstd_c])
        P.op("act", lambda e: e.activation(out=rstd_c.ap[:, 0:n], in_=rstd_c.ap[:, 0:n], func=AF.Exp, scale=-0.5), r=[rstd_c], w=[rstd_c])
        for ch in range(4):
            P.op("pool", lambda e, ch=ch: e.tensor_tensor(out=scrF.ap[:, 4 + ch, 0:n], in0=scrF.ap[:, ch, 0:n], in1=mean_s.ap[:, 0:n], op=ALU.subtract),
                 r=[scrF, mean_s], w=[scrF])
            P.op("pool", lambda e, ch=ch: e.tensor_tensor(out=scrF.ap[:, 4 + ch, 0:n], in0=scrF.ap[:, 4 + ch, 0:n], in1=rstd_c.ap[:, 0:n], op=ALU.mult),
                 r=[scrF, rstd_c], w=[scrF])
            P.op("act", lambda e, ch=ch: e.activation(out=ys.ap[:, ch, 0:n], in_=scrF.ap[:, 4 + ch, 0:n], func=AF.Silu,
                                                      scale=cv_t.ap[:, ch, 1:2], bias=cv_t.ap[:, ch, 2:3]), r=[scrF, cv_t], w=[ys])

        def merge_oc(oc):
            bc_, bm_, bg_ = banks[2], banks[3], banks[4]
            gb = [banks[5], banks[6], banks[7]]
            for (bk_, w_, src_) in ((bc_, w_co, ys), (bm_, w_mo, om), (bg_, w_go, og)):
                for ch in range(4):
                    P.op("pe", lambda e, ch=ch, bk_=bk_, w_=w_, src_=src_: e.matmul(bk_.ap[:, 0:n], lhsT=w_.ap[:, ch, oc * 128:(oc + 1) * 128],
                                                                                rhs=src_.ap[:, ch, 0:n], start=(ch == 0), stop=(ch == 3)),
                         r=[w_, src_], w=[bk_])
            for b in range(3):
                c0 = b * 1024 + oc * 128
                for kc in range(8):
                    P.op("pe", lambda e, kc=kc, b=b, c0=c0: e.matmul(gb[b].ap[:, 0:n], lhsT=w_g.ap[:, kc, c0:c0 + 128], rhs=hT.ap[:, kc, 0:n],
                                                                 start=(kc == 0), stop=(kc == 7)), r=[w_g, hT], w=[gb[b]])
                P.op("act", lambda e, b=b: e.activation(out=gate[b].ap[:, 0:n], in_=gb[b].ap[:, 0:n], func=AF.Sigmoid,
                                                        bias=bg_t.ap[:, b * 8 + oc:b * 8 + oc + 1]), r=[gb[b], bg_t], w=[gate[b]])
            P.op("dve", lambda e: e.tensor_tensor(out=macc.ap[:, 0:n], in0=bc_.ap[:, 0:n], in1=gate[0].ap[:, 0:n], op=ALU.mult),
                 r=[bc_, gate[0]], w=[macc])
            P.op("dve", lambda e: e.tensor_tensor(out=mtmp[0].ap[:, 0:n], in0=bm_.ap[:, 0:n], in1=gate[1].ap[:, 0:n], op=ALU.mult),
                 r=[bm_, gate[1]], w=[mtmp[0]])
            P.op("dve", lambda e: e.tensor_tensor(out=mtmp[1].ap[:, 0:n], in0=bg_.ap[:, 0:n], in1=gate[2].ap[:, 0:n], op=ALU.mult),
                 r=[bg_, gate[2]], w=[mtmp[1]])
            P.op("pool", lambda e: e.tensor_tensor(out=macc.ap[:, 0:n], in0=macc.ap[:, 0:n], in1=mtmp[0].ap[:, 0:n], op=ALU.add),
                 r=[macc, mtmp[0]], w=[macc])
            P.op("pool", lambda e: e.tensor_tensor(out=merged.ap[:, oc, 0:n], in0=macc.ap[:, 0:n], in1=mtmp[1].ap[:, 0:n], op=ALU.add),
                 r=[macc, mtmp[1]], w=[merged])

        for oc_ in range(8):
            merge_oc(oc_)

        def outproj_oc(oc):
            bk = banks[oc % 2]
            for kc in range(8):
                P.op("pe", lambda e, kc=kc: e.matmul(bk.ap[:, 0:n], lhsT=w_o.ap[:, kc, oc * 128:(oc + 1) * 128], rhs=merged.ap[:, kc, 0:n],
                                                     start=(kc == 0), stop=(kc == 7)), r=[w_o, merged], w=[bk])
            P.op("dve", lambda e: e.scalar_tensor_tensor(out=xm.ap[:, oc, 0:n], in0=bk.ap[:, 0:n], scalar=G1.ap[:, oc, grp:grp + 1],
                                                         in1=xs.ap[:, oc, 0:n], op0=ALU.mult, op1=ALU.add),
                 r=[bk, G1, xs], w=[xm])

        for oc_ in range(8):
            outproj_oc(oc_)
        P.dma(None, wview(xm_o)[:, :, t0:t0 + n], xm.ap[:, :, 0:n], r=[xm])
        emit_norm_mod(C, xm, n, ones_bf, A2, B2, grp, h2, scrF, scrB, banks[2], rstd)
        P.dma(None, wview(h2_o)[:, :, t0:t0 + n], h2.ap[:, :, 0:n], r=[h2])

    for (t0_, n_, grp_) in MBLOCKS:
        do_block(t0_, n_, grp_)
    return C.finish(), C


def build_ffn():
    C = Ctx()
    xm_d = C.inp("xmT", [D, TT])
    h2l = C.inp("h2l", [D, TL + 2], BF16)
    h2c = C.inp("h2c", [D, CTX + 2], BF16)
    modv = C.inp("modv", [128, 6, 8, 2])
    w_up = C.inp("w_up", [D, 2 * D_FF])
    w_down = C.inp("w_down", [D_FF, D])
    fwc = C.inp("fwc", [128, 44, 4])
    x_o = C.out("xT_out", [D, TT])
    P = C.start()
    C.cv_engs = ("pool", "act", "dve")
    C.nst = 2
    banks = C.banks
    modc = C.load_f32("modc", modv, [6, 8, 2])
    G2 = T(modc.ap[:, 5], "G2")
    fw_t = C.load_f32("fw_t", fwc, [44, 4])
    wu = C.load_bf16("wu", wview(w_up), 8, 2 * D_FF)
    wd = C.load_bf16("wd", wview(w_down), 22, D)
    h2 = P.sb("h2", [8, 514], BF16)
    xmc = [P.sb(f"xmc{i}", [512], F32) for i in range(2)]
    act_t = P.sb("act_t", [22, 512], BF16)
    ta = [P.sb(f"ta{i}", [512], F32) for i in range(2)]
    tg = [P.sb(f"tg{i}", [512], F32) for i in range(2)]
    sgt = [P.sb(f"sgt{i}", [512], F32) for i in range(2)]
    cnt = [0]

    def conv_chunk(cc, n, dst):
        i = cnt[0]
        cnt[0] += 1
        bk = banks[2 + 2 * (i % 3)]
        bh = banks[3 + 2 * (i % 3)]
        for kc in range(8):
            P.op("pe", lambda e, kc=kc: e.matmul(bk.ap[:, 0:n], lhsT=wu.ap[:, kc, cc * 128:(cc + 1) * 128], rhs=h2.ap[:, kc, 1:n + 1],
                                                 start=(kc == 0), stop=(kc == 7)), r=[wu, h2], w=[bk])
        for kc in range(8):
            P.op("pe", lambda e, kc=kc: e.matmul(bh.ap[:, 0:2], lhsT=wu.ap[:, kc, cc * 128:(cc + 1) * 128], rhs=h2.ap[:, kc, 0:n + 2:n + 1],
                                                 start=(kc == 0), stop=(kc == 7)), r=[wu, h2], w=[bh])
        P.op("act", lambda e: e.activation(out=dst.ap[:, 0:n], in_=bk.ap[:, 0:n], func=AF.Identity, scale=fw_t.ap[:, cc, 1:2],
                                           bias=fw_t.ap[:, cc, 3:4]), r=[bk, fw_t], w=[dst])
        P.op("dve", lambda e: e.scalar_tensor_tensor(out=dst.ap[:, 1:n], in0=bk.ap[:, 0:n - 1], scalar=fw_t.ap[:, cc, 0:1], in1=dst.ap[:, 1:n],
                                                     op0=ALU.mult, op1=ALU.add), r=[bk, fw_t, dst], w=[dst])
        P.op("dve", lambda e: e.scalar_tensor_tensor(out=dst.ap[:, 0:n - 1], in0=bk.ap[:, 1:n], scalar=fw_t.ap[:, cc, 2:3], in1=dst.ap[:, 0:n - 1],
                                                     op0=ALU.mult, op1=ALU.add), r=[bk, fw_t, dst], w=[dst])
        P.op("dve", lambda e: e.scalar_tensor_tensor(out=dst.ap[:, 0:1], in0=bh.ap[:, 0:1], scalar=fw_t.ap[:, cc, 0:1], in1=dst.ap[:, 0:1],
                                                     op0=ALU.mult, op1=ALU.add), r=[bh, fw_t, dst], w=[dst])
        P.op("dve", lambda e: e.scalar_tensor_tensor(out=dst.ap[:, n - 1:n], in0=bh.ap[:, 1:2], scalar=fw_t.ap[:, cc, 2:3], in1=dst.ap[:, n - 1:n],
                                                     op0=ALU.mult, op1=ALU.add), r=[bh, fw_t, dst], w=[dst])

    def do_block(t0, n, grp):
        hsrc = h2l if grp == 0 else h2c
        h0 = t0 if grp == 0 else 0
        P.dma("sp", h2.ap[:, :, 0:n + 2], wview(hsrc)[:, :, h0:h0 + n + 2], w=[h2])
        for j in range(22):
            a_, g_, s_ = ta[j % 2], tg[j % 2], sgt[j % 2]
            conv_chunk(22 + j, n, g_)
            conv_chunk(j, n, a_)
            P.op("act", lambda e, g_=g_, s_=s_: e.activation(out=s_.ap[:, 0:n], in_=g_.ap[:, 0:n], func=AF.Silu), r=[g_], w=[s_])
            P.op("pool", lambda e, j=j, a_=a_, s_=s_: e.tensor_tensor(out=act_t.ap[:, j, 0:n], in0=a_.ap[:, 0:n], in1=s_.ap[:, 0:n], op=ALU.mult),
                 r=[a_, s_], w=[act_t])

        def down_oc(oc):
            bk = banks[oc % 2]
            xc = xmc[oc % 2]
            P.dma("pool", xc.ap[:, 0:n], xm_d[oc * 128:(oc + 1) * 128, t0:t0 + n], w=[xc])
            for j in range(22):
                P.op("pe", lambda e, j=j: e.matmul(bk.ap[:, 0:n], lhsT=wd.ap[:, j, oc * 128:(oc + 1) * 128], rhs=act_t.ap[:, j, 0:n],
                                                   start=(j == 0), stop=(j == 21)), r=[wd, act_t], w=[bk])
            P.op("dve", lambda e: e.scalar_tensor_tensor(out=xc.ap[:, 0:n], in0=bk.ap[:, 0:n], scalar=G2.ap[:, oc, grp:grp + 1],
                                                         in1=xc.ap[:, 0:n], op0=ALU.mult, op1=ALU.add),
                 r=[bk, G2, xc], w=[xc])
            P.dma("sp", x_o[oc * 128:(oc + 1) * 128, t0:t0 + n], xc.ap[:, 0:n], r=[xc])

        for oc_ in range(8):
            down_oc(oc_)

    for (t0_, n_, grp_) in BLOCKS:
        do_block(t0_, n_, grp_)
    return C.finish(), C


def _cols(v, n=128):
    v = np.asarray(v, np.float32)
    return np.ascontiguousarray(v.reshape(-1, n).T)


def _rope_tables():
    tabs = []
    tok = np.arange(SEQ)
    row = (tok // 64).astype(np.float32)
    col = (tok % 64).astype(np.float32)

    def mk(rot_dim, off):
        nf = rot_dim // 4
        inv = (1.0 / (10000.0 ** (np.arange(nf, dtype=np.float32) / nf))).astype(np.float32)
        ar = row[:, None] * inv
        ac = col[:, None] * inv
        ang = np.concatenate([ar, ar, ac, ac], axis=-1)
        cos = np.ones((96, SEQ), np.float32)
        sin = np.zeros((96, SEQ), np.float32)
        cos[off:off + rot_dim] = np.cos(ang).T
        sin[off:off + rot_dim] = np.sin(ang).T
        return cos, sin

    cm, sm = mk(32, 64)
    cg, sg = mk(64, 0)
    for c in range(NCORES):
        sl = slice(c * TL, (c + 1) * TL)
        M = np.zeros((96, 2, TT), np.float32)
        G = np.zeros((96, 2, TT), np.float32)
        M[:, 0, :TL] = cm[:, sl]; M[:, 1, :TL] = sm[:, sl]; M[:, 0, TL:] = 1.0
        G[:, 0, :TL] = cg[:, sl]; G[:, 1, :TL] = sg[:, sl]; G[:, 0, TL:] = 1.0
        tabs.append((M, G))
    rot = np.zeros((96, 2, 96), np.float32)
    for i in range(8):
        rot[72 + i, 0, 64 + i] = -1.0; rot[64 + i, 0, 72 + i] = 1.0
        rot[88 + i, 0, 80 + i] = -1.0; rot[80 + i, 0, 88 + i] = 1.0
    for i in range(16):
        rot[16 + i, 1, i] = -1.0; rot[i, 1, 16 + i] = 1.0
        rot[48 + i, 1, 32 + i] = -1.0; rot[32 + i, 1, 48 + i] = 1.0
    shift = np.zeros((32, 96), np.float32)
    for i in range(32):
        shift[i, 64 + i] = 1.0
    return tabs, rot.astype(NPBF), shift.astype(NPBF)


_PROGS = {}


def _prog(name):
    if name not in _PROGS:
        _PROGS[name] = {"pre": build_pre, "attn": build_attn, "ffn": build_ffn}[name]()[0]
    return _PROGS[name]


def _run(name, in_maps):
    res = run_bass_kernel_spmd(_prog(name), in_maps, core_ids=list(range(NCORES)))
    return res.results


def kernel(**inp):
    f32 = np.float32
    x = np.asarray(inp["x"], f32)[0]
    ctx = np.asarray(inp["ctx"], f32)[0]
    tabs, rot, shift = _rope_tables()
    c2 = np.ascontiguousarray(np.stack([_cols(inp["c"][0]), _cols(inp["c_ctx"])], axis=-1))
    xT = [np.ascontiguousarray(np.concatenate([x[c * TL:(c + 1) * TL], ctx], axis=0).T) for c in range(NCORES)]
    for l in range(DEPTH):
        hg = np.zeros((96, 4), f32)
        hg[:, 0] = inp["g_mla_q"][l]; hg[:, 1] = inp["g_mla_k"][l]
        hg[:64, 2] = inp["g_gqa_q"][l]; hg[:64, 3] = inp["g_gqa_k"][l]
        w_in_l = np.ascontiguousarray(inp["w_in"][l], dtype=f32)
        pre_in = []
        for c in range(NCORES):
            pre_in.append(dict(
                xT=xT[c], c2=c2, w_mod=np.ascontiguousarray(inp["w_mod"][l], dtype=f32), b_modc=_cols(inp["b_mod"][l]),
                gn1c=_cols(inp["g_norm1"][l]), gn2c=_cols(inp["g_norm2"][l]), w_in=w_in_l,
                gqac=_cols(inp["g_q_a"][l]), gkvac=_cols(inp["g_kv_a"][l]),
                w_q_b=np.ascontiguousarray(inp["w_q_b"][l], dtype=f32), w_kv_b=np.ascontiguousarray(inp["w_kv_b"][l], dtype=f32),
                hgain=hg, ropeM=tabs[c][0], ropeG=tabs[c][1], rotm=rot, shiftm=shift))
        r1 = _run("pre", pre_in)
        kTf = np.concatenate([r1[0]["kT"][:, :, TL:]] + [r1[c]["kT"][:, :, :TL] for c in range(NCORES)], axis=2)
        vPf = np.concatenate([r1[0]["vP"][TL:]] + [r1[c]["vP"][:TL] for c in range(NCORES)], axis=0)
        vPf = np.ascontiguousarray(vPf.transpose(1, 0, 2))
        kTf = np.ascontiguousarray(kTf)
        yfull = np.concatenate([r1[c]["y"][:, :TL] for c in range(NCORES)], axis=1)
        ypad = np.concatenate([np.zeros((512, 15), f32), yfull, np.zeros((512, 15), f32)], axis=1)
        cvec = np.ascontiguousarray(np.stack([_cols(inp["conv_dw_b"][l]), _cols(inp["conv_ln_g"][l]), _cols(inp["conv_ln_b"][l])], axis=-1))
        cwc = np.ascontiguousarray(np.asarray(inp["conv_dw_w"][l], f32).T.reshape(4, 128, 31).transpose(1, 0, 2))
        at_in = []
        for c in range(NCORES):
            yc = np.concatenate([np.zeros((512, 15), f32), r1[c]["y"][:, TL:], np.zeros((512, 15), f32)], axis=1)
            at_in.append(dict(
                qT=r1[c]["qT"], kTf=kTf, vPf=vPf, hT=r1[c]["hT"], xT=xT[c],
                yh=np.ascontiguousarray(ypad[:, c * TL:c * TL + TL + 30]), yc=np.ascontiguousarray(yc),
                modv=r1[c]["modv"], w_in=w_in_l, bgc=_cols(inp["b_gate"][l]), cwc=cwc, cvec=cvec,
                w_conv_out=np.ascontiguousarray(inp["w_conv_out"][l], dtype=f32), w_mla_o=np.ascontiguousarray(inp["w_mla_o"][l], dtype=f32),
                w_gqa_o=np.ascontiguousarray(inp["w_gqa_o"][l], dtype=f32), w_out=np.ascontiguousarray(inp["w_out"][l], dtype=f32)))
        r2 = _run("attn", at_in)
        h2full = np.concatenate([r2[c]["h2T"][:, :TL] for c in range(NCORES)], axis=1)
        zc = np.zeros((D, 1), NPBF)
        h2pad = np.concatenate([zc, h2full, zc], axis=1)
        fw = np.asarray(inp["ffn_dw_w"][l], f32)
        fwc = np.ascontiguousarray(np.stack([_cols(fw[0]), _cols(fw[1]), _cols(fw[2]), _cols(inp["ffn_dw_b"][l])], axis=-1))
        ff_in = []
        for c in range(NCORES):
            h2c = np.concatenate([zc, r2[c]["h2T"][:, TL:], zc], axis=1)
            ff_in.append(dict(
                xmT=r2[c]["xmT"], h2l=np.ascontiguousarray(h2pad[:, c * TL:c * TL + TL + 2]), h2c=np.ascontiguousarray(h2c),
                modv=r1[c]["modv"], w_up=np.ascontiguousarray(inp["w_up"][l], dtype=f32),
                w_down=np.ascontiguousarray(inp["w_down"][l], dtype=f32), fwc=fwc))
        r3 = _run("ffn", ff_in)
        xT = [r3[c]["xT_out"] for c in range(NCORES)]
    out = np.concatenate([xT[c][:, :TL].T for c in range(NCORES)], axis=0)[None]
    return np.ascontiguousarray(out.astype(np.float32))
```

```python
import numpy as np
import ml_dtypes
from contextlib import ExitStack
import concourse.bass as bass
import concourse.mybir as mybir
from concourse.bass_utils import run_bass_kernel_spmd

F32 = mybir.dt.float32
BF16 = mybir.dt.bfloat16
U8 = mybir.dt.uint8
AF = mybir.ActivationFunctionType
ALU = mybir.AluOpType
NPBF = ml_dtypes.bfloat16

NCORES = 8
D = 1024
SEQ = 16384
TL = SEQ // NCORES
CTX = 256
TT = TL + CTX
NK = CTX + SEQ
DEPTH = 4
EPS = 1e-6
BLOCKS = [(0, 512, 0), (512, 512, 0), (1024, 512, 0), (1536, 512, 0), (2048, 256, 1)]
NQH = 16
NKH = 10
MLA_SCALE = 96 ** -0.5
GQA_SCALE = 64 ** -0.5
C_GLU, C_QA, C_KVA, C_KR, C_GQ, C_GK, C_GV, C_GATE = 0, 1024, 1408, 1664, 1696, 2208, 2336, 2464
IN_COLS = 5536
D_FF = 2816

ENGS = ("pe", "act", "dve", "pool", "sp")
SEM_LIM = 30000
DMA_K = 12


class T:
    __slots__ = ("ap", "name")

    def __init__(self, ap, name):
        self.ap = ap
        self.name = name


class Prog:
    def __init__(self, nc, arena, arena_bytes):
        self.nc = nc
        self.ops = []
        self.last_w = {}
        self.readers = {}
        self.arena_bytes = arena_bytes
        self.arena = arena
        self.off = 0
        self.ndma = {e: 0 for e in ENGS}
        self.last_op = {e: None for e in ENGS}
        self.peak = 0
        self.dmaq = 0

    def sb(self, name, shape, dtype, parts=128):
        esz = 4 if dtype == F32 else 2
        n = int(np.prod(shape))
        nbytes = n * esz
        start = (self.off + 63) // 64 * 64
        assert start + nbytes <= self.arena_bytes, f"SBUF arena overflow {name} {start + nbytes}"
        self.off = start + nbytes
        self.peak = max(self.peak, self.off)
        ap = self.arena[0:parts, start:start + nbytes].bitcast(dtype)
        if len(shape) == 2:
            ap = ap.rearrange("p (a b) -> p a b", b=shape[1])
        elif len(shape) == 3:
            ap = ap.rearrange("p (a b c) -> p a b c", b=shape[1], c=shape[2])
        return T(ap, name)

    def mark(self):
        return self.off

    def release(self, mark):
        self.off = mark

    def _deps(self, idx, r, w):
        deps = set()
        for k in r:
            lw = self.last_w.get(k)
            if lw is not None:
                deps.add(lw)
        for k in w:
            lw = self.last_w.get(k)
            if lw is not None:
                deps.add(lw)
            for rd in self.readers.get(k, ()):
                deps.add(rd)
        for k in r:
            self.readers.setdefault(k, []).append(idx)
        for k in w:
            self.last_w[k] = idx
            self.readers[k] = []
        deps.discard(idx)
        return deps

    def op(self, eng, fn, r=(), w=()):
        idx = len(self.ops)
        deps = self._deps(idx, r, w)
        self.ops.append(dict(eng=eng, fn=fn, deps=deps, dma=False))
        self.last_op[eng] = idx
        return idx

    def dma(self, q, out, in_, r=(), w=()):
        if q is None:
            q = ("sp", "pool")[self.dmaq % 2]
            self.dmaq += 1
        idx = len(self.ops)
        deps = self._deps(idx, r, w)
        j = self.ndma[q]
        self.ndma[q] += 1
        self.ops.append(dict(eng=q, fn=lambda e: e.dma_start(out=out, in_=in_), deps=deps, dma=True, j=j))
        self.last_op[q] = idx
        return idx

    def barrier(self):
        lasts = {e: v for e, v in self.last_op.items() if v is not None}
        dmas = []
        for q in ENGS:
            cnt = 0
            for i in range(len(self.ops) - 1, -1, -1):
                o = self.ops[i]
                if o["dma"] and o["eng"] == q:
                    dmas.append(i)
                    cnt += 1
                    if cnt >= DMA_K:
                        break
        for e in ENGS:
            idx = len(self.ops)
            deps = set(v for ee, v in lasts.items() if ee != e and not self.ops[v]["dma"]) | set(dmas)
            self.ops.append(dict(eng=e, fn=None, deps=deps, dma=False))
            self.last_op[e] = idx
        self.last_w.clear()
        self.readers.clear()

    def emit(self, block, sems):
        ops = self.ops
        signaled = set()
        for o in ops:
            for d in o["deps"]:
                if not ops[d]["dma"]:
                    if ops[d]["eng"] == "pe" and o["eng"] == "pe":
                        continue
                    signaled.add(d)
        cnt = {e: 0 for e in ENGS}
        for i, o in enumerate(ops):
            if i in signaled:
                o["sig"] = cnt[o["eng"]]
                cnt[o["eng"]] += 1
        sem_iter = iter(sems)
        eng_sems = {}
        for e in ENGS:
            n_ep = cnt[e] // SEM_LIM + 1
            eng_sems[e] = [next(sem_iter) for _ in range(n_ep)]
        dma_sems = {}
        for q in ENGS:
            if self.ndma[q]:
                dma_sems[q] = [next(sem_iter) for _ in range(DMA_K)]
        per_eng = {e: [] for e in ENGS}
        for i, o in enumerate(ops):
            per_eng[o["eng"]].append(i)

        def run(e_name, e):
            waited = {x: -1 for x in ENGS}
            dwaited = set()
            for i in per_eng[e_name]:
                o = ops[i]
                need = {}
                for d in sorted(o["deps"]):
                    po = ops[d]
                    if po["dma"]:
                        if d not in dwaited:
                            dwaited.add(d)
                            q = po["eng"]
                            j = po["j"]
                            e.wait_ge(dma_sems[q][j % DMA_K], 16 * (j // DMA_K + 1))
                    else:
                        if po["eng"] == "pe" and e_name == "pe":
                            continue
                        s = po["sig"]
                        if s > waited[po["eng"]]:
                            need[po["eng"]] = max(need.get(po["eng"], -1), s)
                for pe_, s in need.items():
                    waited[pe_] = s
                    e.wait_ge(eng_sems[pe_][s // SEM_LIM], s % SEM_LIM + 1)
                if o["dma"]:
                    j = o["j"]
                    if j >= DMA_K:
                        e.wait_ge(dma_sems[e_name][j % DMA_K], 16 * (j // DMA_K))
                    ins = o["fn"](e)
                    ins.then_inc(dma_sems[e_name][j % DMA_K], 16)
                else:
                    if o["fn"] is None:
                        ins = e.nop() if "sig" in o else None
                    else:
                        ins = o["fn"](e)
                    if "sig" in o:
                        s = o["sig"]
                        ins.then_inc(eng_sems[e_name][s // SEM_LIM], 1)

        @block.tensor
        def _(e):
            run("pe", e)

        @block.scalar
        def _(e):
            run("act", e)

        @block.vector
        def _(e):
            run("dve", e)

        @block.gpsimd
        def _(e):
            run("pool", e)

        @block.sync
        def _(e):
            run("sp", e)


class Ctx:
    def __init__(self):
        self.nc = bass.Bass("TRN2", target_bir_lowering=False)
        self.es = ExitStack()
        self.ins = {}
        self.outs = {}

    def inp(self, name, shape, dt=F32):
        t = self.nc.dram_tensor(name, list(shape), dt, kind="ExternalInput").ap()
        self.ins[name] = t
        return t

    def out(self, name, shape, dt=F32):
        t = self.nc.dram_tensor(name, list(shape), dt, kind="ExternalOutput").ap()
        self.outs[name] = t
        return t

    def scratch(self, name, shape, dt):
        return self.nc.dram_tensor(name, list(shape), dt).ap()

    def start(self, arena_kb=204):
        nc, es = self.nc, self.es
        arena = es.enter_context(nc.sbuf_tensor("arena", [128, arena_kb * 1024], U8))
        psum = es.enter_context(nc.psum_tensor("psum", [128, 4096], F32))
        self.sems = [es.enter_context(nc.semaphore(f"s{i}")) for i in range(70)]
        self.block = es.enter_context(nc.Block())
        self.P = Prog(nc, arena[:], arena_kb * 1024)
        self.psum = psum[:]
        self.banks = [T(psum[:, i * 512:(i + 1) * 512], f"bank{i}") for i in range(8)]
        self.stage = None
        self.nstage = 0
        self.eps_t = self.P.sb("eps_t", [1], F32)
        self.P.op("pool", lambda e: e.memset(self.eps_t.ap, EPS), w=[self.eps_t])
        self.ncv = 0
        return self.P

    def finish(self):
        self.P.barrier()
        self.P.emit(self.block, self.sems)
        self.es.close()
        return self.nc

    def load_f32(self, name, src, shape, parts=128):
        t = self.P.sb(name, shape, F32)
        self.P.dma(None, t.ap[0:parts], src, w=[t])
        return t

    def load_bf16(self, name, src3, kc, ncols, parts=128, dst=None, dst_ap=None):
        P = self.P
        if self.stage is None:
            self.stage = [P.sb(f"stage{i}", [2048], F32) for i in range(self.nst)]
        if dst is None:
            dst = P.sb(name, [kc, ncols], BF16)
            dst_ap = dst.ap
        for k in range(kc):
            for c0 in range(0, ncols, 2048):
                cn = min(2048, ncols - c0)
                st = self.stage[self.nstage % self.nst]
                self.nstage += 1
                P.dma(None, st.ap[0:parts, 0:cn], src3[:, k, c0:c0 + cn], w=[st])
                eng = ("pool", "act", "dve")[self.ncv % 3] if self.cv_engs is None else self.cv_engs[self.ncv % len(self.cv_engs)]
                self.ncv += 1
                o = dst_ap[0:parts, k, c0:c0 + cn]
                i_ = st.ap[0:parts, 0:cn]
                if eng == "act":
                    P.op("act", lambda e, o=o, i_=i_: e.activation(out=o, in_=i_, func=AF.Copy), r=[st], w=[dst])
                elif eng == "pool":
                    P.op("pool", lambda e, o=o, i_=i_: e.tensor_copy(out=o, in_=i_), r=[st], w=[dst])
                else:
                    P.op("dve", lambda e, o=o, i_=i_: e.tensor_copy(out=o, in_=i_), r=[st], w=[dst])
        return dst

    cv_engs = None
    nst = 3


def wview(ap2d):
    return ap2d.rearrange("(kc p) n -> p kc n", p=128)


def emit_norm_mod(C, xs, n, ones_bf, Acol, Bcol, grp, hT, tmpA, sqbuf, bank_mean, rstd):
    P = C.P
    P.op("act", lambda e: e.activation(out=sqbuf.ap[:, :, 0:n], in_=xs.ap[:, :, 0:n], func=AF.Square, scale=1.0 / 32.0),
         r=[xs], w=[sqbuf])
    for kc in range(8):
        P.op("pe", lambda e, kc=kc: e.matmul(bank_mean.ap[:, 0:n], lhsT=ones_bf.ap[:, 0:128], rhs=sqbuf.ap[:, kc, 0:n],
                                             start=(kc == 0), stop=(kc == 7)), r=[sqbuf, ones_bf], w=[bank_mean])
    P.op("act", lambda e: e.activation(out=rstd.ap[:, 0:n], in_=bank_mean.ap[:, 0:n], func=AF.Ln, bias=C.eps_t.ap[:, 0:1]), r=[bank_mean, C.eps_t], w=[rstd])
    P.op("act", lambda e: e.activation(out=rstd.ap[:, 0:n], in_=rstd.ap[:, 0:n], func=AF.Exp, scale=-0.5), r=[rstd], w=[rstd])
    for kc in range(8):
        tA = tmpA[kc % len(tmpA)]
        P.op("dve", lambda e, kc=kc, tA=tA: e.scalar_tensor_tensor(out=tA.ap[:, 0:n], in0=xs.ap[:, kc, 0:n],
                                                                   scalar=Acol.ap[:, kc, grp:grp + 1], in1=rstd.ap[:, 0:n],
                                                                   op0=ALU.mult, op1=ALU.mult), r=[xs, Acol, rstd], w=[tA])
        P.op("act", lambda e, kc=kc, tA=tA: e.activation(out=hT.ap[:, kc, 0:n], in_=tA.ap[:, 0:n], func=AF.Identity,
                                                         bias=Bcol.ap[:, kc, grp:grp + 1]), r=[tA, Bcol], w=[hT])


def build_pre():
    C = Ctx()
    xT = C.inp("xT", [D, TT])
    c2 = C.inp("c2", [128, 8, 2])
    w_mod = C.inp("w_mod", [D, 6 * D])
    b_modc = C.inp("b_modc", [128, 48])
    gn1c = C.inp("gn1c", [128, 8])
    gn2c = C.inp("gn2c", [128, 8])
    w_in = C.inp("w_in", [D, IN_COLS])
    gqac = C.inp("gqac", [128, 3])
    gkvac = C.inp("gkvac", [128, 2])
    w_q_b = C.inp("w_q_b", [384, 768])
    w_kv_b = C.inp("w_kv_b", [256, 1024])
    hgain = C.inp("hgain", [96, 4])
    ropeM = C.inp("ropeM", [96, 2, TT])
    ropeG = C.inp("ropeG", [96, 2, TT])
    rotm = C.inp("rotm", [96, 2, 96], BF16)
    shiftm = C.inp("shiftm", [32, 96], BF16)
    modv_o = C.out("modv", [128, 6, 8, 2])
    hT_o = C.out("hT", [D, TT], BF16)
    qT_o = C.out("qT", [NQH, 96, TT], BF16)
    kT_o = C.out("kT", [NKH, 96, TT], BF16)
    vP_o = C.out("vP", [TT, NKH, 65], BF16)
    y_o = C.out("y", [512, TT])
    P = C.start()
    eps_t = C.eps_t
    C.cv_engs = ("pool", "act")
    C.nst = 2
    banks = C.banks

    ones_bf = P.sb("ones_bf", [128], BF16)
    P.op("pool", lambda e: e.memset(ones_bf.ap, 1.0), w=[ones_bf])
    rot_t = P.sb("rot_t", [2, 96], BF16)
    P.dma(None, rot_t.ap[0:96], rotm, w=[rot_t])
    shift_t = P.sb("shift_t", [96], BF16)
    P.dma(None, shift_t.ap[0:32], shiftm, w=[shift_t])
    hgain_t = C.load_f32("hgain_t", hgain, [4], parts=96)
    gqa_t = C.load_f32("gqa_t", gqac, [3])
    gkva_t = C.load_f32("gkva_t", gkvac, [2])
    gn1_t = C.load_f32("gn1_t", gn1c, [8])
    gn2_t = C.load_f32("gn2_t", gn2c, [8])
    bmod_t = C.load_f32("bmod_t", b_modc, [48])

    c2_t = C.load_f32("c2_t", c2, [8, 2])
    sc_t = P.sb("sc_t", [8, 2], F32)
    P.op("act", lambda e: e.activation(out=sc_t.ap, in_=c2_t.ap, func=AF.Silu), r=[c2_t], w=[sc_t])
    modv = P.sb("modv", [6, 8, 2], F32)
    m0 = P.mark()
    wm = [P.sb(f"wm{i}", [8, 512], F32) for i in range(2)]
    wmv = wview(w_mod)
    for g in range(12):
        wt = wm[g % 2]
        P.dma(None, wt.ap, wmv[:, :, g * 512:(g + 1) * 512], w=[wt])
        for s in range(4):
            j = g * 4 + s
            bk = banks[j % 2]
            for kc in range(8):
                P.op("pe", lambda e, bk=bk, wt=wt, kc=kc, s=s: e.matmul(bk.ap[:, 0:2], lhsT=wt.ap[:, kc, s * 128:(s + 1) * 128],
                                                                     rhs=sc_t.ap[:, kc, :], start=(kc == 0), stop=(kc == 7)),
                     r=[wt, sc_t], w=[bk])
            P.op("dve", lambda e, bk=bk, j=j: e.tensor_scalar(out=modv.ap[:, j // 8, j % 8, :], in0=bk.ap[:, 0:2],
                                                            scalar1=bmod_t.ap[:, j:j + 1], scalar2=None, op0=ALU.add),
                 r=[bk, bmod_t], w=[modv])
    P.barrier()
    P.release(m0)
    modc = P.sb("modc", [6, 8, 2], F32)
    for (dst, src, gn) in ((0, 1, gn1_t), (3, 4, gn2_t)):
        for cnd in range(2):
            P.op("dve", lambda e, dst=dst, src=src, gn=gn, cnd=cnd: e.scalar_tensor_tensor(
                out=modc.ap[:, dst, :, cnd], in0=modv.ap[:, src, :, cnd], scalar=1.0, in1=gn.ap,
                op0=ALU.add, op1=ALU.mult), r=[modv, gn], w=[modc])
    for (dst, src) in ((1, 0), (2, 2), (4, 3), (5, 5)):
        P.op("dve", lambda e, dst=dst, src=src: e.tensor_copy(out=modc.ap[:, dst], in_=modv.ap[:, src]), r=[modv], w=[modc])
    P.dma(None, modv_o, modc.ap, r=[modc])
    import os
    STOP = int(os.environ.get('PRE_STOP', '9'))
    if STOP <= 1:
        return C.finish(), C
    Acol = T(modc.ap[:, 0], "A1")
    Bcol = T(modc.ap[:, 1], "B1")

    wiv = wview(w_in)
    w_a = C.load_bf16("w_a", wiv[:, :, 0:C_GATE], 8, C_GATE)
    wqb = C.load_bf16("wqb", wview(w_q_b), 3, 768)
    wkb = P.sb("wkb", [2, 8, 96], BF16)
    P.op("pool", lambda e: e.memset(wkb.ap, 0.0), w=[wkb])
    wkv = P.sb("wkv", [2, 8, 64], BF16)
    wkbv = wview(w_kv_b).rearrange("p kc (h t) -> p kc h t", t=128)
    stg = C.stage
    for kc in range(2):
        st = stg[C.nstage % C.nst]
        C.nstage += 1
        P.dma(None, st.ap[:, 0:1024], wview(w_kv_b)[:, kc, :], w=[st])
        sv = st.ap[:, 0:1024].rearrange("p (h t) -> p h t", t=128)
        P.op("pool", lambda e, kc=kc, sv=sv: e.tensor_copy(out=wkb.ap[:, kc, :, 0:64], in_=sv[:, :, 0:64]), r=[st], w=[wkb])
        P.op("pool", lambda e, kc=kc, sv=sv: e.tensor_copy(out=wkv.ap[:, kc, :, :], in_=sv[:, :, 64:128]), r=[st], w=[wkv])

    if STOP <= 2:
        return C.finish(), C
    xs = P.sb("xs", [8, 512], F32)
    sqb = P.sb("sqb", [8, 512], BF16)
    tmpA = [P.sb(f"tmpA{i}", [512], F32) for i in range(3)]
    hT = P.sb("hT", [8, 512], BF16)
    rstd = P.sb("rstd", [512], F32)
    sg = [P.sb(f"sg{i}", [512], F32) for i in range(2)]
    yt = [P.sb(f"yt{i}", [512], F32) for i in range(2)]
    zq = P.sb("zq", [3, 512], F32)
    sqa = P.sb("sqa", [3, 512], BF16)
    qan = P.sb("qan", [3, 512], BF16)
    zkv = P.sb("zkv", [2, 512], F32)
    sqkv = P.sb("sqkv", [2, 512], BF16)
    kvn = P.sb("kvn", [2, 512], BF16)
    zkr = P.sb("zkr", [512], BF16)
    rstd2 = P.sb("rstd2", [512], F32)
    ropeM_t = P.sb("ropeM_t", [2, 512], F32)
    ropeG_t = P.sb("ropeG_t", [2, 512], F32)
    vt = [P.sb(f"vt{i}", [NKH, 65], BF16) for i in range(2)]
    for v_ in vt:
        P.op("pool", lambda e, v_=v_: e.memset(v_.ap, 1.0), w=[v_])
    NU = 4
    u_sq = [P.sb(f"u_sq{i}", [512], BF16) for i in range(NU)]
    u_zc = [P.sb(f"u_zc{i}", [512], F32) for i in range(NU)]
    u_rs = [P.sb(f"u_rs{i}", [512], F32) for i in range(NU)]
    u_xn = [P.sb(f"u_xn{i}", [512], BF16) for i in range(NU)]
    u_t1 = [P.sb(f"u_t1{i}", [512], F32) for i in range(NU)]
    u_t2 = [P.sb(f"u_t2{i}", [512], F32) for i in range(NU)]
    u_o = [P.sb(f"u_o{i}", [512], BF16) for i in range(NU)]
    u_og = [P.sb(f"u_og{i}", [512], BF16) for i in range(NU)]
    for o_ in u_og:
        P.op("pool", lambda e, o_=o_: e.memset(o_.ap, 0.0), w=[o_])
    ucnt = [0]

    def inproj(bank, col0, m, n):
        for kc in range(8):
            P.op("pe", lambda e, kc=kc: e.matmul(bank.ap[0:m, 0:n], lhsT=w_a.ap[:, kc, col0:col0 + m], rhs=hT.ap[:, kc, 0:n],
                                                 start=(kc == 0), stop=(kc == 7)), r=[w_a, hT], w=[bank])

    def normrope(main_fn, d, gcol, rope_t, ridx, out_dram, n):
        u = ucnt[0] % NU
        ucnt[0] += 1
        bA, bB = banks[2 * u], banks[2 * u + 1]
        bC = bA
        main_fn(bA)
        sq, zc, rs_, xn, t1, t2, o_ = u_sq[u], u_zc[u], u_rs[u], u_xn[u], u_t1[u], u_t2[u], (u_o[u] if d == 96 else u_og[u])
        P.op("act", lambda e: e.activation(out=sq.ap[0:d, 0:n], in_=bA.ap[0:d, 0:n], func=AF.Square, scale=float(d) ** -0.5),
             r=[bA], w=[sq])
        P.op("act", lambda e: e.activation(out=zc.ap[0:d, 0:n], in_=bA.ap[0:d, 0:n], func=AF.Copy), r=[bA], w=[zc])
        P.op("pe", lambda e: e.matmul(bB.ap[0:96, 0:n], lhsT=ones_bf.ap[0:d, 0:96], rhs=sq.ap[0:d, 0:n], start=True, stop=True),
             r=[sq, ones_bf], w=[bB])
        P.op("act", lambda e: e.activation(out=rs_.ap[0:d, 0:n], in_=bB.ap[0:d, 0:n], func=AF.Ln, bias=eps_t.ap[0:d, 0:1]), r=[bB, eps_t], w=[rs_])
        P.op("act", lambda e: e.activation(out=rs_.ap[0:d, 0:n], in_=rs_.ap[0:d, 0:n], func=AF.Exp, scale=-0.5), r=[rs_], w=[rs_])
        P.op("dve", lambda e: e.scalar_tensor_tensor(out=xn.ap[0:d, 0:n], in0=zc.ap[0:d, 0:n], scalar=hgain_t.ap[0:d, gcol:gcol + 1],
                                                     in1=rs_.ap[0:d, 0:n], op0=ALU.mult, op1=ALU.mult),
             r=[zc, rs_, hgain_t], w=[xn])
        P.op("pe", lambda e: e.matmul(bC.ap[0:96, 0:n], lhsT=rot_t.ap[0:d, ridx, 0:96], rhs=xn.ap[0:d, 0:n], start=True, stop=True),
             r=[xn, rot_t], w=[bC])
        P.op("pool", lambda e: e.tensor_tensor(out=t1.ap[0:d, 0:n], in0=xn.ap[0:d, 0:n], in1=rope_t.ap[0:d, 0, 0:n], op=ALU.mult),
             r=[xn, rope_t], w=[t1])
        P.op("dve", lambda e: e.tensor_tensor(out=t2.ap[0:d, 0:n], in0=bC.ap[0:d, 0:n], in1=rope_t.ap[0:d, 1, 0:n], op=ALU.mult),
             r=[bC, rope_t], w=[t2])
        P.op("pool", lambda e: e.tensor_tensor(out=o_.ap[0:d, 0:n], in0=t1.ap[0:d, 0:n], in1=t2.ap[0:d, 0:n], op=ALU.add),
             r=[t1, t2], w=[o_])
        P.dma(None, out_dram, o_.ap[0:96, 0:n], r=[o_])

    def do_block(t0, n, grp):
        P.dma("sp", xs.ap[:, :, 0:n], wview(xT)[:, :, t0:t0 + n], w=[xs])
        P.dma("pool", ropeM_t.ap[0:96, :, 0:n], ropeM[:, :, t0:t0 + n], w=[ropeM_t])
        P.dma("pool", ropeG_t.ap[0:96, :, 0:n], ropeG[:, :, t0:t0 + n], w=[ropeG_t])
        emit_norm_mod(C, xs, n, ones_bf, Acol, Bcol, grp, hT, tmpA, sqb, banks[6], rstd)
        P.dma(None, wview(hT_o)[:, :, t0:t0 + n], hT.ap[:, :, 0:n], r=[hT])
        if STOP <= 3:
            return
        for j in range(4):
            bg, ba = banks[6], banks[7]
            inproj(bg, C_GLU + 512 + j * 128, 128, n)
            s_ = sg[j % 2]
            P.op("act", lambda e, s_=s_, bg=bg: e.activation(out=s_.ap[:, 0:n], in_=bg.ap[:, 0:n], func=AF.Sigmoid), r=[bg], w=[s_])
            inproj(ba, C_GLU + j * 128, 128, n)
            y_ = yt[j % 2]
            P.op("dve", lambda e, y_=y_, ba=ba, s_=s_: e.tensor_tensor(out=y_.ap[:, 0:n], in0=ba.ap[:, 0:n], in1=s_.ap[:, 0:n], op=ALU.mult),
                 r=[ba, s_], w=[y_])
            P.dma(None, y_o[j * 128:(j + 1) * 128, t0:t0 + n], y_.ap[:, 0:n], r=[y_])
        if STOP <= 4:
            return
        for (zt, sqt, outn, col0, nch, dim, gt) in ((zq, sqa, qan, C_QA, 3, 384, gqa_t), (zkv, sqkv, kvn, C_KVA, 2, 256, gkva_t)):
            for j in range(nch):
                bk = banks[6 + (j % 2)]
                inproj(bk, col0 + j * 128, 128, n)
                P.op("act", lambda e, bk=bk, j=j, sqt=sqt, dim=dim: e.activation(out=sqt.ap[:, j, 0:n], in_=bk.ap[:, 0:n], func=AF.Square,
                                                                             scale=float(dim) ** -0.5), r=[bk], w=[sqt])
                P.op("act", lambda e, bk=bk, j=j, zt=zt: e.activation(out=zt.ap[:, j, 0:n], in_=bk.ap[:, 0:n], func=AF.Copy), r=[bk], w=[zt])
            bk = banks[6]
            for j in range(nch):
                P.op("pe", lambda e, j=j, sqt=sqt, bk=bk, nch=nch: e.matmul(bk.ap[:, 0:n], lhsT=ones_bf.ap[:, 0:128], rhs=sqt.ap[:, j, 0:n],
                                                                        start=(j == 0), stop=(j == nch - 1)), r=[sqt, ones_bf], w=[bk])
            P.op("act", lambda e, bk=bk: e.activation(out=rstd2.ap[:, 0:n], in_=bk.ap[:, 0:n], func=AF.Ln, bias=eps_t.ap[:, 0:1]), r=[bk, eps_t], w=[rstd2])
            P.op("act", lambda e, bk=bk: e.activation(out=rstd2.ap[:, 0:n], in_=rstd2.ap[:, 0:n], func=AF.Exp, scale=-0.5), r=[rstd2], w=[rstd2])
            for j in range(nch):
                P.op("dve", lambda e, j=j, zt=zt, outn=outn, gt=gt: e.scalar_tensor_tensor(
                    out=outn.ap[:, j, 0:n], in0=zt.ap[:, j, 0:n], scalar=gt.ap[:, j:j + 1], in1=rstd2.ap[:, 0:n],
                    op0=ALU.mult, op1=ALU.mult), r=[zt, gt, rstd2], w=[outn])
        if STOP <= 5:
            return
        bk = banks[7]
        inproj(bk, C_KR, 32, n)
        P.op("act", lambda e, bk=bk: e.activation(out=zkr.ap[0:32, 0:n], in_=bk.ap[0:32, 0:n], func=AF.Copy), r=[bk], w=[zkr])
        for h in range(8):
            def mq(bank, h=h):
                for j in range(3):
                    P.op("pe", lambda e, j=j: e.matmul(bank.ap[0:96, 0:n], lhsT=wqb.ap[:, j, h * 96:(h + 1) * 96], rhs=qan.ap[:, j, 0:n],
                                                       start=(j == 0), stop=(j == 2)), r=[wqb, qan], w=[bank])
            normrope(mq, 96, 0, ropeM_t, 0, qT_o[h, :, t0:t0 + n], n)
        if STOP <= 6:
            return
        for h in range(8):
            def mk(bank, h=h):
                for j in range(2):
                    P.op("pe", lambda e, j=j: e.matmul(bank.ap[0:96, 0:n], lhsT=wkb.ap[:, j, h, :], rhs=kvn.ap[:, j, 0:n],
                                                       start=(j == 0), stop=False), r=[wkb, kvn], w=[bank])
                P.op("pe", lambda e: e.matmul(bank.ap[0:96, 0:n], lhsT=shift_t.ap[0:32, 0:96], rhs=zkr.ap[0:32, 0:n],
                                              start=False, stop=True), r=[shift_t, zkr], w=[bank])
            normrope(mk, 96, 1, ropeM_t, 0, kT_o[h, :, t0:t0 + n], n)
        if STOP <= 7:
            return
        for h in range(8):
            def gq(bank, h=h):
                inproj(bank, C_GQ + h * 64, 64, n)
            normrope(gq, 64, 2, ropeG_t, 1, qT_o[8 + h, :, t0:t0 + n], n)
        for h in range(2):
            def gk(bank, h=h):
                inproj(bank, C_GK + h * 64, 64, n)
            normrope(gk, 64, 3, ropeG_t, 1, kT_o[8 + h, :, t0:t0 + n], n)
        if STOP <= 8:
            return
        for ti in range(n // 128):
            v_ = vt[ti % 2]
            bm, bg = banks[6], banks[7]
            for j in range(2):
                P.op("pe", lambda e, j=j, ti=ti, bm=bm: e.matmul(bm.ap[:, 0:512], lhsT=kvn.ap[:, j, ti * 128:(ti + 1) * 128],
                                                             rhs=wkv.ap[:, j].rearrange("p h d -> p (h d)"),
                                                             start=(j == 0), stop=(j == 1)), r=[kvn, wkv], w=[bm])
            for kc in range(8):
                P.op("pe", lambda e, kc=kc, ti=ti, bg=bg: e.matmul(bg.ap[:, 0:128], lhsT=hT.ap[:, kc, ti * 128:(ti + 1) * 128],
                                                               rhs=w_a.ap[:, kc, C_GV:C_GV + 128],
                                                               start=(kc == 0), stop=(kc == 7)), r=[hT, w_a], w=[bg])
            P.op("act", lambda e, v_=v_, bm=bm: e.activation(out=v_.ap[:, 0:8, 0:64], in_=bm.ap[:, 0:512].rearrange("p (h d) -> p h d", d=64),
                                                            func=AF.Copy), r=[bm], w=[v_])
            P.op("act", lambda e, v_=v_, bg=bg: e.activation(out=v_.ap[:, 8:10, 0:64], in_=bg.ap[:, 0:128].rearrange("p (h d) -> p h d", d=64),
                                                            func=AF.Copy), r=[bg], w=[v_])
            P.dma(None, vP_o[t0 + ti * 128:t0 + (ti + 1) * 128], v_.ap, r=[v_])
    for (t0_, n_, grp_) in BLOCKS:
        do_block(t0_, n_, grp_)
    return C.finish(), C


KB = 1024


def attention_phase(C, heads, NQ, key_blocks, ones_f):
    P = C.P
    banks = C.banks
    psum_all = C.psum
    m0 = P.mark()
    NBUF = 4
    HW = min(1024, NQ)
    NH_ = NQ // HW
    SW = min(512, HW)
    NS = HW // SW
    kbuf = [P.sb(f"kbuf{i}", [KB], BF16) for i in range(NBUF)]
    vbuf = [P.sb(f"vbuf{i}", [KB // 128, 65], BF16) for i in range(NBUF)]
    qbuf = [P.sb(f"qbuf{i}", [NQ], BF16) for i in range(2)]
    pT = [P.sb(f"pT{i}", [HW], BF16) for i in range(3)]
    rs = P.sb("rs", [NQ], F32)
    bcs = P.sb("bcs", [NQ], F32)
    obuf = [P.sb(f"obuf{i}", [NQ], BF16) for i in range(2)]
    Ob = banks[0:4]
    Sb = [banks[4:6], banks[6:8]]
    nblk_total = 0
    sidx = [0]
    pidx = [0]
    dk = 96
    for hi, hd in enumerate(heads):
        qb = qbuf[hi % 2]
        P.dma("sp", qb.ap[0:dk, :], hd["q"], w=[qb])
        nchunks_total = sum(n // 128 for _, n in key_blocks)
        ci = 0
        its = []
        binfo = []
        for (k0, kn) in key_blocks:
            kb_ = kbuf[nblk_total % NBUF]
            vb_ = vbuf[nblk_total % NBUF]
            nblk_total += 1
            nch = kn // 128
            binfo.append(dict(k0=k0, kn=kn, kb=kb_, vb=vb_, nch=nch, loaded=False))
            for c in range(nch):
                first = (ci == 0)
                last = (ci == nchunks_total - 1)
                ci += 1
                for half in range(NH_):
                    its.append(dict(kb=kb_, vb=vb_, c=c, kn=kn, nch=nch, half=half, first=first, last=last, blk=len(binfo) - 1))

        def load_upto(nb):
            for b in binfo[:nb + 1]:
                if not b["loaded"]:
                    b["loaded"] = True
                    k0, kn, nch = b["k0"], b["kn"], b["nch"]
                    P.dma("sp", b["kb"].ap[0:dk, 0:kn], hd["k"][:, k0:k0 + kn], w=[b["kb"]])
                    P.dma("pool", b["vb"].ap[:, 0:nch, :], hd["v"][k0:k0 + kn, :].rearrange("(p c) e -> p c e", c=nch), w=[b["vb"]])

        def emit_S(it):
            load_upto(it["blk"] + 2)
            sb2 = Sb[sidx[0] % 2]
            it["sb2"] = sb2
            it["sbase"] = (4 + 2 * (sidx[0] % 2)) * 512
            sidx[0] += 1
            for j in range(NS):
                q0 = it["half"] * HW + j * SW
                P.op("pe", lambda e, o=sb2[j].ap[:, 0:SW], l=it["kb"].ap[0:dk, it["c"]:it["kn"]:it["nch"]], r=qb.ap[0:dk, q0:q0 + SW]:
                     e.matmul(o, lhsT=l, rhs=r, start=True, stop=True), r=[it["kb"], qb], w=[sb2[j]])

        def emit_exp(it):
            pt = pT[pidx[0] % 3]
            pidx[0] += 1
            it["pt"] = pt
            if NS == 2:
                s_ap = psum_all[:, it["sbase"]:it["sbase"] + 1024]
            else:
                s_ap = psum_all[:, it["sbase"]:it["sbase"] + SW]
            P.op("act", lambda e, o=pt.ap[:, 0:HW], i=s_ap, sc=hd["scale"]: e.activation(out=o, in_=i, func=AF.Exp, scale=sc),
                 r=list(it["sb2"][0:NS]), w=[pt])

        def emit_PV(it):
            for j in range(NS):
                ob = Ob[it["half"] * NS + j]
                P.op("pe", lambda e, o=ob.ap[0:65, 0:SW], l=it["vb"].ap[:, it["c"], :], r=it["pt"].ap[:, j * SW:(j + 1) * SW],
                     f=it["first"], la=it["last"]: e.matmul(o, lhsT=l, rhs=r, start=f, stop=la), r=[it["vb"], it["pt"]], w=[ob])

        emit_S(its[0])
        emit_exp(its[0])
        for i in range(len(its)):
            if i + 1 < len(its):
                emit_S(its[i + 1])
            emit_PV(its[i])
            if i + 1 < len(its):
                emit_exp(its[i + 1])
        nob = NH_ * NS
        for j in range(nob):
            P.op("dve", lambda e, j=j: e.reciprocal(out=rs.ap[64:65, j * SW:(j + 1) * SW], in_=Ob[j].ap[64:65, 0:SW]), r=[Ob[j]], w=[rs])
        for j in range(nob):
            sbk = banks[4 + j]
            P.op("pe", lambda e, sbk=sbk, j=j: e.matmul(sbk.ap[0:64, 0:SW], lhsT=ones_f.ap[64:65, 0:64], rhs=rs.ap[64:65, j * SW:(j + 1) * SW],
                                                    start=True, stop=True), r=[rs, ones_f], w=[sbk])
            P.op("act", lambda e, sbk=sbk, j=j: e.activation(out=bcs.ap[0:64, j * SW:(j + 1) * SW], in_=sbk.ap[0:64, 0:SW], func=AF.Copy),
                 r=[sbk], w=[bcs])
        ob_ = obuf[hi % 2]
        for j in range(nob):
            P.op("dve", lambda e, j=j, ob_=ob_: e.tensor_tensor(out=ob_.ap[0:64, j * SW:(j + 1) * SW], in0=Ob[j].ap[0:64, 0:SW],
                                                               in1=bcs.ap[0:64, j * SW:(j + 1) * SW], op=ALU.mult),
                 r=[Ob[j], bcs], w=[ob_])
        P.dma("sp", hd["o"], ob_.ap[0:64, :], r=[ob_])
    P.release(m0)


def build_attn():
    C = Ctx()
    qT = C.inp("qT", [NQH, 96, TT], BF16)
    kTf = C.inp("kTf", [NKH, 96, NK], BF16)
    vPf = C.inp("vPf", [NKH, NK, 65], BF16)
    hT_d = C.inp("hT", [D, TT], BF16)
    xT = C.inp("xT", [D, TT])
    yh = C.inp("yh", [512, TL + 30])
    yc = C.inp("yc", [512, CTX + 30])
    modv = C.inp("modv", [128, 6, 8, 2])
    w_in = C.inp("w_in", [D, IN_COLS])
    bgc = C.inp("bgc", [128, 24])
    cwc = C.inp("cwc", [128, 4, 31])
    cvec = C.inp("cvec", [128, 4, 3])
    w_conv_out = C.inp("w_conv_out", [512, D])
    w_mla_o = C.inp("w_mla_o", [512, D])
    w_gqa_o = C.inp("w_gqa_o", [512, D])
    w_out = C.inp("w_out", [D, D])
    xm_o = C.out("xmT", [D, TT])
    h2_o = C.out("h2T", [D, TT], BF16)
    oT_d = C.scratch("oT_d", [NQH * 64, TT], BF16)
    P = C.start()
    eps_t = C.eps_t
    banks = C.banks
    ones_f = P.sb("ones_f", [64], F32)
    P.op("pool", lambda e: e.memset(ones_f.ap, 1.0), w=[ones_f])
    lat_blocks = [(0, CTX)] + [(CTX + b * KB, KB) for b in range(SEQ // KB)]
    heads_l, heads_c = [], []
    for h in range(NQH):
        kvh = h if h < 8 else 8 + (h - 8) // 4
        sc = MLA_SCALE if h < 8 else GQA_SCALE
        heads_l.append(dict(q=qT[h, :, 0:TL], k=kTf[kvh], v=vPf[kvh], o=oT_d[h * 64:(h + 1) * 64, 0:TL], scale=sc))
        heads_c.append(dict(q=qT[h, :, TL:TT], k=kTf[kvh], v=vPf[kvh], o=oT_d[h * 64:(h + 1) * 64, TL:TT], scale=sc))
    attention_phase(C, heads_c, CTX, [(0, CTX)], ones_f)
    attention_phase(C, heads_l, TL, lat_blocks, ones_f)
    P.barrier()
    C.cv_engs = ("pool", "act", "dve")
    C.nst = 2
    NB = 256
    ones_bf = P.sb("ones_bf", [128], BF16)
    P.op("pool", lambda e: e.memset(ones_bf.ap, 1.0), w=[ones_bf])
    ones512 = P.sb("ones512", [128], BF16)
    P.op("pool", lambda e: e.memset(ones512.ap, 1.0 / 512.0), w=[ones512])
    modc = C.load_f32("modc", modv, [6, 8, 2])
    G1 = T(modc.ap[:, 2], "G1")
    A2 = T(modc.ap[:, 3], "A2")
    B2 = T(modc.ap[:, 4], "B2")
    bg_t = C.load_f32("bg_t", bgc, [24])
    cw_t = C.load_f32("cw_t", cwc, [4, 31])
    cv_t = C.load_f32("cv_t", cvec, [4, 3])
    wiv = wview(w_in)
    w_g = C.load_bf16("w_g", wiv[:, :, C_GATE:IN_COLS], 8, 3072)
    w_co = C.load_bf16("w_co", wview(w_conv_out), 4, D)
    w_mo = C.load_bf16("w_mo", wview(w_mla_o), 4, D)
    w_go = C.load_bf16("w_go", wview(w_gqa_o), 4, D)
    w_o = C.load_bf16("w_o", wview(w_out), 8, D)
    xs = P.sb("xs", [8, NB], F32)
    hT = P.sb("hT", [8, NB], BF16)
    om = P.sb("om", [4, NB], BF16)
    og = P.sb("og", [4, NB], BF16)
    FB = []
    for i in range(2):
        FB.append(dict(ytile=P.sb(f"ytile{i}", [4, NB + 30], F32), scrF=P.sb(f"scrF{i}", [8, NB], F32), scrB=P.sb(f"scrB{i}", [8, NB], BF16),
                       mean_s=P.sb(f"mean_s{i}", [NB], F32), var_s=P.sb(f"var_s{i}", [NB], F32), rstd_c=P.sb(f"rstd_c{i}", [NB], F32),
                       ys=P.sb(f"ys{i}", [4, NB], BF16)))
    gate = [P.sb(f"gate{i}", [NB], F32) for i in range(3)]
    mtmp = [P.sb(f"mtmp{i}", [NB], F32) for i in range(2)]
    macc = P.sb("macc", [NB], F32)
    merged = P.sb("merged", [8, NB], BF16)
    xm = P.sb("xm", [8, NB], F32)
    h2 = P.sb("h2", [8, NB], BF16)
    rstd = P.sb("rstd", [NB], F32)
    sqb2 = P.sb("sqb2", [8, NB], BF16)
    ntmp = [P.sb(f"ntmp{i}", [NB], F32) for i in range(3)]
    oT3 = oT_d.rearrange("(b j p) t -> b p j t", b=2, p=128)
    MBLOCKS = [(i * NB, NB, 0) for i in range(TL // NB)] + [(TL, CTX, 1)]

    def front_pieces(bi):
        t0, n, grp = MBLOCKS[bi]
        fb = FB[bi % 2]
        ytile, scrF, scrB, mean_s, var_s, rstd_c, ys = fb["ytile"], fb["scrF"], fb["scrB"], fb["mean_s"], fb["var_s"], fb["rstd_c"], fb["ys"]
        pieces = []

        def load():
            ysrc = yh if grp == 0 else yc
            y0 = t0 if grp == 0 else 0
            P.dma("sp", ytile.ap[:, :, 0:n + 30], wview(ysrc)[:, :, y0:y0 + n + 30], w=[ytile])

        def conv_part(ch, k0, k1):
            def f():
                if ch == 0 and k0 == 0:
                    load()
                for k in range(k0, k1):
                    if k == 0:
                        P.op("dve", lambda e: e.tensor_scalar(out=scrF.ap[:, ch, 0:n], in0=ytile.ap[:, ch, 0:n], scalar1=cw_t.ap[:, ch, 0:1],
                                                              scalar2=cv_t.ap[:, ch, 0:1], op0=ALU.mult, op1=ALU.add),
                             r=[ytile, cw_t, cv_t], w=[scrF])
                    else:
                        P.op("dve", lambda e, k=k: e.scalar_tensor_tensor(out=scrF.ap[:, ch, 0:n], in0=ytile.ap[:, ch, k:k + n],
                                                                          scalar=cw_t.ap[:, ch, k:k + 1], in1=scrF.ap[:, ch, 0:n],
                                                                          op0=ALU.mult, op1=ALU.add), r=[ytile, cw_t, scrF], w=[scrF])
            return f
        for ch in range(4):
            pieces.append(conv_part(ch, 0, 16))
            pieces.append(conv_part(ch, 16, 31))

        def ln():
            P.op("act", lambda e: e.activation(out=scrB.ap[:, 0:4, 0:n], in_=scrF.ap[:, 0:4, 0:n], func=AF.Copy), r=[scrF], w=[scrB])
            P.op("act", lambda e: e.activation(out=scrB.ap[:, 4:8, 0:n], in_=scrF.ap[:, 0:4, 0:n], func=AF.Square, scale=512.0 ** -0.5),
                 r=[scrF], w=[scrB])
            bm, bv = banks[0], banks[1]
            for ch in range(4):
                P.op("pe", lambda e, ch=ch: e.matmul(bm.ap[:, 0:n], lhsT=ones512.ap[:, 0:128], rhs=scrB.ap[:, ch, 0:n], start=(ch == 0), stop=(ch == 3)),
                     r=[scrB, ones512], w=[bm])
            for ch in range(4):
                P.op("pe", lambda e, ch=ch: e.matmul(bv.ap[:, 0:n], lhsT=ones_bf.ap[:, 0:128], rhs=scrB.ap[:, 4 + ch, 0:n], start=(ch == 0), stop=(ch == 3)),
                     r=[scrB, ones_bf], w=[bv])
            P.op("act", lambda e: e.activation(out=mean_s.ap[:, 0:n], in_=bm.ap[:, 0:n], func=AF.Copy), r=[bm], w=[mean_s])
            P.op("dve", lambda e: e.tensor_tensor(out=var_s.ap[:, 0:n], in0=mean_s.ap[:, 0:n], in1=mean_s.ap[:, 0:n], op=ALU.mult), r=[mean_s], w=[var_s])
            P.op("dve", lambda e: e.tensor_tensor(out=var_s.ap[:, 0:n], in0=bv.ap[:, 0:n], in1=var_s.ap[:, 0:n], op=ALU.subtract), r=[bv, var_s], w=[var_s])
            P.op("act", lambda e: e.activation(out=rstd_c.ap[:, 0:n], in_=var_s.ap[:, 0:n], func=AF.Ln, bias=eps_t.ap[:, 0:1]), r=[var_s, eps_t], w=[rstd_c])
            P.op("act", lambda e: e.activation(out=rstd_c.ap[:, 0:n], in_=rstd_c.ap[:, 0:n], func=AF.Exp, scale=-0.5), r=[rstd_c], w=[rstd_c])
            for ch in range(4):
                P.op("dve", lambda e, ch=ch: e.tensor_tensor(out=scrF.ap[:, 4 + ch, 0:n], in0=scrF.ap[:, ch, 0:n], in1=mean_s.ap[:, 0:n], op=ALU.subtract),
                     r=[scrF, mean_s], w=[scrF])
                P.op("dve", lambda e, ch=ch: e.tensor_tensor(out=scrF.ap[:, 4 + ch, 0:n], in0=scrF.ap[:, 4 + ch, 0:n], in1=rstd_c.ap[:, 0:n], op=ALU.mult),
                     r=[scrF, rstd_c], w=[scrF])
                P.op("act", lambda e, ch=ch: e.activation(out=ys.ap[:, ch, 0:n], in_=scrF.ap[:, 4 + ch, 0:n], func=AF.Silu,
                                                          scale=cv_t.ap[:, ch, 1:2], bias=cv_t.ap[:, ch, 2:3]), r=[scrF, cv_t], w=[ys])
        pieces.append(ln)
        return pieces

    def back(bi, nxt):
        t0, n, grp = MBLOCKS[bi]
        ys = FB[bi % 2]["ys"]
        P.dma("sp", xs.ap[:, :, 0:n], wview(xT)[:, :, t0:t0 + n], w=[xs])
        P.dma("pool", hT.ap[:, :, 0:n], wview(hT_d)[:, :, t0:t0 + n], w=[hT])
        P.dma("pool", om.ap[:, :, 0:n], oT3[0][:, :, t0:t0 + n], w=[om])
        P.dma("pool", og.ap[:, :, 0:n], oT3[1][:, :, t0:t0 + n], w=[og])

        def merge_oc(oc):
            bc_, bm_, bg_ = banks[2], banks[3], banks[4]
            gb = [banks[5], banks[6], banks[7]]
            for (bk_, w_, src_) in ((bc_, w_co, ys), (bm_, w_mo, om), (bg_, w_go, og)):
                for ch in range(4):
                    P.op("pe", lambda e, ch=ch, bk_=bk_, w_=w_, src_=src_: e.matmul(bk_.ap[:, 0:n], lhsT=w_.ap[:, ch, oc * 128:(oc + 1) * 128],
                                                                                rhs=src_.ap[:, ch, 0:n], start=(ch == 0), stop=(ch == 3)),
                         r=[w_, src_], w=[bk_])
            for b in range(3):
                c0 = b * 1024 + oc * 128
                for kc in range(8):
                    P.op("pe", lambda e, kc=kc, b=b, c0=c0: e.matmul(gb[b].ap[:, 0:n], lhsT=w_g.ap[:, kc, c0:c0 + 128], rhs=hT.ap[:, kc, 0:n],
                                                                 start=(kc == 0), stop=(kc == 7)), r=[w_g, hT], w=[gb[b]])
                P.op("act", lambda e, b=b: e.activation(out=gate[b].ap[:, 0:n], in_=gb[b].ap[:, 0:n], func=AF.Sigmoid,
                                                        bias=bg_t.ap[:, b * 8 + oc:b * 8 + oc + 1]), r=[gb[b], bg_t], w=[gate[b]])
            P.op("dve", lambda e: e.tensor_tensor(out=macc.ap[:, 0:n], in0=bc_.ap[:, 0:n], in1=gate[0].ap[:, 0:n], op=ALU.mult),
                 r=[bc_, gate[0]], w=[macc])
            P.op("dve", lambda e: e.tensor_tensor(out=mtmp[0].ap[:, 0:n], in0=bm_.ap[:, 0:n], in1=gate[1].ap[:, 0:n], op=ALU.mult),
                 r=[bm_, gate[1]], w=[mtmp[0]])
            P.op("dve", lambda e: e.tensor_tensor(out=mtmp[1].ap[:, 0:n], in0=bg_.ap[:, 0:n], in1=gate[2].ap[:, 0:n], op=ALU.mult),
                 r=[bg_, gate[2]], w=[mtmp[1]])
            P.op("dve", lambda e: e.tensor_tensor(out=macc.ap[:, 0:n], in0=macc.ap[:, 0:n], in1=mtmp[0].ap[:, 0:n], op=ALU.add),
                 r=[macc, mtmp[0]], w=[macc])
            P.op("dve", lambda e: e.tensor_tensor(out=merged.ap[:, oc, 0:n], in0=macc.ap[:, 0:n], in1=mtmp[1].ap[:, 0:n], op=ALU.add),
                 r=[macc, mtmp[1]], w=[merged])

        for oc_ in range(8):
            merge_oc(oc_)
            if nxt:
                nxt.pop(0)()
        while nxt:
            nxt.pop(0)()

        def outproj_oc(oc):
            bk = banks[oc % 2]
            for kc in range(8):
                P.op("pe", lambda e, kc=kc: e.matmul(bk.ap[:, 0:n], lhsT=w_o.ap[:, kc, oc * 128:(oc + 1) * 128], rhs=merged.ap[:, kc, 0:n],
                                                     start=(kc == 0), stop=(kc == 7)), r=[w_o, merged], w=[bk])
            P.op("dve", lambda e: e.scalar_tensor_tensor(out=xm.ap[:, oc, 0:n], in0=bk.ap[:, 0:n], scalar=G1.ap[:, oc, grp:grp + 1],
                                                         in1=xs.ap[:, oc, 0:n], op0=ALU.mult, op1=ALU.add),
                 r=[bk, G1, xs], w=[xm])

        for oc_ in range(8):
            outproj_oc(oc_)
        P.dma(None, wview(xm_o)[:, :, t0:t0 + n], xm.ap[:, :, 0:n], r=[xm])
        emit_norm_mod(C, xm, n, ones_bf, A2, B2, grp, h2, ntmp, sqb2, banks[2], rstd)
        P.dma(None, wview(h2_o)[:, :, t0:t0 + n], h2.ap[:, :, 0:n], r=[h2])

    for f_ in front_pieces(0):
        f_()
    for bi in range(len(MBLOCKS)):
        nxt = front_pieces(bi + 1) if bi + 1 < len(MBLOCKS) else []
        back(bi, nxt)
    return C.finish(), C


def build_ffn():
    C = Ctx()
    xm_d = C.inp("xmT", [D, TT])
    h2l = C.inp("h2l", [D, TL + 2], BF16)
    h2c = C.inp("h2c", [D, CTX + 2], BF16)
    modv = C.inp("modv", [128, 6, 8, 2])
    w_up = C.inp("w_up", [D, 2 * D_FF])
    w_down = C.inp("w_down", [D_FF, D])
    fwc = C.inp("fwc", [128, 44, 4])
    x_o = C.out("xT_out", [D, TT])
    P = C.start()
    C.cv_engs = ("pool", "act", "dve")
    C.nst = 2
    banks = C.banks
    modc = C.load_f32("modc", modv, [6, 8, 2])
    G2 = T(modc.ap[:, 5], "G2")
    fw_t = C.load_f32("fw_t", fwc, [44, 4])
    wu = C.load_bf16("wu", wview(w_up), 8, 2 * D_FF)
    wd = C.load_bf16("wd", wview(w_down), 22, D)
    h2 = P.sb("h2", [8, 514], BF16)
    xmc = [P.sb(f"xmc{i}", [512], F32) for i in range(2)]
    act_t = P.sb("act_t", [22, 512], BF16)
    ta = [P.sb(f"ta{i}", [512], F32) for i in range(2)]
    tg = [P.sb(f"tg{i}", [512], F32) for i in range(2)]
    sgt = [P.sb(f"sgt{i}", [512], F32) for i in range(2)]
    cnt = [0]

    def conv_chunk(cc, n, dst):
        i = cnt[0]
        cnt[0] += 1
        bk = banks[2 + 2 * (i % 3)]
        bh = banks[3 + 2 * (i % 3)]
        for kc in range(8):
            P.op("pe", lambda e, kc=kc: e.matmul(bk.ap[:, 0:n], lhsT=wu.ap[:, kc, cc * 128:(cc + 1) * 128], rhs=h2.ap[:, kc, 1:n + 1],
                                                 start=(kc == 0), stop=(kc == 7)), r=[wu, h2], w=[bk])
        for kc in range(8):
            P.op("pe", lambda e, kc=kc: e.matmul(bh.ap[:, 0:2], lhsT=wu.ap[:, kc, cc * 128:(cc + 1) * 128], rhs=h2.ap[:, kc, 0:n + 2:n + 1],
                                                 start=(kc == 0), stop=(kc == 7)), r=[wu, h2], w=[bh])
        P.op("act", lambda e: e.activation(out=dst.ap[:, 0:n], in_=bk.ap[:, 0:n], func=AF.Identity, scale=fw_t.ap[:, cc, 1:2],
                                           bias=fw_t.ap[:, cc, 3:4]), r=[bk, fw_t], w=[dst])
        P.op("dve", lambda e: e.scalar_tensor_tensor(out=dst.ap[:, 1:n], in0=bk.ap[:, 0:n - 1], scalar=fw_t.ap[:, cc, 0:1], in1=dst.ap[:, 1:n],
                                                     op0=ALU.mult, op1=ALU.add), r=[bk, fw_t, dst], w=[dst])
        P.op("dve", lambda e: e.scalar_tensor_tensor(out=dst.ap[:, 0:n - 1], in0=bk.ap[:, 1:n], scalar=fw_t.ap[:, cc, 2:3], in1=dst.ap[:, 0:n - 1],
                                                     op0=ALU.mult, op1=ALU.add), r=[bk, fw_t, dst], w=[dst])
        P.op("dve", lambda e: e.scalar_tensor_tensor(out=dst.ap[:, 0:1], in0=bh.ap[:, 0:1], scalar=fw_t.ap[:, cc, 0:1], in1=dst.ap[:, 0:1],
                                                     op0=ALU.mult, op1=ALU.add), r=[bh, fw_t, dst], w=[dst])
        P.op("dve", lambda e: e.scalar_tensor_tensor(out=dst.ap[:, n - 1:n], in0=bh.ap[:, 1:2], scalar=fw_t.ap[:, cc, 2:3], in1=dst.ap[:, n - 1:n],
                                                     op0=ALU.mult, op1=ALU.add), r=[bh, fw_t, dst], w=[dst])

    def do_block(t0, n, grp):
        hsrc = h2l if grp == 0 else h2c
        h0 = t0 if grp == 0 else 0
        P.dma("sp", h2.ap[:, :, 0:n + 2], wview(hsrc)[:, :, h0:h0 + n + 2], w=[h2])
        for j in range(22):
            a_, g_, s_ = ta[j % 2], tg[j % 2], sgt[j % 2]
            conv_chunk(22 + j, n, g_)
            conv_chunk(j, n, a_)
            P.op("act", lambda e, g_=g_, s_=s_: e.activation(out=s_.ap[:, 0:n], in_=g_.ap[:, 0:n], func=AF.Silu), r=[g_], w=[s_])
            P.op("pool", lambda e, j=j, a_=a_, s_=s_: e.tensor_tensor(out=act_t.ap[:, j, 0:n], in0=a_.ap[:, 0:n], in1=s_.ap[:, 0:n], op=ALU.mult),
                 r=[a_, s_], w=[act_t])

        def down_oc(oc):
            bk = banks[oc % 2]
            xc = xmc[oc % 2]
            P.dma("pool", xc.ap[:, 0:n], xm_d[oc * 128:(oc + 1) * 128, t0:t0 + n], w=[xc])
            for j in range(22):
                P.op("pe", lambda e, j=j: e.matmul(bk.ap[:, 0:n], lhsT=wd.ap[:, j, oc * 128:(oc + 1) * 128], rhs=act_t.ap[:, j, 0:n],
                                                   start=(j == 0), stop=(j == 21)), r=[wd, act_t], w=[bk])
            P.op("dve", lambda e: e.scalar_tensor_tensor(out=xc.ap[:, 0:n], in0=bk.ap[:, 0:n], scalar=G2.ap[:, oc, grp:grp + 1],
                                                         in1=xc.ap[:, 0:n], op0=ALU.mult, op1=ALU.add),
                 r=[bk, G2, xc], w=[xc])
            P.dma("sp", x_o[oc * 128:(oc + 1) * 128, t0:t0 + n], xc.ap[:, 0:n], r=[xc])

        for oc_ in range(8):
            down_oc(oc_)

    for (t0_, n_, grp_) in BLOCKS:
        do_block(t0_, n_, grp_)
    return C.finish(), C


def _cols(v, n=128):
    v = np.asarray(v, np.float32)
    return np.ascontiguousarray(v.reshape(-1, n).T)


def _rope_tables():
    tabs = []
    tok = np.arange(SEQ)
    row = (tok // 64).astype(np.float32)
    col = (tok % 64).astype(np.float32)

    def mk(rot_dim, off):
        nf = rot_dim // 4
        inv = (1.0 / (10000.0 ** (np.arange(nf, dtype=np.float32) / nf))).astype(np.float32)
        ar = row[:, None] * inv
        ac = col[:, None] * inv
        ang = np.concatenate([ar, ar, ac, ac], axis=-1)
        cos = np.ones((96, SEQ), np.float32)
        sin = np.zeros((96, SEQ), np.float32)
        cos[off:off + rot_dim] = np.cos(ang).T
        sin[off:off + rot_dim] = np.sin(ang).T
        return cos, sin

    cm, sm = mk(32, 64)
    cg, sg = mk(64, 0)
    for c in range(NCORES):
        sl = slice(c * TL, (c + 1) * TL)
        M = np.zeros((96, 2, TT), np.float32)
        G = np.zeros((96, 2, TT), np.float32)
        M[:, 0, :TL] = cm[:, sl]; M[:, 1, :TL] = sm[:, sl]; M[:, 0, TL:] = 1.0
        G[:, 0, :TL] = cg[:, sl]; G[:, 1, :TL] = sg[:, sl]; G[:, 0, TL:] = 1.0
        tabs.append((M, G))
    rot = np.zeros((96, 2, 96), np.float32)
    for i in range(8):
        rot[72 + i, 0, 64 + i] = -1.0; rot[64 + i, 0, 72 + i] = 1.0
        rot[88 + i, 0, 80 + i] = -1.0; rot[80 + i, 0, 88 + i] = 1.0
    for i in range(16):
        rot[16 + i, 1, i] = -1.0; rot[i, 1, 16 + i] = 1.0
        rot[48 + i, 1, 32 + i] = -1.0; rot[32 + i, 1, 48 + i] = 1.0
    shift = np.zeros((32, 96), np.float32)
    for i in range(32):
        shift[i, 64 + i] = 1.0
    return tabs, rot.astype(NPBF), shift.astype(NPBF)


_PROGS = {}


def _prog(name):
    if name not in _PROGS:
        _PROGS[name] = {"pre": build_pre, "attn": build_attn, "ffn": build_ffn}[name]()[0]
    return _PROGS[name]


def _run(name, in_maps):
    res = run_bass_kernel_spmd(_prog(name), in_maps, core_ids=list(range(NCORES)))
    return res.results


def kernel(**inp):
    f32 = np.float32
    x = np.asarray(inp["x"], f32)[0]
    ctx = np.asarray(inp["ctx"], f32)[0]
    tabs, rot, shift = _rope_tables()
    c2 = np.ascontiguousarray(np.stack([_cols(inp["c"][0]), _cols(inp["c_ctx"])], axis=-1))
    xT = [np.ascontiguousarray(np.concatenate([x[c * TL:(c + 1) * TL], ctx], axis=0).T) for c in range(NCORES)]
    for l in range(DEPTH):
        hg = np.zeros((96, 4), f32)
        hg[:, 0] = inp["g_mla_q"][l]; hg[:, 1] = inp["g_mla_k"][l]
        hg[:64, 2] = inp["g_gqa_q"][l]; hg[:64, 3] = inp["g_gqa_k"][l]
        w_in_l = np.ascontiguousarray(inp["w_in"][l], dtype=f32)
        pre_in = []
        for c in range(NCORES):
            pre_in.append(dict(
                xT=xT[c], c2=c2, w_mod=np.ascontiguousarray(inp["w_mod"][l], dtype=f32), b_modc=_cols(inp["b_mod"][l]),
                gn1c=_cols(inp["g_norm1"][l]), gn2c=_cols(inp["g_norm2"][l]), w_in=w_in_l,
                gqac=_cols(inp["g_q_a"][l]), gkvac=_cols(inp["g_kv_a"][l]),
                w_q_b=np.ascontiguousarray(inp["w_q_b"][l], dtype=f32), w_kv_b=np.ascontiguousarray(inp["w_kv_b"][l], dtype=f32),
                hgain=hg, ropeM=tabs[c][0], ropeG=tabs[c][1], rotm=rot, shiftm=shift))
        r1 = _run("pre", pre_in)
        kTf = np.concatenate([r1[0]["kT"][:, :, TL:]] + [r1[c]["kT"][:, :, :TL] for c in range(NCORES)], axis=2)
        vPf = np.concatenate([r1[0]["vP"][TL:]] + [r1[c]["vP"][:TL] for c in range(NCORES)], axis=0)
        vPf = np.ascontiguousarray(vPf.transpose(1, 0, 2))
        kTf = np.ascontiguousarray(kTf)
        yfull = np.concatenate([r1[c]["y"][:, :TL] for c in range(NCORES)], axis=1)
        ypad = np.concatenate([np.zeros((512, 15), f32), yfull, np.zeros((512, 15), f32)], axis=1)
        cvec = np.ascontiguousarray(np.stack([_cols(inp["conv_dw_b"][l]), _cols(inp["conv_ln_g"][l]), _cols(inp["conv_ln_b"][l])], axis=-1))
        cwc = np.ascontiguousarray(np.asarray(inp["conv_dw_w"][l], f32).T.reshape(4, 128, 31).transpose(1, 0, 2))
        at_in = []
        for c in range(NCORES):
            yc = np.concatenate([np.zeros((512, 15), f32), r1[c]["y"][:, TL:], np.zeros((512, 15), f32)], axis=1)
            at_in.append(dict(
                qT=r1[c]["qT"], kTf=kTf, vPf=vPf, hT=r1[c]["hT"], xT=xT[c],
                yh=np.ascontiguousarray(ypad[:, c * TL:c * TL + TL + 30]), yc=np.ascontiguousarray(yc),
                modv=r1[c]["modv"], w_in=w_in_l, bgc=_cols(inp["b_gate"][l]), cwc=cwc, cvec=cvec,
                w_conv_out=np.ascontiguousarray(inp["w_conv_out"][l], dtype=f32), w_mla_o=np.ascontiguousarray(inp["w_mla_o"][l], dtype=f32),
                w_gqa_o=np.ascontiguousarray(inp["w_gqa_o"][l], dtype=f32), w_out=np.ascontiguousarray(inp["w_out"][l], dtype=f32)))
        r2 = _run("attn", at_in)
        h2full = np.concatenate([r2[c]["h2T"][:, :TL] for c in range(NCORES)], axis=1)
        zc = np.zeros((D, 1), NPBF)
        h2pad = np.concatenate([zc, h2full, zc], axis=1)
        fw = np.asarray(inp["ffn_dw_w"][l], f32)
        fwc = np.ascontiguousarray(np.stack([_cols(fw[0]), _cols(fw[1]), _cols(fw[2]), _cols(inp["ffn_dw_b"][l])], axis=-1))
        ff_in = []
        for c in range(NCORES):
            h2c = np.concatenate([zc, r2[c]["h2T"][:, TL:], zc], axis=1)
            ff_in.append(dict(
                xmT=r2[c]["xmT"], h2l=np.ascontiguousarray(h2pad[:, c * TL:c * TL + TL + 2]), h2c=np.ascontiguousarray(h2c),
                modv=r1[c]["modv"], w_up=np.ascontiguousarray(inp["w_up"][l], dtype=f32),
                w_down=np.ascontiguousarray(inp["w_down"][l], dtype=f32), fwc=fwc))
        r3 = _run("ffn", ff_in)
        xT = [r3[c]["xT_out"] for c in range(NCORES)]
    out = np.concatenate([xT[c][:, :TL].T for c in range(NCORES)], axis=0)[None]
    return np.ascontiguousarray(out.astype(np.float32))
```

```python
import numpy as np
import ml_dtypes
from contextlib import ExitStack
import concourse.bass as bass
import concourse.mybir as mybir
from concourse.bass_utils import run_bass_kernel_spmd

F32 = mybir.dt.float32
BF16 = mybir.dt.bfloat16
U8 = mybir.dt.uint8
AF = mybir.ActivationFunctionType
ALU = mybir.AluOpType
NPBF = ml_dtypes.bfloat16

NCORES = 8
D = 1024
SEQ = 16384
TL = SEQ // NCORES
CTX = 256
TT = TL + CTX
NK = CTX + SEQ
DEPTH = 4
EPS = 1e-6
BLOCKS = [(0, 512, 0), (512, 512, 0), (1024, 512, 0), (1536, 512, 0), (2048, 256, 1)]
NQH = 16
NKH = 10
MLA_SCALE = 96 ** -0.5
GQA_SCALE = 64 ** -0.5
C_GLU, C_QA, C_KVA, C_KR, C_GQ, C_GK, C_GV, C_GATE = 0, 1024, 1408, 1664, 1696, 2208, 2336, 2464
IN_COLS = 5536
D_FF = 2816

ENGS = ("pe", "act", "dve", "pool", "sp")
SEM_LIM = 30000
DMA_K = 12


class T:
    __slots__ = ("ap", "name")

    def __init__(self, ap, name):
        self.ap = ap
        self.name = name


class Prog:
    def __init__(self, nc, arena, arena_bytes):
        self.nc = nc
        self.ops = []
        self.last_w = {}
        self.readers = {}
        self.arena_bytes = arena_bytes
        self.arena = arena
        self.off = 0
        self.ndma = {e: 0 for e in ENGS}
        self.last_op = {e: None for e in ENGS}
        self.peak = 0
        self.dmaq = 0

    def sb(self, name, shape, dtype, parts=128):
        esz = 4 if dtype == F32 else 2
        n = int(np.prod(shape))
        nbytes = n * esz
        start = (self.off + 63) // 64 * 64
        assert start + nbytes <= self.arena_bytes, f"SBUF arena overflow {name} {start + nbytes}"
        self.off = start + nbytes
        self.peak = max(self.peak, self.off)
        ap = self.arena[0:parts, start:start + nbytes].bitcast(dtype)
        if len(shape) == 2:
            ap = ap.rearrange("p (a b) -> p a b", b=shape[1])
        elif len(shape) == 3:
            ap = ap.rearrange("p (a b c) -> p a b c", b=shape[1], c=shape[2])
        return T(ap, name)

    def mark(self):
        return self.off

    def release(self, mark):
        self.off = mark

    def _deps(self, idx, r, w):
        deps = set()
        for k in r:
            lw = self.last_w.get(k)
            if lw is not None:
                deps.add(lw)
        for k in w:
            lw = self.last_w.get(k)
            if lw is not None:
                deps.add(lw)
            for rd in self.readers.get(k, ()):
                deps.add(rd)
        for k in r:
            self.readers.setdefault(k, []).append(idx)
        for k in w:
            self.last_w[k] = idx
            self.readers[k] = []
        deps.discard(idx)
        return deps

    def op(self, eng, fn, r=(), w=()):
        idx = len(self.ops)
        deps = self._deps(idx, r, w)
        self.ops.append(dict(eng=eng, fn=fn, deps=deps, dma=False))
        self.last_op[eng] = idx
        return idx

    def dma(self, q, out, in_, r=(), w=()):
        if q is None:
            q = ("sp", "pool")[self.dmaq % 2]
            self.dmaq += 1
        idx = len(self.ops)
        deps = self._deps(idx, r, w)
        j = self.ndma[q]
        self.ndma[q] += 1
        self.ops.append(dict(eng=q, fn=lambda e: e.dma_start(out=out, in_=in_), deps=deps, dma=True, j=j))
        self.last_op[q] = idx
        return idx

    def barrier(self):
        lasts = {e: v for e, v in self.last_op.items() if v is not None}
        dmas = []
        for q in ENGS:
            cnt = 0
            for i in range(len(self.ops) - 1, -1, -1):
                o = self.ops[i]
                if o["dma"] and o["eng"] == q:
                    dmas.append(i)
                    cnt += 1
                    if cnt >= DMA_K:
                        break
        for e in ENGS:
            idx = len(self.ops)
            deps = set(v for ee, v in lasts.items() if ee != e and not self.ops[v]["dma"]) | set(dmas)
            self.ops.append(dict(eng=e, fn=None, deps=deps, dma=False))
            self.last_op[e] = idx
        self.last_w.clear()
        self.readers.clear()

    def emit(self, block, sems):
        ops = self.ops
        signaled = set()
        for o in ops:
            for d in o["deps"]:
                if not ops[d]["dma"]:
                    if ops[d]["eng"] == "pe" and o["eng"] == "pe":
                        continue
                    signaled.add(d)
        cnt = {e: 0 for e in ENGS}
        for i, o in enumerate(ops):
            if i in signaled:
                o["sig"] = cnt[o["eng"]]
                cnt[o["eng"]] += 1
        sem_iter = iter(sems)
        eng_sems = {}
        for e in ENGS:
            n_ep = cnt[e] // SEM_LIM + 1
            eng_sems[e] = [next(sem_iter) for _ in range(n_ep)]
        dma_sems = {}
        for q in ENGS:
            if self.ndma[q]:
                dma_sems[q] = [next(sem_iter) for _ in range(DMA_K)]
        per_eng = {e: [] for e in ENGS}
        for i, o in enumerate(ops):
            per_eng[o["eng"]].append(i)

        def run(e_name, e):
            waited = {x: -1 for x in ENGS}
            dwaited = set()
            for i in per_eng[e_name]:
                o = ops[i]
                need = {}
                for d in sorted(o["deps"]):
                    po = ops[d]
                    if po["dma"]:
                        if d not in dwaited:
                            dwaited.add(d)
                            q = po["eng"]
                            j = po["j"]
                            e.wait_ge(dma_sems[q][j % DMA_K], 16 * (j // DMA_K + 1))
                    else:
                        if po["eng"] == "pe" and e_name == "pe":
                            continue
                        s = po["sig"]
                        if s > waited[po["eng"]]:
                            need[po["eng"]] = max(need.get(po["eng"], -1), s)
                for pe_, s in need.items():
                    waited[pe_] = s
                    e.wait_ge(eng_sems[pe_][s // SEM_LIM], s % SEM_LIM + 1)
                if o["dma"]:
                    j = o["j"]
                    if j >= DMA_K:
                        e.wait_ge(dma_sems[e_name][j % DMA_K], 16 * (j // DMA_K))
                    ins = o["fn"](e)
                    ins.then_inc(dma_sems[e_name][j % DMA_K], 16)
                else:
                    if o["fn"] is None:
                        ins = e.nop() if "sig" in o else None
                    else:
                        ins = o["fn"](e)
                    if "sig" in o:
                        s = o["sig"]
                        ins.then_inc(eng_sems[e_name][s // SEM_LIM], 1)

        @block.tensor
        def _(e):
            run("pe", e)

        @block.scalar
        def _(e):
            run("act", e)

        @block.vector
        def _(e):
            run("dve", e)

        @block.gpsimd
        def _(e):
            run("pool", e)

        @block.sync
        def _(e):
            run("sp", e)


class Ctx:
    def __init__(self):
        self.nc = bass.Bass("TRN2", target_bir_lowering=False)
        self.es = ExitStack()
        self.ins = {}
        self.outs = {}

    def inp(self, name, shape, dt=F32):
        t = self.nc.dram_tensor(name, list(shape), dt, kind="ExternalInput").ap()
        self.ins[name] = t
        return t

    def out(self, name, shape, dt=F32):
        t = self.nc.dram_tensor(name, list(shape), dt, kind="ExternalOutput").ap()
        self.outs[name] = t
        return t

    def scratch(self, name, shape, dt):
        return self.nc.dram_tensor(name, list(shape), dt).ap()

    def start(self, arena_kb=204):
        nc, es = self.nc, self.es
        arena = es.enter_context(nc.sbuf_tensor("arena", [128, arena_kb * 1024], U8))
        psum = es.enter_context(nc.psum_tensor("psum", [128, 4096], F32))
        self.sems = [es.enter_context(nc.semaphore(f"s{i}")) for i in range(70)]
        self.block = es.enter_context(nc.Block())
        self.P = Prog(nc, arena[:], arena_kb * 1024)
        self.psum = psum[:]
        self.banks = [T(psum[:, i * 512:(i + 1) * 512], f"bank{i}") for i in range(8)]
        self.stage = None
        self.nstage = 0
        self.eps_t = self.P.sb("eps_t", [1], F32)
        self.P.op("pool", lambda e: e.memset(self.eps_t.ap, EPS), w=[self.eps_t])
        self.ncv = 0
        return self.P

    def finish(self):
        self.P.barrier()
        self.P.emit(self.block, self.sems)
        self.es.close()
        return self.nc

    def load_f32(self, name, src, shape, parts=128):
        t = self.P.sb(name, shape, F32)
        self.P.dma(None, t.ap[0:parts], src, w=[t])
        return t

    def load_bf16(self, name, src3, kc, ncols, parts=128, dst=None, dst_ap=None):
        P = self.P
        if self.stage is None:
            self.stage = [P.sb(f"stage{i}", [2048], F32) for i in range(self.nst)]
        if dst is None:
            dst = P.sb(name, [kc, ncols], BF16)
            dst_ap = dst.ap
        for k in range(kc):
            for c0 in range(0, ncols, 2048):
                cn = min(2048, ncols - c0)
                st = self.stage[self.nstage % self.nst]
                self.nstage += 1
                P.dma(None, st.ap[0:parts, 0:cn], src3[:, k, c0:c0 + cn], w=[st])
                eng = ("pool", "act", "dve")[self.ncv % 3] if self.cv_engs is None else self.cv_engs[self.ncv % len(self.cv_engs)]
                self.ncv += 1
                o = dst_ap[0:parts, k, c0:c0 + cn]
                i_ = st.ap[0:parts, 0:cn]
                if eng == "act":
                    P.op("act", lambda e, o=o, i_=i_: e.activation(out=o, in_=i_, func=AF.Copy), r=[st], w=[dst])
                elif eng == "pool":
                    P.op("pool", lambda e, o=o, i_=i_: e.tensor_copy(out=o, in_=i_), r=[st], w=[dst])
                else:
                    P.op("dve", lambda e, o=o, i_=i_: e.tensor_copy(out=o, in_=i_), r=[st], w=[dst])
        return dst

    cv_engs = None
    nst = 3


def wview(ap2d):
    return ap2d.rearrange("(kc p) n -> p kc n", p=128)


def emit_norm_mod(C, xs, n, ones_bf, Acol, Bcol, grp, hT, tmpA, sqbuf, bank_mean, rstd):
    P = C.P
    P.op("act", lambda e: e.activation(out=sqbuf.ap[:, :, 0:n], in_=xs.ap[:, :, 0:n], func=AF.Square, scale=1.0 / 32.0),
         r=[xs], w=[sqbuf])
    for kc in range(8):
        P.op("pe", lambda e, kc=kc: e.matmul(bank_mean.ap[:, 0:n], lhsT=ones_bf.ap[:, 0:128], rhs=sqbuf.ap[:, kc, 0:n],
                                             start=(kc == 0), stop=(kc == 7)), r=[sqbuf, ones_bf], w=[bank_mean])
    P.op("act", lambda e: e.activation(out=rstd.ap[:, 0:n], in_=bank_mean.ap[:, 0:n], func=AF.Ln, bias=C.eps_t.ap[:, 0:1]), r=[bank_mean, C.eps_t], w=[rstd])
    P.op("act", lambda e: e.activation(out=rstd.ap[:, 0:n], in_=rstd.ap[:, 0:n], func=AF.Exp, scale=-0.5), r=[rstd], w=[rstd])
    for kc in range(8):
        tA = tmpA[kc % len(tmpA)]
        P.op("dve", lambda e, kc=kc, tA=tA: e.scalar_tensor_tensor(out=tA.ap[:, 0:n], in0=xs.ap[:, kc, 0:n],
                                                                   scalar=Acol.ap[:, kc, grp:grp + 1], in1=rstd.ap[:, 0:n],
                                                                   op0=ALU.mult, op1=ALU.mult), r=[xs, Acol, rstd], w=[tA])
        P.op("act", lambda e, kc=kc, tA=tA: e.activation(out=hT.ap[:, kc, 0:n], in_=tA.ap[:, 0:n], func=AF.Identity,
                                                         bias=Bcol.ap[:, kc, grp:grp + 1]), r=[tA, Bcol], w=[hT])


def build_pre():
    C = Ctx()
    xT = C.inp("xT", [D, TT])
    c2 = C.inp("c2", [128, 8, 2])
    w_mod = C.inp("w_mod", [D, 6 * D])
    b_modc = C.inp("b_modc", [128, 48])
    gn1c = C.inp("gn1c", [128, 8])
    gn2c = C.inp("gn2c", [128, 8])
    w_in = C.inp("w_in", [D, IN_COLS])
    gqac = C.inp("gqac", [128, 3])
    gkvac = C.inp("gkvac", [128, 2])
    w_q_b = C.inp("w_q_b", [384, 768])
    w_kv_b = C.inp("w_kv_b", [256, 1024])
    hgain = C.inp("hgain", [96, 4])
    ropeM = C.inp("ropeM", [96, 2, TT])
    ropeG = C.inp("ropeG", [96, 2, TT])
    rotm = C.inp("rotm", [96, 2, 96], BF16)
    shiftm = C.inp("shiftm", [32, 96], BF16)
    modv_o = C.out("modv", [128, 6, 8, 2])
    hT_o = C.out("hT", [D, TT], BF16)
    qT_o = C.out("qT", [NQH, 96, TT], BF16)
    kT_o = C.out("kT", [NKH, 96, TT], BF16)
    vP_o = C.out("vP", [TT, NKH, 65], BF16)
    y_o = C.out("y", [512, TT])
    P = C.start()
    eps_t = C.eps_t
    C.cv_engs = ("pool", "act")
    C.nst = 2
    banks = C.banks

    ones_bf = P.sb("ones_bf", [128], BF16)
    P.op("pool", lambda e: e.memset(ones_bf.ap, 1.0), w=[ones_bf])
    rot_t = P.sb("rot_t", [2, 96], BF16)
    P.dma(None, rot_t.ap[0:96], rotm, w=[rot_t])
    shift_t = P.sb("shift_t", [96], BF16)
    P.dma(None, shift_t.ap[0:32], shiftm, w=[shift_t])
    hgain_t = C.load_f32("hgain_t", hgain, [4], parts=96)
    gqa_t = C.load_f32("gqa_t", gqac, [3])
    gkva_t = C.load_f32("gkva_t", gkvac, [2])
    gn1_t = C.load_f32("gn1_t", gn1c, [8])
    gn2_t = C.load_f32("gn2_t", gn2c, [8])
    bmod_t = C.load_f32("bmod_t", b_modc, [48])

    c2_t = C.load_f32("c2_t", c2, [8, 2])
    sc_t = P.sb("sc_t", [8, 2], F32)
    P.op("act", lambda e: e.activation(out=sc_t.ap, in_=c2_t.ap, func=AF.Silu), r=[c2_t], w=[sc_t])
    modv = P.sb("modv", [6, 8, 2], F32)
    m0 = P.mark()
    wm = [P.sb(f"wm{i}", [8, 512], F32) for i in range(2)]
    wmv = wview(w_mod)
    for g in range(12):
        wt = wm[g % 2]
        P.dma(None, wt.ap, wmv[:, :, g * 512:(g + 1) * 512], w=[wt])
        for s in range(4):
            j = g * 4 + s
            bk = banks[j % 2]
            for kc in range(8):
                P.op("pe", lambda e, bk=bk, wt=wt, kc=kc, s=s: e.matmul(bk.ap[:, 0:2], lhsT=wt.ap[:, kc, s * 128:(s + 1) * 128],
                                                                     rhs=sc_t.ap[:, kc, :], start=(kc == 0), stop=(kc == 7)),
                     r=[wt, sc_t], w=[bk])
            P.op("dve", lambda e, bk=bk, j=j: e.tensor_scalar(out=modv.ap[:, j // 8, j % 8, :], in0=bk.ap[:, 0:2],
                                                            scalar1=bmod_t.ap[:, j:j + 1], scalar2=None, op0=ALU.add),
                 r=[bk, bmod_t], w=[modv])
    P.barrier()
    P.release(m0)
    modc = P.sb("modc", [6, 8, 2], F32)
    for (dst, src, gn) in ((0, 1, gn1_t), (3, 4, gn2_t)):
        for cnd in range(2):
            P.op("dve", lambda e, dst=dst, src=src, gn=gn, cnd=cnd: e.scalar_tensor_tensor(
                out=modc.ap[:, dst, :, cnd], in0=modv.ap[:, src, :, cnd], scalar=1.0, in1=gn.ap,
                op0=ALU.add, op1=ALU.mult), r=[modv, gn], w=[modc])
    for (dst, src) in ((1, 0), (2, 2), (4, 3), (5, 5)):
        P.op("dve", lambda e, dst=dst, src=src: e.tensor_copy(out=modc.ap[:, dst], in_=modv.ap[:, src]), r=[modv], w=[modc])
    P.dma(None, modv_o, modc.ap, r=[modc])
    import os
    STOP = int(os.environ.get('PRE_STOP', '9'))
    if STOP <= 1:
        return C.finish(), C
    Acol = T(modc.ap[:, 0], "A1")
    Bcol = T(modc.ap[:, 1], "B1")

    wiv = wview(w_in)
    w_a = C.load_bf16("w_a", wiv[:, :, 0:C_GATE], 8, C_GATE)
    wqb = C.load_bf16("wqb", wview(w_q_b), 3, 768)
    wkb = P.sb("wkb", [2, 8, 96], BF16)
    P.op("pool", lambda e: e.memset(wkb.ap, 0.0), w=[wkb])
    wkv = P.sb("wkv", [2, 8, 64], BF16)
    wkbv = wview(w_kv_b).rearrange("p kc (h t) -> p kc h t", t=128)
    stg = C.stage
    for kc in range(2):
        st = stg[C.nstage % C.nst]
        C.nstage += 1
        P.dma(None, st.ap[:, 0:1024], wview(w_kv_b)[:, kc, :], w=[st])
        sv = st.ap[:, 0:1024].rearrange("p (h t) -> p h t", t=128)
        P.op("pool", lambda e, kc=kc, sv=sv: e.tensor_copy(out=wkb.ap[:, kc, :, 0:64], in_=sv[:, :, 0:64]), r=[st], w=[wkb])
        P.op("pool", lambda e, kc=kc, sv=sv: e.tensor_copy(out=wkv.ap[:, kc, :, :], in_=sv[:, :, 64:128]), r=[st], w=[wkv])

    if STOP <= 2:
        return C.finish(), C
    xs = P.sb("xs", [8, 512], F32)
    sqb = P.sb("sqb", [8, 512], BF16)
    tmpA = [P.sb(f"tmpA{i}", [512], F32) for i in range(3)]
    hT = P.sb("hT", [8, 512], BF16)
    rstd = P.sb("rstd", [512], F32)
    sg = [P.sb(f"sg{i}", [512], F32) for i in range(2)]
    yt = [P.sb(f"yt{i}", [512], F32) for i in range(2)]
    zq = P.sb("zq", [3, 512], F32)
    sqa = P.sb("sqa", [3, 512], BF16)
    qan = P.sb("qan", [3, 512], BF16)
    zkv = P.sb("zkv", [2, 512], F32)
    sqkv = P.sb("sqkv", [2, 512], BF16)
    kvn = P.sb("kvn", [2, 512], BF16)
    zkr = P.sb("zkr", [512], BF16)
    rstd2 = P.sb("rstd2", [512], F32)
    ropeM_t = P.sb("ropeM_t", [2, 512], F32)
    ropeG_t = P.sb("ropeG_t", [2, 512], F32)
    vt = [P.sb(f"vt{i}", [NKH, 65], BF16) for i in range(2)]
    for v_ in vt:
        P.op("pool", lambda e, v_=v_: e.memset(v_.ap, 1.0), w=[v_])
    NU = 4
    u_sq = [P.sb(f"u_sq{i}", [512], BF16) for i in range(NU)]
    u_zc = [P.sb(f"u_zc{i}", [512], F32) for i in range(NU)]
    u_rs = [P.sb(f"u_rs{i}", [512], F32) for i in range(NU)]
    u_xn = [P.sb(f"u_xn{i}", [512], BF16) for i in range(NU)]
    u_t1 = [P.sb(f"u_t1{i}", [512], F32) for i in range(NU)]
    u_t2 = [P.sb(f"u_t2{i}", [512], F32) for i in range(NU)]
    u_o = [P.sb(f"u_o{i}", [512], BF16) for i in range(NU)]
    u_og = [P.sb(f"u_og{i}", [512], BF16) for i in range(NU)]
    for o_ in u_og:
        P.op("pool", lambda e, o_=o_: e.memset(o_.ap, 0.0), w=[o_])
    ucnt = [0]

    def inproj(bank, col0, m, n):
        for kc in range(8):
            P.op("pe", lambda e, kc=kc: e.matmul(bank.ap[0:m, 0:n], lhsT=w_a.ap[:, kc, col0:col0 + m], rhs=hT.ap[:, kc, 0:n],
                                                 start=(kc == 0), stop=(kc == 7)), r=[w_a, hT], w=[bank])

    def normrope(main_fn, d, gcol, rope_t, ridx, out_dram, n):
        u = ucnt[0] % NU
        ucnt[0] += 1
        bA, bB = banks[2 * u], banks[2 * u + 1]
        bC = bA
        main_fn(bA)
        sq, zc, rs_, xn, t1, t2, o_ = u_sq[u], u_zc[u], u_rs[u], u_xn[u], u_t1[u], u_t2[u], (u_o[u] if d == 96 else u_og[u])
        P.op("act", lambda e: e.activation(out=sq.ap[0:d, 0:n], in_=bA.ap[0:d, 0:n], func=AF.Square, scale=float(d) ** -0.5),
             r=[bA], w=[sq])
        P.op("act", lambda e: e.activation(out=zc.ap[0:d, 0:n], in_=bA.ap[0:d, 0:n], func=AF.Copy), r=[bA], w=[zc])
        P.op("pe", lambda e: e.matmul(bB.ap[0:96, 0:n], lhsT=ones_bf.ap[0:d, 0:96], rhs=sq.ap[0:d, 0:n], start=True, stop=True),
             r=[sq, ones_bf], w=[bB])
        P.op("act", lambda e: e.activation(out=rs_.ap[0:d, 0:n], in_=bB.ap[0:d, 0:n], func=AF.Ln, bias=eps_t.ap[0:d, 0:1]), r=[bB, eps_t], w=[rs_])
        P.op("act", lambda e: e.activation(out=rs_.ap[0:d, 0:n], in_=rs_.ap[0:d, 0:n], func=AF.Exp, scale=-0.5), r=[rs_], w=[rs_])
        P.op("dve", lambda e: e.scalar_tensor_tensor(out=xn.ap[0:d, 0:n], in0=zc.ap[0:d, 0:n], scalar=hgain_t.ap[0:d, gcol:gcol + 1],
                                                     in1=rs_.ap[0:d, 0:n], op0=ALU.mult, op1=ALU.mult),
             r=[zc, rs_, hgain_t], w=[xn])
        P.op("pe", lambda e: e.matmul(bC.ap[0:96, 0:n], lhsT=rot_t.ap[0:d, ridx, 0:96], rhs=xn.ap[0:d, 0:n], start=True, stop=True),
             r=[xn, rot_t], w=[bC])
        P.op("pool", lambda e: e.tensor_tensor(out=t1.ap[0:d, 0:n], in0=xn.ap[0:d, 0:n], in1=rope_t.ap[0:d, 0, 0:n], op=ALU.mult),
             r=[xn, rope_t], w=[t1])
        P.op("dve", lambda e: e.tensor_tensor(out=t2.ap[0:d, 0:n], in0=bC.ap[0:d, 0:n], in1=rope_t.ap[0:d, 1, 0:n], op=ALU.mult),
             r=[bC, rope_t], w=[t2])
        P.op("pool", lambda e: e.tensor_tensor(out=o_.ap[0:d, 0:n], in0=t1.ap[0:d, 0:n], in1=t2.ap[0:d, 0:n], op=ALU.add),
             r=[t1, t2], w=[o_])
        P.dma(None, out_dram, o_.ap[0:96, 0:n], r=[o_])

    def do_block(t0, n, grp):
        P.dma("sp", xs.ap[:, :, 0:n], wview(xT)[:, :, t0:t0 + n], w=[xs])
        P.dma("pool", ropeM_t.ap[0:96, :, 0:n], ropeM[:, :, t0:t0 + n], w=[ropeM_t])
        P.dma("pool", ropeG_t.ap[0:96, :, 0:n], ropeG[:, :, t0:t0 + n], w=[ropeG_t])
        emit_norm_mod(C, xs, n, ones_bf, Acol, Bcol, grp, hT, tmpA, sqb, banks[6], rstd)
        P.dma(None, wview(hT_o)[:, :, t0:t0 + n], hT.ap[:, :, 0:n], r=[hT])
        if STOP <= 3:
            return
        for j in range(4):
            bg, ba = banks[6], banks[7]
            inproj(bg, C_GLU + 512 + j * 128, 128, n)
            s_ = sg[j % 2]
            P.op("act", lambda e, s_=s_, bg=bg: e.activation(out=s_.ap[:, 0:n], in_=bg.ap[:, 0:n], func=AF.Sigmoid), r=[bg], w=[s_])
            inproj(ba, C_GLU + j * 128, 128, n)
            y_ = yt[j % 2]
            P.op("dve", lambda e, y_=y_, ba=ba, s_=s_: e.tensor_tensor(out=y_.ap[:, 0:n], in0=ba.ap[:, 0:n], in1=s_.ap[:, 0:n], op=ALU.mult),
                 r=[ba, s_], w=[y_])
            P.dma(None, y_o[j * 128:(j + 1) * 128, t0:t0 + n], y_.ap[:, 0:n], r=[y_])
        if STOP <= 4:
            return
        for (zt, sqt, outn, col0, nch, dim, gt) in ((zq, sqa, qan, C_QA, 3, 384, gqa_t), (zkv, sqkv, kvn, C_KVA, 2, 256, gkva_t)):
            for j in range(nch):
                bk = banks[6 + (j % 2)]
                inproj(bk, col0 + j * 128, 128, n)
                P.op("act", lambda e, bk=bk, j=j, sqt=sqt, dim=dim: e.activation(out=sqt.ap[:, j, 0:n], in_=bk.ap[:, 0:n], func=AF.Square,
                                                                             scale=float(dim) ** -0.5), r=[bk], w=[sqt])
                P.op("act", lambda e, bk=bk, j=j, zt=zt: e.activation(out=zt.ap[:, j, 0:n], in_=bk.ap[:, 0:n], func=AF.Copy), r=[bk], w=[zt])
            bk = banks[6]
            for j in range(nch):
                P.op("pe", lambda e, j=j, sqt=sqt, bk=bk, nch=nch: e.matmul(bk.ap[:, 0:n], lhsT=ones_bf.ap[:, 0:128], rhs=sqt.ap[:, j, 0:n],
                                                                        start=(j == 0), stop=(j == nch - 1)), r=[sqt, ones_bf], w=[bk])
            P.op("act", lambda e, bk=bk: e.activation(out=rstd2.ap[:, 0:n], in_=bk.ap[:, 0:n], func=AF.Ln, bias=eps_t.ap[:, 0:1]), r=[bk, eps_t], w=[rstd2])
            P.op("act", lambda e, bk=bk: e.activation(out=rstd2.ap[:, 0:n], in_=rstd2.ap[:, 0:n], func=AF.Exp, scale=-0.5), r=[rstd2], w=[rstd2])
            for j in range(nch):
                P.op("dve", lambda e, j=j, zt=zt, outn=outn, gt=gt: e.scalar_tensor_tensor(
                    out=outn.ap[:, j, 0:n], in0=zt.ap[:, j, 0:n], scalar=gt.ap[:, j:j + 1], in1=rstd2.ap[:, 0:n],
                    op0=ALU.mult, op1=ALU.mult), r=[zt, gt, rstd2], w=[outn])
        if STOP <= 5:
            return
        bk = banks[7]
        inproj(bk, C_KR, 32, n)
        P.op("act", lambda e, bk=bk: e.activation(out=zkr.ap[0:32, 0:n], in_=bk.ap[0:32, 0:n], func=AF.Copy), r=[bk], w=[zkr])
        for h in range(8):
            def mq(bank, h=h):
                for j in range(3):
                    P.op("pe", lambda e, j=j: e.matmul(bank.ap[0:96, 0:n], lhsT=wqb.ap[:, j, h * 96:(h + 1) * 96], rhs=qan.ap[:, j, 0:n],
                                                       start=(j == 0), stop=(j == 2)), r=[wqb, qan], w=[bank])
            normrope(mq, 96, 0, ropeM_t, 0, qT_o[h, :, t0:t0 + n], n)
        if STOP <= 6:
            return
        for h in range(8):
            def mk(bank, h=h):
                for j in range(2):
                    P.op("pe", lambda e, j=j: e.matmul(bank.ap[0:96, 0:n], lhsT=wkb.ap[:, j, h, :], rhs=kvn.ap[:, j, 0:n],
                                                       start=(j == 0), stop=False), r=[wkb, kvn], w=[bank])
                P.op("pe", lambda e: e.matmul(bank.ap[0:96, 0:n], lhsT=shift_t.ap[0:32, 0:96], rhs=zkr.ap[0:32, 0:n],
                                              start=False, stop=True), r=[shift_t, zkr], w=[bank])
            normrope(mk, 96, 1, ropeM_t, 0, kT_o[h, :, t0:t0 + n], n)
        if STOP <= 7:
            return
        for h in range(8):
            def gq(bank, h=h):
                inproj(bank, C_GQ + h * 64, 64, n)
            normrope(gq, 64, 2, ropeG_t, 1, qT_o[8 + h, :, t0:t0 + n], n)
        for h in range(2):
            def gk(bank, h=h):
                inproj(bank, C_GK + h * 64, 64, n)
            normrope(gk, 64, 3, ropeG_t, 1, kT_o[8 + h, :, t0:t0 + n], n)
        if STOP <= 8:
            return
        for ti in range(n // 128):
            v_ = vt[ti % 2]
            bm, bg = banks[6], banks[7]
            for j in range(2):
                P.op("pe", lambda e, j=j, ti=ti, bm=bm: e.matmul(bm.ap[:, 0:512], lhsT=kvn.ap[:, j, ti * 128:(ti + 1) * 128],
                                                             rhs=wkv.ap[:, j].rearrange("p h d -> p (h d)"),
                                                             start=(j == 0), stop=(j == 1)), r=[kvn, wkv], w=[bm])
            for kc in range(8):
                P.op("pe", lambda e, kc=kc, ti=ti, bg=bg: e.matmul(bg.ap[:, 0:128], lhsT=hT.ap[:, kc, ti * 128:(ti + 1) * 128],
                                                               rhs=w_a.ap[:, kc, C_GV:C_GV + 128],
                                                               start=(kc == 0), stop=(kc == 7)), r=[hT, w_a], w=[bg])
            P.op("act", lambda e, v_=v_, bm=bm: e.activation(out=v_.ap[:, 0:8, 0:64], in_=bm.ap[:, 0:512].rearrange("p (h d) -> p h d", d=64),
                                                            func=AF.Copy), r=[bm], w=[v_])
            P.op("act", lambda e, v_=v_, bg=bg: e.activation(out=v_.ap[:, 8:10, 0:64], in_=bg.ap[:, 0:128].rearrange("p (h d) -> p h d", d=64),
                                                            func=AF.Copy), r=[bg], w=[v_])
            P.dma(None, vP_o[t0 + ti * 128:t0 + (ti + 1) * 128], v_.ap, r=[v_])
    for (t0_, n_, grp_) in BLOCKS:
        do_block(t0_, n_, grp_)
    return C.finish(), C


KB = 1024


def attention_phase(C, heads, NQ, key_blocks, ones_f):
    P = C.P
    banks = C.banks
    psum_all = C.psum
    m0 = P.mark()
    NBUF = 4
    HW = min(1024, NQ)
    assert NQ == HW
    SW = min(512, HW)
    NS = HW // SW
    nob = NS
    kbuf = [P.sb(f"kbuf{i}", [KB], BF16) for i in range(NBUF)]
    vbuf = [P.sb(f"vbuf{i}", [KB // 128, 65], BF16) for i in range(NBUF)]
    qbuf = [P.sb(f"qbuf{i}", [NQ], BF16) for i in range(2)]
    NPT = 4
    pT = [P.sb(f"pT{i}", [HW], BF16) for i in range(NPT)]
    rs = P.sb("rs", [NQ], F32)
    bcs = P.sb("bcs", [NQ], F32)
    obuf = [P.sb(f"obuf{i}", [NQ], BF16) for i in range(2)]
    Ob = banks[0:nob]
    Sb = []
    bi_ = nob
    while bi_ + NS <= 8:
        Sb.append(list(range(bi_, bi_ + NS)))
        bi_ += NS
    Sb = Sb[:3]
    LA = len(Sb) - 1
    nblk_total = 0
    sidx = [0]
    pidx = [0]
    dk = 96
    for hi, hd in enumerate(heads):
        qb = qbuf[hi % 2]
        P.dma("sp", qb.ap[0:dk, :], hd["q"], w=[qb])
        nchunks_total = sum(n // 128 for _, n in key_blocks)
        ci = 0
        its = []
        binfo = []
        for (k0, kn) in key_blocks:
            kb_ = kbuf[nblk_total % NBUF]
            vb_ = vbuf[nblk_total % NBUF]
            nblk_total += 1
            nch = kn // 128
            binfo.append(dict(k0=k0, kn=kn, kb=kb_, vb=vb_, nch=nch, loaded=False))
            for c in range(nch):
                its.append(dict(kb=kb_, vb=vb_, c=c, kn=kn, nch=nch, first=(ci == 0), last=(ci == nchunks_total - 1), blk=len(binfo) - 1))
                ci += 1

        def load_upto(nb):
            for b in binfo[:nb + 1]:
                if not b["loaded"]:
                    b["loaded"] = True
                    k0, kn, nch = b["k0"], b["kn"], b["nch"]
                    P.dma("sp", b["kb"].ap[0:dk, 0:kn], hd["k"][:, k0:k0 + kn], w=[b["kb"]])
                    P.dma("pool", b["vb"].ap[:, 0:nch, :], hd["v"][k0:k0 + kn, :].rearrange("(p c) e -> p c e", c=nch), w=[b["vb"]])

        def emit_S(it):
            load_upto(it["blk"] + 2)
            sbi = Sb[sidx[0] % len(Sb)]
            sidx[0] += 1
            it["sbi"] = sbi
            for j in range(NS):
                bk = banks[sbi[j]]
                P.op("pe", lambda e, o=bk.ap[:, 0:SW], l=it["kb"].ap[0:dk, it["c"]:it["kn"]:it["nch"]], r=qb.ap[0:dk, j * SW:(j + 1) * SW]:
                     e.matmul(o, lhsT=l, rhs=r, start=True, stop=True), r=[it["kb"], qb], w=[bk])

        def emit_exp(it):
            pt = pT[pidx[0] % NPT]
            pidx[0] += 1
            it["pt"] = pt
            sbi = it["sbi"]
            s_ap = psum_all[:, sbi[0] * 512:sbi[0] * 512 + (1024 if NS == 2 else SW)]
            P.op("act", lambda e, o=pt.ap[:, 0:HW], i=s_ap, sc=hd["scale"]: e.activation(out=o, in_=i, func=AF.Exp, scale=sc),
                 r=[banks[x] for x in sbi], w=[pt])

        def emit_PV(it):
            for j in range(NS):
                ob = Ob[j]
                P.op("pe", lambda e, o=ob.ap[0:65, 0:SW], l=it["vb"].ap[:, it["c"], :], r=it["pt"].ap[:, j * SW:(j + 1) * SW],
                     f=it["first"], la=it["last"]: e.matmul(o, lhsT=l, rhs=r, start=f, stop=la), r=[it["vb"], it["pt"]], w=[ob])

        n_it = len(its)
        for i in range(min(LA, n_it)):
            emit_S(its[i])
        emit_exp(its[0])
        for i in range(n_it):
            if i + LA < n_it:
                emit_S(its[i + LA])
            if i + 1 < n_it:
                emit_exp(its[i + 1])
            emit_PV(its[i])
        for j in range(nob):
            P.op("dve", lambda e, j=j: e.reciprocal(out=rs.ap[64:65, j * SW:(j + 1) * SW], in_=Ob[j].ap[64:65, 0:SW]), r=[Ob[j]], w=[rs])
        for j in range(nob):
            sbk = banks[nob + j]
            P.op("pe", lambda e, sbk=sbk, j=j: e.matmul(sbk.ap[0:64, 0:SW], lhsT=ones_f.ap[64:65, 0:64], rhs=rs.ap[64:65, j * SW:(j + 1) * SW],
                                                    start=True, stop=True), r=[rs, ones_f], w=[sbk])
            P.op("act", lambda e, sbk=sbk, j=j: e.activation(out=bcs.ap[0:64, j * SW:(j + 1) * SW], in_=sbk.ap[0:64, 0:SW], func=AF.Copy),
                 r=[sbk], w=[bcs])
        ob_ = obuf[hi % 2]
        for j in range(nob):
            P.op("dve", lambda e, j=j, ob_=ob_: e.tensor_tensor(out=ob_.ap[0:64, j * SW:(j + 1) * SW], in0=Ob[j].ap[0:64, 0:SW],
                                                               in1=bcs.ap[0:64, j * SW:(j + 1) * SW], op=ALU.mult),
                 r=[Ob[j], bcs], w=[ob_])
        P.dma("sp", hd["o"], ob_.ap[0:64, :], r=[ob_])
    P.release(m0)


def build_attn():
    C = Ctx()
    qT = C.inp("qT", [NQH, 96, TT], BF16)
    kTf = C.inp("kTf", [NKH, 96, NK], BF16)
    vPf = C.inp("vPf", [NKH, NK, 65], BF16)
    hT_d = C.inp("hT", [D, TT], BF16)
    xT = C.inp("xT", [D, TT])
    yh = C.inp("yh", [512, TL + 30])
    yc = C.inp("yc", [512, CTX + 30])
    modv = C.inp("modv", [128, 6, 8, 2])
    w_in = C.inp("w_in", [D, IN_COLS])
    bgc = C.inp("bgc", [128, 24])
    cwc = C.inp("cwc", [128, 4, 31])
    cvec = C.inp("cvec", [128, 4, 3])
    w_conv_out = C.inp("w_conv_out", [512, D])
    w_mla_o = C.inp("w_mla_o", [512, D])
    w_gqa_o = C.inp("w_gqa_o", [512, D])
    w_out = C.inp("w_out", [D, D])
    xm_o = C.out("xmT", [D, TT])
    h2_o = C.out("h2T", [D, TT], BF16)
    oT_d = C.scratch("oT_d", [NQH * 64, TT], BF16)
    P = C.start()
    eps_t = C.eps_t
    banks = C.banks
    ones_f = P.sb("ones_f", [64], F32)
    P.op("pool", lambda e: e.memset(ones_f.ap, 1.0), w=[ones_f])
    lat_blocks = [(0, CTX)] + [(CTX + b * KB, KB) for b in range(SEQ // KB)]
    heads_l, heads_c = [], []
    for h in range(NQH):
        kvh = h if h < 8 else 8 + (h - 8) // 4
        sc = MLA_SCALE if h < 8 else GQA_SCALE
        for qh in range(TL // 1024):
            heads_l.append(dict(q=qT[h, :, qh * 1024:(qh + 1) * 1024], k=kTf[kvh], v=vPf[kvh],
                                o=oT_d[h * 64:(h + 1) * 64, qh * 1024:(qh + 1) * 1024], scale=sc))
        heads_c.append(dict(q=qT[h, :, TL:TT], k=kTf[kvh], v=vPf[kvh], o=oT_d[h * 64:(h + 1) * 64, TL:TT], scale=sc))
    attention_phase(C, heads_c, CTX, [(0, CTX)], ones_f)
    attention_phase(C, heads_l, 1024, lat_blocks, ones_f)
    P.barrier()
    C.cv_engs = ("pool", "act", "dve")
    C.nst = 2
    NB = 256
    ones_bf = P.sb("ones_bf", [128], BF16)
    P.op("pool", lambda e: e.memset(ones_bf.ap, 1.0), w=[ones_bf])
    ones512 = P.sb("ones512", [128], BF16)
    P.op("pool", lambda e: e.memset(ones512.ap, 1.0 / 512.0), w=[ones512])
    modc = C.load_f32("modc", modv, [6, 8, 2])
    G1 = T(modc.ap[:, 2], "G1")
    A2 = T(modc.ap[:, 3], "A2")
    B2 = T(modc.ap[:, 4], "B2")
    bg_t = C.load_f32("bg_t", bgc, [24])
    cw_t = C.load_f32("cw_t", cwc, [4, 31])
    cv_t = C.load_f32("cv_t", cvec, [4, 3])
    wiv = wview(w_in)
    w_g = C.load_bf16("w_g", wiv[:, :, C_GATE:IN_COLS], 8, 3072)
    w_co = C.load_bf16("w_co", wview(w_conv_out), 4, D)
    w_mo = C.load_bf16("w_mo", wview(w_mla_o), 4, D)
    w_go = C.load_bf16("w_go", wview(w_gqa_o), 4, D)
    w_o = C.load_bf16("w_o", wview(w_out), 8, D)
    xs = P.sb("xs", [8, NB], F32)
    hT = P.sb("hT", [8, NB], BF16)
    om = P.sb("om", [4, NB], BF16)
    og = P.sb("og", [4, NB], BF16)
    FB = []
    for i in range(2):
        FB.append(dict(ytile=P.sb(f"ytile{i}", [4, NB + 30], F32), scrF=P.sb(f"scrF{i}", [8, NB], F32), scrB=P.sb(f"scrB{i}", [8, NB], BF16),
                       mean_s=P.sb(f"mean_s{i}", [NB], F32), var_s=P.sb(f"var_s{i}", [NB], F32), rstd_c=P.sb(f"rstd_c{i}", [NB], F32),
                       ys=P.sb(f"ys{i}", [4, NB], BF16)))
    gate = [P.sb(f"gate{i}", [NB], F32) for i in range(3)]
    mtmp = [P.sb(f"mtmp{i}", [NB], F32) for i in range(2)]
    macc = P.sb("macc", [NB], F32)
    merged = P.sb("merged", [8, NB], BF16)
    xm = P.sb("xm", [8, NB], F32)
    h2 = P.sb("h2", [8, NB], BF16)
    rstd = P.sb("rstd", [NB], F32)
    sqb2 = P.sb("sqb2", [8, NB], BF16)
    ntmp = [P.sb(f"ntmp{i}", [NB], F32) for i in range(3)]
    oT3 = oT_d.rearrange("(b j p) t -> b p j t", b=2, p=128)
    MBLOCKS = [(i * NB, NB, 0) for i in range(TL // NB)] + [(TL, CTX, 1)]

    def front_pieces(bi):
        t0, n, grp = MBLOCKS[bi]
        fb = FB[bi % 2]
        ytile, scrF, scrB, mean_s, var_s, rstd_c, ys = fb["ytile"], fb["scrF"], fb["scrB"], fb["mean_s"], fb["var_s"], fb["rstd_c"], fb["ys"]
        pieces = []

        def load():
            ysrc = yh if grp == 0 else yc
            y0 = t0 if grp == 0 else 0
            P.dma("sp", ytile.ap[:, :, 0:n + 30], wview(ysrc)[:, :, y0:y0 + n + 30], w=[ytile])

        def conv_part(ch, k0, k1):
            def f():
                if ch == 0 and k0 == 0:
                    load()
                for k in range(k0, k1):
                    if k == 0:
                        P.op("dve", lambda e: e.tensor_scalar(out=scrF.ap[:, ch, 0:n], in0=ytile.ap[:, ch, 0:n], scalar1=cw_t.ap[:, ch, 0:1],
                                                              scalar2=cv_t.ap[:, ch, 0:1], op0=ALU.mult, op1=ALU.add),
                             r=[ytile, cw_t, cv_t], w=[scrF])
                    else:
                        P.op("dve", lambda e, k=k: e.scalar_tensor_tensor(out=scrF.ap[:, ch, 0:n], in0=ytile.ap[:, ch, k:k + n],
                                                                          scalar=cw_t.ap[:, ch, k:k + 1], in1=scrF.ap[:, ch, 0:n],
                                                                          op0=ALU.mult, op1=ALU.add), r=[ytile, cw_t, scrF], w=[scrF])
            return f
        for ch in range(4):
            pieces.append(conv_part(ch, 0, 16))
            pieces.append(conv_part(ch, 16, 31))

        def ln():
            P.op("act", lambda e: e.activation(out=scrB.ap[:, 0:4, 0:n], in_=scrF.ap[:, 0:4, 0:n], func=AF.Copy), r=[scrF], w=[scrB])
            P.op("act", lambda e: e.activation(out=scrB.ap[:, 4:8, 0:n], in_=scrF.ap[:, 0:4, 0:n], func=AF.Square, scale=512.0 ** -0.5),
                 r=[scrF], w=[scrB])
            bm, bv = banks[0], banks[1]
            for ch in range(4):
                P.op("pe", lambda e, ch=ch: e.matmul(bm.ap[:, 0:n], lhsT=ones512.ap[:, 0:128], rhs=scrB.ap[:, ch, 0:n], start=(ch == 0), stop=(ch == 3)),
                     r=[scrB, ones512], w=[bm])
            for ch in range(4):
                P.op("pe", lambda e, ch=ch: e.matmul(bv.ap[:, 0:n], lhsT=ones_bf.ap[:, 0:128], rhs=scrB.ap[:, 4 + ch, 0:n], start=(ch == 0), stop=(ch == 3)),
                     r=[scrB, ones_bf], w=[bv])
            P.op("act", lambda e: e.activation(out=mean_s.ap[:, 0:n], in_=bm.ap[:, 0:n], func=AF.Copy), r=[bm], w=[mean_s])
            P.op("dve", lambda e: e.tensor_tensor(out=var_s.ap[:, 0:n], in0=mean_s.ap[:, 0:n], in1=mean_s.ap[:, 0:n], op=ALU.mult), r=[mean_s], w=[var_s])
            P.op("dve", lambda e: e.tensor_tensor(out=var_s.ap[:, 0:n], in0=bv.ap[:, 0:n], in1=var_s.ap[:, 0:n], op=ALU.subtract), r=[bv, var_s], w=[var_s])
            P.op("act", lambda e: e.activation(out=rstd_c.ap[:, 0:n], in_=var_s.ap[:, 0:n], func=AF.Ln, bias=eps_t.ap[:, 0:1]), r=[var_s, eps_t], w=[rstd_c])
            P.op("act", lambda e: e.activation(out=rstd_c.ap[:, 0:n], in_=rstd_c.ap[:, 0:n], func=AF.Exp, scale=-0.5), r=[rstd_c], w=[rstd_c])
            for ch in range(4):
                P.op("dve", lambda e, ch=ch: e.tensor_tensor(out=scrF.ap[:, 4 + ch, 0:n], in0=scrF.ap[:, ch, 0:n], in1=mean_s.ap[:, 0:n], op=ALU.subtract),
                     r=[scrF, mean_s], w=[scrF])
                P.op("dve", lambda e, ch=ch: e.tensor_tensor(out=scrF.ap[:, 4 + ch, 0:n], in0=scrF.ap[:, 4 + ch, 0:n], in1=rstd_c.ap[:, 0:n], op=ALU.mult),
                     r=[scrF, rstd_c], w=[scrF])
                P.op("act", lambda e, ch=ch: e.activation(out=ys.ap[:, ch, 0:n], in_=scrF.ap[:, 4 + ch, 0:n], func=AF.Silu,
                                                          scale=cv_t.ap[:, ch, 1:2], bias=cv_t.ap[:, ch, 2:3]), r=[scrF, cv_t], w=[ys])
        pieces.append(ln)
        return pieces

    def back(bi, nxt):
        t0, n, grp = MBLOCKS[bi]
        ys = FB[bi % 2]["ys"]
        P.dma("sp", xs.ap[:, :, 0:n], wview(xT)[:, :, t0:t0 + n], w=[xs])
        P.dma("pool", hT.ap[:, :, 0:n], wview(hT_d)[:, :, t0:t0 + n], w=[hT])
        P.dma("pool", om.ap[:, :, 0:n], oT3[0][:, :, t0:t0 + n], w=[om])
        P.dma("pool", og.ap[:, :, 0:n], oT3[1][:, :, t0:t0 + n], w=[og])

        def merge_oc(oc):
            bc_, bm_, bg_ = banks[2], banks[3], banks[4]
            gb = [banks[5], banks[6], banks[7]]
            for (bk_, w_, src_) in ((bc_, w_co, ys), (bm_, w_mo, om), (bg_, w_go, og)):
                for ch in range(4):
                    P.op("pe", lambda e, ch=ch, bk_=bk_, w_=w_, src_=src_: e.matmul(bk_.ap[:, 0:n], lhsT=w_.ap[:, ch, oc * 128:(oc + 1) * 128],
                                                                                rhs=src_.ap[:, ch, 0:n], start=(ch == 0), stop=(ch == 3)),
                         r=[w_, src_], w=[bk_])
            for b in range(3):
                c0 = b * 1024 + oc * 128
                for kc in range(8):
                    P.op("pe", lambda e, kc=kc, b=b, c0=c0: e.matmul(gb[b].ap[:, 0:n], lhsT=w_g.ap[:, kc, c0:c0 + 128], rhs=hT.ap[:, kc, 0:n],
                                                                 start=(kc == 0), stop=(kc == 7)), r=[w_g, hT], w=[gb[b]])
                P.op("act", lambda e, b=b: e.activation(out=gate[b].ap[:, 0:n], in_=gb[b].ap[:, 0:n], func=AF.Sigmoid,
                                                        bias=bg_t.ap[:, b * 8 + oc:b * 8 + oc + 1]), r=[gb[b], bg_t], w=[gate[b]])
            P.op("dve", lambda e: e.tensor_tensor(out=macc.ap[:, 0:n], in0=bc_.ap[:, 0:n], in1=gate[0].ap[:, 0:n], op=ALU.mult),
                 r=[bc_, gate[0]], w=[macc])
            P.op("dve", lambda e: e.tensor_tensor(out=mtmp[0].ap[:, 0:n], in0=bm_.ap[:, 0:n], in1=gate[1].ap[:, 0:n], op=ALU.mult),
                 r=[bm_, gate[1]], w=[mtmp[0]])
            P.op("dve", lambda e: e.tensor_tensor(out=mtmp[1].ap[:, 0:n], in0=bg_.ap[:, 0:n], in1=gate[2].ap[:, 0:n], op=ALU.mult),
                 r=[bg_, gate[2]], w=[mtmp[1]])
            P.op("dve", lambda e: e.tensor_tensor(out=macc.ap[:, 0:n], in0=macc.ap[:, 0:n], in1=mtmp[0].ap[:, 0:n], op=ALU.add),
                 r=[macc, mtmp[0]], w=[macc])
            P.op("dve", lambda e: e.tensor_tensor(out=merged.ap[:, oc, 0:n], in0=macc.ap[:, 0:n], in1=mtmp[1].ap[:, 0:n], op=ALU.add),
                 r=[macc, mtmp[1]], w=[merged])

        for oc_ in range(8):
            merge_oc(oc_)
            if nxt:
                nxt.pop(0)()
        while nxt:
            nxt.pop(0)()

        def outproj_oc(oc):
            bk = banks[oc % 2]
            for kc in range(8):
                P.op("pe", lambda e, kc=kc: e.matmul(bk.ap[:, 0:n], lhsT=w_o.ap[:, kc, oc * 128:(oc + 1) * 128], rhs=merged.ap[:, kc, 0:n],
                                                     start=(kc == 0), stop=(kc == 7)), r=[w_o, merged], w=[bk])
            P.op("dve", lambda e: e.scalar_tensor_tensor(out=xm.ap[:, oc, 0:n], in0=bk.ap[:, 0:n], scalar=G1.ap[:, oc, grp:grp + 1],
                                                         in1=xs.ap[:, oc, 0:n], op0=ALU.mult, op1=ALU.add),
                 r=[bk, G1, xs], w=[xm])

        for oc_ in range(8):
            outproj_oc(oc_)
        P.dma(None, wview(xm_o)[:, :, t0:t0 + n], xm.ap[:, :, 0:n], r=[xm])
        emit_norm_mod(C, xm, n, ones_bf, A2, B2, grp, h2, ntmp, sqb2, banks[2], rstd)
        P.dma(None, wview(h2_o)[:, :, t0:t0 + n], h2.ap[:, :, 0:n], r=[h2])

    for f_ in front_pieces(0):
        f_()
    for bi in range(len(MBLOCKS)):
        nxt = front_pieces(bi + 1) if bi + 1 < len(MBLOCKS) else []
        back(bi, nxt)
    return C.finish(), C


def build_ffn():
    C = Ctx()
    xm_d = C.inp("xmT", [D, TT])
    h2l = C.inp("h2l", [D, TL + 2], BF16)
    h2c = C.inp("h2c", [D, CTX + 2], BF16)
    modv = C.inp("modv", [128, 6, 8, 2])
    w_up = C.inp("w_up", [D, 2 * D_FF])
    w_down = C.inp("w_down", [D_FF, D])
    fwc = C.inp("fwc", [128, 44, 4])
    x_o = C.out("xT_out", [D, TT])
    P = C.start()
    C.cv_engs = ("pool", "act", "dve")
    C.nst = 2
    banks = C.banks
    modc = C.load_f32("modc", modv, [6, 8, 2])
    G2 = T(modc.ap[:, 5], "G2")
    fw_t = C.load_f32("fw_t", fwc, [44, 4])
    wu = C.load_bf16("wu", wview(w_up), 8, 2 * D_FF)
    wd = C.load_bf16("wd", wview(w_down), 22, D)
    h2 = P.sb("h2", [8, 514], BF16)
    xmc = [P.sb(f"xmc{i}", [512], F32) for i in range(2)]
    act_t = P.sb("act_t", [22, 512], BF16)
    ta = [P.sb(f"ta{i}", [512], F32) for i in range(2)]
    tg = [P.sb(f"tg{i}", [512], F32) for i in range(2)]
    sgt = [P.sb(f"sgt{i}", [512], F32) for i in range(2)]
    cnt = [0]

    def conv_chunk(cc, n, dst):
        i = cnt[0]
        cnt[0] += 1
        bk = banks[2 + 2 * (i % 3)]
        bh = banks[3 + 2 * (i % 3)]
        for kc in range(8):
            P.op("pe", lambda e, kc=kc: e.matmul(bk.ap[:, 0:n], lhsT=wu.ap[:, kc, cc * 128:(cc + 1) * 128], rhs=h2.ap[:, kc, 1:n + 1],
                                                 start=(kc == 0), stop=(kc == 7)), r=[wu, h2], w=[bk])
        for kc in range(8):
            P.op("pe", lambda e, kc=kc: e.matmul(bh.ap[:, 0:2], lhsT=wu.ap[:, kc, cc * 128:(cc + 1) * 128], rhs=h2.ap[:, kc, 0:n + 2:n + 1],
                                                 start=(kc == 0), stop=(kc == 7)), r=[wu, h2], w=[bh])
        P.op("act", lambda e: e.activation(out=dst.ap[:, 0:n], in_=bk.ap[:, 0:n], func=AF.Identity, scale=fw_t.ap[:, cc, 1:2],
                                           bias=fw_t.ap[:, cc, 3:4]), r=[bk, fw_t], w=[dst])
        P.op("dve", lambda e: e.scalar_tensor_tensor(out=dst.ap[:, 1:n], in0=bk.ap[:, 0:n - 1], scalar=fw_t.ap[:, cc, 0:1], in1=dst.ap[:, 1:n],
                                                     op0=ALU.mult, op1=ALU.add), r=[bk, fw_t, dst], w=[dst])
        P.op("dve", lambda e: e.scalar_tensor_tensor(out=dst.ap[:, 0:n - 1], in0=bk.ap[:, 1:n], scalar=fw_t.ap[:, cc, 2:3], in1=dst.ap[:, 0:n - 1],
                                                     op0=ALU.mult, op1=ALU.add), r=[bk, fw_t, dst], w=[dst])
        P.op("dve", lambda e: e.scalar_tensor_tensor(out=dst.ap[:, 0:1], in0=bh.ap[:, 0:1], scalar=fw_t.ap[:, cc, 0:1], in1=dst.ap[:, 0:1],
                                                     op0=ALU.mult, op1=ALU.add), r=[bh, fw_t, dst], w=[dst])
        P.op("dve", lambda e: e.scalar_tensor_tensor(out=dst.ap[:, n - 1:n], in0=bh.ap[:, 1:2], scalar=fw_t.ap[:, cc, 2:3], in1=dst.ap[:, n - 1:n],
                                                     op0=ALU.mult, op1=ALU.add), r=[bh, fw_t, dst], w=[dst])

    def do_block(t0, n, grp):
        hsrc = h2l if grp == 0 else h2c
        h0 = t0 if grp == 0 else 0
        P.dma("sp", h2.ap[:, :, 0:n + 2], wview(hsrc)[:, :, h0:h0 + n + 2], w=[h2])
        for j in range(22):
            a_, g_, s_ = ta[j % 2], tg[j % 2], sgt[j % 2]
            conv_chunk(22 + j, n, g_)
            conv_chunk(j, n, a_)
            P.op("act", lambda e, g_=g_, s_=s_: e.activation(out=s_.ap[:, 0:n], in_=g_.ap[:, 0:n], func=AF.Silu), r=[g_], w=[s_])
            P.op("pool", lambda e, j=j, a_=a_, s_=s_: e.tensor_tensor(out=act_t.ap[:, j, 0:n], in0=a_.ap[:, 0:n], in1=s_.ap[:, 0:n], op=ALU.mult),
                 r=[a_, s_], w=[act_t])

        def down_oc(oc):
            bk = banks[oc % 2]
            xc = xmc[oc % 2]
            P.dma("pool", xc.ap[:, 0:n], xm_d[oc * 128:(oc + 1) * 128, t0:t0 + n], w=[xc])
            for j in range(22):
                P.op("pe", lambda e, j=j: e.matmul(bk.ap[:, 0:n], lhsT=wd.ap[:, j, oc * 128:(oc + 1) * 128], rhs=act_t.ap[:, j, 0:n],
                                                   start=(j == 0), stop=(j == 21)), r=[wd, act_t], w=[bk])
            P.op("dve", lambda e: e.scalar_tensor_tensor(out=xc.ap[:, 0:n], in0=bk.ap[:, 0:n], scalar=G2.ap[:, oc, grp:grp + 1],
                                                         in1=xc.ap[:, 0:n], op0=ALU.mult, op1=ALU.add),
                 r=[bk, G2, xc], w=[xc])
            P.dma("sp", x_o[oc * 128:(oc + 1) * 128, t0:t0 + n], xc.ap[:, 0:n], r=[xc])

        for oc_ in range(8):
            down_oc(oc_)

    for (t0_, n_, grp_) in BLOCKS:
        do_block(t0_, n_, grp_)
    return C.finish(), C


def _cols(v, n=128):
    v = np.asarray(v, np.float32)
    return np.ascontiguousarray(v.reshape(-1, n).T)


def _rope_tables():
    tabs = []
    tok = np.arange(SEQ)
    row = (tok // 64).astype(np.float32)
    col = (tok % 64).astype(np.float32)

    def mk(rot_dim, off):
        nf = rot_dim // 4
        inv = (1.0 / (10000.0 ** (np.arange(nf, dtype=np.float32) / nf))).astype(np.float32)
        ar = row[:, None] * inv
        ac = col[:, None] * inv
        ang = np.concatenate([ar, ar, ac, ac], axis=-1)
        cos = np.ones((96, SEQ), np.float32)
        sin = np.zeros((96, SEQ), np.float32)
        cos[off:off + rot_dim] = np.cos(ang).T
        sin[off:off + rot_dim] = np.sin(ang).T
        return cos, sin

    cm, sm = mk(32, 64)
    cg, sg = mk(64, 0)
    for c in range(NCORES):
        sl = slice(c * TL, (c + 1) * TL)
        M = np.zeros((96, 2, TT), np.float32)
        G = np.zeros((96, 2, TT), np.float32)
        M[:, 0, :TL] = cm[:, sl]; M[:, 1, :TL] = sm[:, sl]; M[:, 0, TL:] = 1.0
        G[:, 0, :TL] = cg[:, sl]; G[:, 1, :TL] = sg[:, sl]; G[:, 0, TL:] = 1.0
        tabs.append((M, G))
    rot = np.zeros((96, 2, 96), np.float32)
    for i in range(8):
        rot[72 + i, 0, 64 + i] = -1.0; rot[64 + i, 0, 72 + i] = 1.0
        rot[88 + i, 0, 80 + i] = -1.0; rot[80 + i, 0, 88 + i] = 1.0
    for i in range(16):
        rot[16 + i, 1, i] = -1.0; rot[i, 1, 16 + i] = 1.0
        rot[48 + i, 1, 32 + i] = -1.0; rot[32 + i, 1, 48 + i] = 1.0
    shift = np.zeros((32, 96), np.float32)
    for i in range(32):
        shift[i, 64 + i] = 1.0
    return tabs, rot.astype(NPBF), shift.astype(NPBF)


_PROGS = {}


def _prog(name):
    if name not in _PROGS:
        _PROGS[name] = {"pre": build_pre, "attn": build_attn, "ffn": build_ffn}[name]()[0]
    return _PROGS[name]


def _run(name, in_maps):
    res = run_bass_kernel_spmd(_prog(name), in_maps, core_ids=list(range(NCORES)))
    return res.results


def kernel(**inp):
    f32 = np.float32
    x = np.asarray(inp["x"], f32)[0]
    ctx = np.asarray(inp["ctx"], f32)[0]
    tabs, rot, shift = _rope_tables()
    c2 = np.ascontiguousarray(np.stack([_cols(inp["c"][0]), _cols(inp["c_ctx"])], axis=-1))
    xT = [np.ascontiguousarray(np.concatenate([x[c * TL:(c + 1) * TL], ctx], axis=0).T) for c in range(NCORES)]
    for l in range(DEPTH):
        hg = np.zeros((96, 4), f32)
        hg[:, 0] = inp["g_mla_q"][l]; hg[:, 1] = inp["g_mla_k"][l]
        hg[:64, 2] = inp["g_gqa_q"][l]; hg[:64, 3] = inp["g_gqa_k"][l]
        w_in_l = np.ascontiguousarray(inp["w_in"][l], dtype=f32)
        pre_in = []
        for c in range(NCORES):
            pre_in.append(dict(
                xT=xT[c], c2=c2, w_mod=np.ascontiguousarray(inp["w_mod"][l], dtype=f32), b_modc=_cols(inp["b_mod"][l]),
                gn1c=_cols(inp["g_norm1"][l]), gn2c=_cols(inp["g_norm2"][l]), w_in=w_in_l,
                gqac=_cols(inp["g_q_a"][l]), gkvac=_cols(inp["g_kv_a"][l]),
                w_q_b=np.ascontiguousarray(inp["w_q_b"][l], dtype=f32), w_kv_b=np.ascontiguousarray(inp["w_kv_b"][l], dtype=f32),
                hgain=hg, ropeM=tabs[c][0], ropeG=tabs[c][1], rotm=rot, shiftm=shift))
        r1 = _run("pre", pre_in)
        kTf = np.concatenate([r1[0]["kT"][:, :, TL:]] + [r1[c]["kT"][:, :, :TL] for c in range(NCORES)], axis=2)
        vPf = np.concatenate([r1[0]["vP"][TL:]] + [r1[c]["vP"][:TL] for c in range(NCORES)], axis=0)
        vPf = np.ascontiguousarray(vPf.transpose(1, 0, 2))
        kTf = np.ascontiguousarray(kTf)
        yfull = np.concatenate([r1[c]["y"][:, :TL] for c in range(NCORES)], axis=1)
        ypad = np.concatenate([np.zeros((512, 15), f32), yfull, np.zeros((512, 15), f32)], axis=1)
        cvec = np.ascontiguousarray(np.stack([_cols(inp["conv_dw_b"][l]), _cols(inp["conv_ln_g"][l]), _cols(inp["conv_ln_b"][l])], axis=-1))
        cwc = np.ascontiguousarray(np.asarray(inp["conv_dw_w"][l], f32).T.reshape(4, 128, 31).transpose(1, 0, 2))
        at_in = []
        for c in range(NCORES):
            yc = np.concatenate([np.zeros((512, 15), f32), r1[c]["y"][:, TL:], np.zeros((512, 15), f32)], axis=1)
            at_in.append(dict(
                qT=r1[c]["qT"], kTf=kTf, vPf=vPf, hT=r1[c]["hT"], xT=xT[c],
                yh=np.ascontiguousarray(ypad[:, c * TL:c * TL + TL + 30]), yc=np.ascontiguousarray(yc),
                modv=r1[c]["modv"], w_in=w_in_l, bgc=_cols(inp["b_gate"][l]), cwc=cwc, cvec=cvec,
                w_conv_out=np.ascontiguousarray(inp["w_conv_out"][l], dtype=f32), w_mla_o=np.ascontiguousarray(inp["w_mla_o"][l], dtype=f32),
                w_gqa_o=np.ascontiguousarray(inp["w_gqa_o"][l], dtype=f32), w_out=np.ascontiguousarray(inp["w_out"][l], dtype=f32)))
        r2 = _run("attn", at_in)
        h2full = np.concatenate([r2[c]["h2T"][:, :TL] for c in range(NCORES)], axis=1)
        zc = np.zeros((D, 1), NPBF)
        h2pad = np.concatenate([zc, h2full, zc], axis=1)
        fw = np.asarray(inp["ffn_dw_w"][l], f32)
        fwc = np.ascontiguousarray(np.stack([_cols(fw[0]), _cols(fw[1]), _cols(fw[2]), _cols(inp["ffn_dw_b"][l])], axis=-1))
        ff_in = []
        for c in range(NCORES):
            h2c = np.concatenate([zc, r2[c]["h2T"][:, TL:], zc], axis=1)
            ff_in.append(dict(
                xmT=r2[c]["xmT"], h2l=np.ascontiguousarray(h2pad[:, c * TL:c * TL + TL + 2]), h2c=np.ascontiguousarray(h2c),
                modv=r1[c]["modv"], w_up=np.ascontiguousarray(inp["w_up"][l], dtype=f32),
                w_down=np.ascontiguousarray(inp["w_down"][l], dtype=f32), fwc=fwc))
        r3 = _run("ffn", ff_in)
        xT = [r3[c]["xT_out"] for c in range(NCORES)]
    out = np.concatenate([xT[c][:, :TL].T for c in range(NCORES)], axis=0)[None]
    return np.ascontiguousarray(out.astype(np.float32))
```

```python
import numpy as np
import ml_dtypes
from contextlib import ExitStack
import concourse.bass as bass
import concourse.mybir as mybir
from concourse.bass_utils import run_bass_kernel_spmd

F32 = mybir.dt.float32
BF16 = mybir.dt.bfloat16
U8 = mybir.dt.uint8
AF = mybir.ActivationFunctionType
ALU = mybir.AluOpType
NPBF = ml_dtypes.bfloat16

NCORES = 8
D = 1024
SEQ = 16384
TL = SEQ // NCORES
CTX = 256
TT = TL + CTX
NK = CTX + SEQ
DEPTH = 4
EPS = 1e-6
BLOCKS = [(0, 512, 0), (512, 512, 0), (1024, 512, 0), (1536, 512, 0), (2048, 256, 1)]
NQH = 16
NKH = 10
MLA_SCALE = 96 ** -0.5
GQA_SCALE = 64 ** -0.5
C_GLU, C_QA, C_KVA, C_KR, C_GQ, C_GK, C_GV, C_GATE = 0, 1024, 1408, 1664, 1696, 2208, 2336, 2464
IN_COLS = 5536
D_FF = 2816

ENGS = ("pe", "act", "dve", "pool", "sp")
SEM_LIM = 30000
DMA_K = 12


class T:
    __slots__ = ("ap", "name")

    def __init__(self, ap, name):
        self.ap = ap
        self.name = name


class Prog:
    def __init__(self, nc, arena, arena_bytes):
        self.nc = nc
        self.ops = []
        self.last_w = {}
        self.readers = {}
        self.arena_bytes = arena_bytes
        self.arena = arena
        self.off = 0
        self.ndma = {e: 0 for e in ENGS}
        self.last_op = {e: None for e in ENGS}
        self.peak = 0
        self.dmaq = 0

    def sb(self, name, shape, dtype, parts=128):
        esz = 4 if dtype == F32 else 2
        n = int(np.prod(shape))
        nbytes = n * esz
        start = (self.off + 63) // 64 * 64
        assert start + nbytes <= self.arena_bytes, f"SBUF arena overflow {name} {start + nbytes}"
        self.off = start + nbytes
        self.peak = max(self.peak, self.off)
        ap = self.arena[0:parts, start:start + nbytes].bitcast(dtype)
        if len(shape) == 2:
            ap = ap.rearrange("p (a b) -> p a b", b=shape[1])
        elif len(shape) == 3:
            ap = ap.rearrange("p (a b c) -> p a b c", b=shape[1], c=shape[2])
        return T(ap, name)

    def mark(self):
        return self.off

    def release(self, mark):
        self.off = mark

    def _deps(self, idx, r, w):
        deps = set()
        for k in r:
            lw = self.last_w.get(k)
            if lw is not None:
                deps.add(lw)
        for k in w:
            lw = self.last_w.get(k)
            if lw is not None:
                deps.add(lw)
            for rd in self.readers.get(k, ()):
                deps.add(rd)
        for k in r:
            self.readers.setdefault(k, []).append(idx)
        for k in w:
            self.last_w[k] = idx
            self.readers[k] = []
        deps.discard(idx)
        return deps

    def op(self, eng, fn, r=(), w=()):
        idx = len(self.ops)
        deps = self._deps(idx, r, w)
        self.ops.append(dict(eng=eng, fn=fn, deps=deps, dma=False))
        self.last_op[eng] = idx
        return idx

    def dma(self, q, out, in_, r=(), w=()):
        if q is None:
            q = ("sp", "pool")[self.dmaq % 2]
            self.dmaq += 1
        idx = len(self.ops)
        deps = self._deps(idx, r, w)
        j = self.ndma[q]
        self.ndma[q] += 1
        self.ops.append(dict(eng=q, fn=lambda e: e.dma_start(out=out, in_=in_), deps=deps, dma=True, j=j))
        self.last_op[q] = idx
        return idx

    def barrier(self):
        lasts = {e: v for e, v in self.last_op.items() if v is not None}
        dmas = []
        for q in ENGS:
            cnt = 0
            for i in range(len(self.ops) - 1, -1, -1):
                o = self.ops[i]
                if o["dma"] and o["eng"] == q:
                    dmas.append(i)
                    cnt += 1
                    if cnt >= DMA_K:
                        break
        for e in ENGS:
            idx = len(self.ops)
            deps = set(v for ee, v in lasts.items() if ee != e and not self.ops[v]["dma"]) | set(dmas)
            self.ops.append(dict(eng=e, fn=None, deps=deps, dma=False))
            self.last_op[e] = idx
        self.last_w.clear()
        self.readers.clear()

    def emit(self, block, sems):
        ops = self.ops
        signaled = set()
        for o in ops:
            for d in o["deps"]:
                if not ops[d]["dma"]:
                    if ops[d]["eng"] == "pe" and o["eng"] == "pe":
                        continue
                    signaled.add(d)
        cnt = {e: 0 for e in ENGS}
        for i, o in enumerate(ops):
            if i in signaled:
                o["sig"] = cnt[o["eng"]]
                cnt[o["eng"]] += 1
        sem_iter = iter(sems)
        eng_sems = {}
        for e in ENGS:
            n_ep = cnt[e] // SEM_LIM + 1
            eng_sems[e] = [next(sem_iter) for _ in range(n_ep)]
        dma_sems = {}
        for q in ENGS:
            if self.ndma[q]:
                dma_sems[q] = [next(sem_iter) for _ in range(DMA_K)]
        per_eng = {e: [] for e in ENGS}
        for i, o in enumerate(ops):
            per_eng[o["eng"]].append(i)

        def run(e_name, e):
            waited = {x: -1 for x in ENGS}
            dwaited = set()
            for i in per_eng[e_name]:
                o = ops[i]
                need = {}
                for d in sorted(o["deps"]):
                    po = ops[d]
                    if po["dma"]:
                        if d not in dwaited:
                            dwaited.add(d)
                            q = po["eng"]
                            j = po["j"]
                            e.wait_ge(dma_sems[q][j % DMA_K], 16 * (j // DMA_K + 1))
                    else:
                        if po["eng"] == "pe" and e_name == "pe":
                            continue
                        s = po["sig"]
                        if s > waited[po["eng"]]:
                            need[po["eng"]] = max(need.get(po["eng"], -1), s)
                for pe_, s in need.items():
                    waited[pe_] = s
                    e.wait_ge(eng_sems[pe_][s // SEM_LIM], s % SEM_LIM + 1)
                if o["dma"]:
                    j = o["j"]
                    if j >= DMA_K:
                        e.wait_ge(dma_sems[e_name][j % DMA_K], 16 * (j // DMA_K))
                    ins = o["fn"](e)
                    ins.then_inc(dma_sems[e_name][j % DMA_K], 16)
                else:
                    if o["fn"] is None:
                        ins = e.nop() if "sig" in o else None
                    else:
                        ins = o["fn"](e)
                    if "sig" in o:
                        s = o["sig"]
                        ins.then_inc(eng_sems[e_name][s // SEM_LIM], 1)

        @block.tensor
        def _(e):
            run("pe", e)

        @block.scalar
        def _(e):
            run("act", e)

        @block.vector
        def _(e):
            run("dve", e)

        @block.gpsimd
        def _(e):
            run("pool", e)

        @block.sync
        def _(e):
            run("sp", e)


class Ctx:
    def __init__(self):
        self.nc = bass.Bass("TRN2", target_bir_lowering=False)
        self.es = ExitStack()
        self.ins = {}
        self.outs = {}

    def inp(self, name, shape, dt=F32):
        t = self.nc.dram_tensor(name, list(shape), dt, kind="ExternalInput").ap()
        self.ins[name] = t
        return t

    def out(self, name, shape, dt=F32):
        t = self.nc.dram_tensor(name, list(shape), dt, kind="ExternalOutput").ap()
        self.outs[name] = t
        return t

    def scratch(self, name, shape, dt):
        return self.nc.dram_tensor(name, list(shape), dt).ap()

    def start(self, arena_kb=204):
        nc, es = self.nc, self.es
        arena = es.enter_context(nc.sbuf_tensor("arena", [128, arena_kb * 1024], U8))
        psum = es.enter_context(nc.psum_tensor("psum", [128, 4096], F32))
        self.sems = [es.enter_context(nc.semaphore(f"s{i}")) for i in range(70)]
        self.block = es.enter_context(nc.Block())
        self.P = Prog(nc, arena[:], arena_kb * 1024)
        self.psum = psum[:]
        self.banks = [T(psum[:, i * 512:(i + 1) * 512], f"bank{i}") for i in range(8)]
        self.stage = None
        self.nstage = 0
        self.eps_t = self.P.sb("eps_t", [1], F32)
        self.P.op("pool", lambda e: e.memset(self.eps_t.ap, EPS), w=[self.eps_t])
        self.ncv = 0
        return self.P

    def finish(self):
        self.P.barrier()
        self.P.emit(self.block, self.sems)
        self.es.close()
        return self.nc

    def load_f32(self, name, src, shape, parts=128):
        t = self.P.sb(name, shape, F32)
        self.P.dma(None, t.ap[0:parts], src, w=[t])
        return t

    def load_bf16(self, name, src3, kc, ncols, parts=128, dst=None, dst_ap=None):
        P = self.P
        if self.stage is None:
            self.stage = [P.sb(f"stage{i}", [2048], F32) for i in range(self.nst)]
        if dst is None:
            dst = P.sb(name, [kc, ncols], BF16)
            dst_ap = dst.ap
        for k in range(kc):
            for c0 in range(0, ncols, 2048):
                cn = min(2048, ncols - c0)
                st = self.stage[self.nstage % self.nst]
                self.nstage += 1
                P.dma(None, st.ap[0:parts, 0:cn], src3[:, k, c0:c0 + cn], w=[st])
                eng = ("pool", "act", "dve")[self.ncv % 3] if self.cv_engs is None else self.cv_engs[self.ncv % len(self.cv_engs)]
                self.ncv += 1
                o = dst_ap[0:parts, k, c0:c0 + cn]
                i_ = st.ap[0:parts, 0:cn]
                if eng == "act":
                    P.op("act", lambda e, o=o, i_=i_: e.activation(out=o, in_=i_, func=AF.Copy), r=[st], w=[dst])
                elif eng == "pool":
                    P.op("pool", lambda e, o=o, i_=i_: e.tensor_copy(out=o, in_=i_), r=[st], w=[dst])
                else:
                    P.op("dve", lambda e, o=o, i_=i_: e.tensor_copy(out=o, in_=i_), r=[st], w=[dst])
        return dst

    cv_engs = None
    nst = 3


def wview(ap2d):
    return ap2d.rearrange("(kc p) n -> p kc n", p=128)


def emit_norm_mod(C, xs, n, ones_bf, Acol, Bcol, grp, hT, tmpA, sqbuf, bank_mean, rstd):
    P = C.P
    P.op("act", lambda e: e.activation(out=sqbuf.ap[:, :, 0:n], in_=xs.ap[:, :, 0:n], func=AF.Square, scale=1.0 / 32.0),
         r=[xs], w=[sqbuf])
    for kc in range(8):
        P.op("pe", lambda e, kc=kc: e.matmul(bank_mean.ap[:, 0:n], lhsT=ones_bf.ap[:, 0:128], rhs=sqbuf.ap[:, kc, 0:n],
                                             start=(kc == 0), stop=(kc == 7)), r=[sqbuf, ones_bf], w=[bank_mean])
    P.op("act", lambda e: e.activation(out=rstd.ap[:, 0:n], in_=bank_mean.ap[:, 0:n], func=AF.Ln, bias=C.eps_t.ap[:, 0:1]), r=[bank_mean, C.eps_t], w=[rstd])
    P.op("act", lambda e: e.activation(out=rstd.ap[:, 0:n], in_=rstd.ap[:, 0:n], func=AF.Exp, scale=-0.5), r=[rstd], w=[rstd])
    for kc in range(8):
        tA = tmpA[kc % len(tmpA)]
        P.op("dve", lambda e, kc=kc, tA=tA: e.scalar_tensor_tensor(out=tA.ap[:, 0:n], in0=xs.ap[:, kc, 0:n],
                                                                   scalar=Acol.ap[:, kc, grp:grp + 1], in1=rstd.ap[:, 0:n],
                                                                   op0=ALU.mult, op1=ALU.mult), r=[xs, Acol, rstd], w=[tA])
        P.op("act", lambda e, kc=kc, tA=tA: e.activation(out=hT.ap[:, kc, 0:n], in_=tA.ap[:, 0:n], func=AF.Identity,
                                                         bias=Bcol.ap[:, kc, grp:grp + 1]), r=[tA, Bcol], w=[hT])


def build_pre():
    C = Ctx()
    xT = C.inp("xT", [D, TT])
    c2 = C.inp("c2", [128, 8, 2])
    w_mod = C.inp("w_mod", [D, 6 * D])
    b_modc = C.inp("b_modc", [128, 48])
    gn1c = C.inp("gn1c", [128, 8])
    gn2c = C.inp("gn2c", [128, 8])
    w_in = C.inp("w_in", [D, IN_COLS])
    gqac = C.inp("gqac", [128, 3])
    gkvac = C.inp("gkvac", [128, 2])
    w_q_b = C.inp("w_q_b", [384, 768])
    w_kv_b = C.inp("w_kv_b", [256, 1024])
    hgain = C.inp("hgain", [96, 4])
    ropeM = C.inp("ropeM", [96, 2, TT])
    ropeG = C.inp("ropeG", [96, 2, TT])
    rotm = C.inp("rotm", [96, 2, 96], BF16)
    shiftm = C.inp("shiftm", [32, 96], BF16)
    modv_o = C.out("modv", [128, 6, 8, 2])
    hT_o = C.out("hT", [D, TT], BF16)
    qT_o = C.out("qT", [NQH, 96, TT], BF16)
    kT_o = C.out("kT", [NKH, 96, TT], BF16)
    vP_o = C.out("vP", [TT, NKH, 65], BF16)
    y_o = C.out("y", [512, TT])
    P = C.start()
    eps_t = C.eps_t
    C.cv_engs = ("pool", "act")
    C.nst = 2
    banks = C.banks

    ones_bf = P.sb("ones_bf", [128], BF16)
    P.op("pool", lambda e: e.memset(ones_bf.ap, 1.0), w=[ones_bf])
    rot_t = P.sb("rot_t", [2, 96], BF16)
    P.dma(None, rot_t.ap[0:96], rotm, w=[rot_t])
    shift_t = P.sb("shift_t", [96], BF16)
    P.dma(None, shift_t.ap[0:32], shiftm, w=[shift_t])
    hgain_t = C.load_f32("hgain_t", hgain, [4], parts=96)
    gqa_t = C.load_f32("gqa_t", gqac, [3])
    gkva_t = C.load_f32("gkva_t", gkvac, [2])
    gn1_t = C.load_f32("gn1_t", gn1c, [8])
    gn2_t = C.load_f32("gn2_t", gn2c, [8])
    bmod_t = C.load_f32("bmod_t", b_modc, [48])

    c2_t = C.load_f32("c2_t", c2, [8, 2])
    sc_t = P.sb("sc_t", [8, 2], F32)
    P.op("act", lambda e: e.activation(out=sc_t.ap, in_=c2_t.ap, func=AF.Silu), r=[c2_t], w=[sc_t])
    modv = P.sb("modv", [6, 8, 2], F32)
    m0 = P.mark()
    wm = [P.sb(f"wm{i}", [8, 512], F32) for i in range(2)]
    wmv = wview(w_mod)
    for g in range(12):
        wt = wm[g % 2]
        P.dma(None, wt.ap, wmv[:, :, g * 512:(g + 1) * 512], w=[wt])
        for s in range(4):
            j = g * 4 + s
            bk = banks[j % 2]
            for kc in range(8):
                P.op("pe", lambda e, bk=bk, wt=wt, kc=kc, s=s: e.matmul(bk.ap[:, 0:2], lhsT=wt.ap[:, kc, s * 128:(s + 1) * 128],
                                                                     rhs=sc_t.ap[:, kc, :], start=(kc == 0), stop=(kc == 7)),
                     r=[wt, sc_t], w=[bk])
            P.op("dve", lambda e, bk=bk, j=j: e.tensor_scalar(out=modv.ap[:, j // 8, j % 8, :], in0=bk.ap[:, 0:2],
                                                            scalar1=bmod_t.ap[:, j:j + 1], scalar2=None, op0=ALU.add),
                 r=[bk, bmod_t], w=[modv])
    P.barrier()
    P.release(m0)
    modc = P.sb("modc", [6, 8, 2], F32)
    for (dst, src, gn) in ((0, 1, gn1_t), (3, 4, gn2_t)):
        for cnd in range(2):
            P.op("dve", lambda e, dst=dst, src=src, gn=gn, cnd=cnd: e.scalar_tensor_tensor(
                out=modc.ap[:, dst, :, cnd], in0=modv.ap[:, src, :, cnd], scalar=1.0, in1=gn.ap,
                op0=ALU.add, op1=ALU.mult), r=[modv, gn], w=[modc])
    for (dst, src) in ((1, 0), (2, 2), (4, 3), (5, 5)):
        P.op("dve", lambda e, dst=dst, src=src: e.tensor_copy(out=modc.ap[:, dst], in_=modv.ap[:, src]), r=[modv], w=[modc])
    P.dma(None, modv_o, modc.ap, r=[modc])
    import os
    STOP = int(os.environ.get('PRE_STOP', '9'))
    if STOP <= 1:
        return C.finish(), C
    Acol = T(modc.ap[:, 0], "A1")
    Bcol = T(modc.ap[:, 1], "B1")

    wiv = wview(w_in)
    w_a = C.load_bf16("w_a", wiv[:, :, 0:C_GATE], 8, C_GATE)
    wqb = C.load_bf16("wqb", wview(w_q_b), 3, 768)
    wkb = P.sb("wkb", [2, 8, 96], BF16)
    P.op("pool", lambda e: e.memset(wkb.ap, 0.0), w=[wkb])
    wkv = P.sb("wkv", [2, 8, 64], BF16)
    wkbv = wview(w_kv_b).rearrange("p kc (h t) -> p kc h t", t=128)
    stg = C.stage
    for kc in range(2):
        st = stg[C.nstage % C.nst]
        C.nstage += 1
        P.dma(None, st.ap[:, 0:1024], wview(w_kv_b)[:, kc, :], w=[st])
        sv = st.ap[:, 0:1024].rearrange("p (h t) -> p h t", t=128)
        P.op("pool", lambda e, kc=kc, sv=sv: e.tensor_copy(out=wkb.ap[:, kc, :, 0:64], in_=sv[:, :, 0:64]), r=[st], w=[wkb])
        P.op("pool", lambda e, kc=kc, sv=sv: e.tensor_copy(out=wkv.ap[:, kc, :, :], in_=sv[:, :, 64:128]), r=[st], w=[wkv])

    if STOP <= 2:
        return C.finish(), C
    xs = P.sb("xs", [8, 512], F32)
    sqb = P.sb("sqb", [8, 512], BF16)
    tmpA = [P.sb(f"tmpA{i}", [512], F32) for i in range(3)]
    hT = P.sb("hT", [8, 512], BF16)
    rstd = P.sb("rstd", [512], F32)
    sg = [P.sb(f"sg{i}", [512], F32) for i in range(2)]
    yt = [P.sb(f"yt{i}", [512], F32) for i in range(2)]
    zq = P.sb("zq", [3, 512], F32)
    sqa = P.sb("sqa", [3, 512], BF16)
    qan = P.sb("qan", [3, 512], BF16)
    zkv = P.sb("zkv", [2, 512], F32)
    sqkv = P.sb("sqkv", [2, 512], BF16)
    kvn = P.sb("kvn", [2, 512], BF16)
    zkr = P.sb("zkr", [512], BF16)
    rstd2 = P.sb("rstd2", [512], F32)
    ropeM_t = P.sb("ropeM_t", [2, 512], F32)
    ropeG_t = P.sb("ropeG_t", [2, 512], F32)
    vt = [P.sb(f"vt{i}", [NKH, 65], BF16) for i in range(2)]
    for v_ in vt:
        P.op("pool", lambda e, v_=v_: e.memset(v_.ap, 1.0), w=[v_])
    NU = 4
    u_sq = [P.sb(f"u_sq{i}", [512], BF16) for i in range(NU)]
    u_zc = [P.sb(f"u_zc{i}", [512], F32) for i in range(NU)]
    u_rs = [P.sb(f"u_rs{i}", [512], F32) for i in range(NU)]
    u_xn = [P.sb(f"u_xn{i}", [512], BF16) for i in range(NU)]
    u_t1 = [P.sb(f"u_t1{i}", [512], F32) for i in range(NU)]
    u_t2 = [P.sb(f"u_t2{i}", [512], F32) for i in range(NU)]
    u_o = [P.sb(f"u_o{i}", [512], BF16) for i in range(NU)]
    u_og = [P.sb(f"u_og{i}", [512], BF16) for i in range(NU)]
    for o_ in u_og:
        P.op("pool", lambda e, o_=o_: e.memset(o_.ap, 0.0), w=[o_])
    ucnt = [0]

    def inproj(bank, col0, m, n):
        for kc in range(8):
            P.op("pe", lambda e, kc=kc: e.matmul(bank.ap[0:m, 0:n], lhsT=w_a.ap[:, kc, col0:col0 + m], rhs=hT.ap[:, kc, 0:n],
                                                 start=(kc == 0), stop=(kc == 7)), r=[w_a, hT], w=[bank])

    def normrope(main_fn, d, gcol, rope_t, ridx, out_dram, n):
        pending.append((main_fn, d, gcol, rope_t, ridx, out_dram, n))

    def flush_units():
        units = list(pending)
        del pending[:]
        st = []
        for (main_fn, d, gcol, rope_t, ridx, out_dram, n) in units:
            u = ucnt[0] % NU
            ucnt[0] += 1
            st.append(dict(main_fn=main_fn, d=d, gcol=gcol, rope_t=rope_t, ridx=ridx, out_dram=out_dram, n=n, bA=banks[2 * u], bB=banks[2 * u + 1],
                           sq=u_sq[u], zc=u_zc[u], rs=u_rs[u], xn=u_xn[u], t1=u_t1[u], t2=u_t2[u], o=(u_o[u] if d == 96 else u_og[u])))

        def stage_m(x):
            x["main_fn"](x["bA"])

        def stage_a(x):
            d, n, bA, bB, sq, zc = x["d"], x["n"], x["bA"], x["bB"], x["sq"], x["zc"]
            P.op("act", lambda e: e.activation(out=sq.ap[0:d, 0:n], in_=bA.ap[0:d, 0:n], func=AF.Square, scale=float(d) ** -0.5),
                 r=[bA], w=[sq])
            P.op("act", lambda e: e.activation(out=zc.ap[0:d, 0:n], in_=bA.ap[0:d, 0:n], func=AF.Copy), r=[bA], w=[zc])
            P.op("pe", lambda e: e.matmul(bB.ap[0:96, 0:n], lhsT=ones_bf.ap[0:d, 0:96], rhs=sq.ap[0:d, 0:n], start=True, stop=True),
                 r=[sq, ones_bf], w=[bB])

        def stage_b(x):
            d, n, bA, bB, zc, rs_, xn = x["d"], x["n"], x["bA"], x["bB"], x["zc"], x["rs"], x["xn"]
            gcol, ridx = x["gcol"], x["ridx"]
            P.op("act", lambda e: e.activation(out=rs_.ap[0:d, 0:n], in_=bB.ap[0:d, 0:n], func=AF.Ln, bias=eps_t.ap[0:d, 0:1]), r=[bB, eps_t], w=[rs_])
            P.op("act", lambda e: e.activation(out=rs_.ap[0:d, 0:n], in_=rs_.ap[0:d, 0:n], func=AF.Exp, scale=-0.5), r=[rs_], w=[rs_])
            P.op("dve", lambda e: e.scalar_tensor_tensor(out=xn.ap[0:d, 0:n], in0=zc.ap[0:d, 0:n], scalar=hgain_t.ap[0:d, gcol:gcol + 1],
                                                         in1=rs_.ap[0:d, 0:n], op0=ALU.mult, op1=ALU.mult),
                 r=[zc, rs_, hgain_t], w=[xn])
            P.op("pe", lambda e: e.matmul(bA.ap[0:96, 0:n], lhsT=rot_t.ap[0:d, ridx, 0:96], rhs=xn.ap[0:d, 0:n], start=True, stop=True),
                 r=[xn, rot_t], w=[bA])

        def stage_c(x):
            d, n, bA, xn, t1, t2, o_, rope_t = x["d"], x["n"], x["bA"], x["xn"], x["t1"], x["t2"], x["o"], x["rope_t"]
            P.op("pool", lambda e: e.tensor_tensor(out=t1.ap[0:d, 0:n], in0=xn.ap[0:d, 0:n], in1=rope_t.ap[0:d, 0, 0:n], op=ALU.mult),
                 r=[xn, rope_t], w=[t1])
            P.op("dve", lambda e: e.tensor_tensor(out=t2.ap[0:d, 0:n], in0=bA.ap[0:d, 0:n], in1=rope_t.ap[0:d, 1, 0:n], op=ALU.mult),
                 r=[bA, rope_t], w=[t2])
            P.op("pool", lambda e: e.tensor_tensor(out=o_.ap[0:d, 0:n], in0=t1.ap[0:d, 0:n], in1=t2.ap[0:d, 0:n], op=ALU.add),
                 r=[t1, t2], w=[o_])
            P.dma(None, x["out_dram"], o_.ap[0:96, 0:n], r=[o_])

        nU = len(st)
        for i in range(-1, nU + 2):
            if 0 <= i + 1 < nU:
                stage_m(st[i + 1])
            if 0 <= i < nU:
                stage_a(st[i])
            if 0 <= i - 1 < nU:
                stage_b(st[i - 1])
            if 0 <= i - 2 < nU:
                stage_c(st[i - 2])

    pending = []

    def do_block(t0, n, grp):
        P.dma("sp", xs.ap[:, :, 0:n], wview(xT)[:, :, t0:t0 + n], w=[xs])
        P.dma("pool", ropeM_t.ap[0:96, :, 0:n], ropeM[:, :, t0:t0 + n], w=[ropeM_t])
        P.dma("pool", ropeG_t.ap[0:96, :, 0:n], ropeG[:, :, t0:t0 + n], w=[ropeG_t])
        emit_norm_mod(C, xs, n, ones_bf, Acol, Bcol, grp, hT, tmpA, sqb, banks[6], rstd)
        P.dma(None, wview(hT_o)[:, :, t0:t0 + n], hT.ap[:, :, 0:n], r=[hT])
        if STOP <= 3:
            return
        for j in range(4):
            bg, ba = banks[6], banks[7]
            inproj(bg, C_GLU + 512 + j * 128, 128, n)
            s_ = sg[j % 2]
            P.op("act", lambda e, s_=s_, bg=bg: e.activation(out=s_.ap[:, 0:n], in_=bg.ap[:, 0:n], func=AF.Sigmoid), r=[bg], w=[s_])
            inproj(ba, C_GLU + j * 128, 128, n)
            y_ = yt[j % 2]
            P.op("dve", lambda e, y_=y_, ba=ba, s_=s_: e.tensor_tensor(out=y_.ap[:, 0:n], in0=ba.ap[:, 0:n], in1=s_.ap[:, 0:n], op=ALU.mult),
                 r=[ba, s_], w=[y_])
            P.dma(None, y_o[j * 128:(j + 1) * 128, t0:t0 + n], y_.ap[:, 0:n], r=[y_])
        if STOP <= 4:
            return
        for (zt, sqt, outn, col0, nch, dim, gt) in ((zq, sqa, qan, C_QA, 3, 384, gqa_t), (zkv, sqkv, kvn, C_KVA, 2, 256, gkva_t)):
            for j in range(nch):
                bk = banks[6 + (j % 2)]
                inproj(bk, col0 + j * 128, 128, n)
                P.op("act", lambda e, bk=bk, j=j, sqt=sqt, dim=dim: e.activation(out=sqt.ap[:, j, 0:n], in_=bk.ap[:, 0:n], func=AF.Square,
                                                                             scale=float(dim) ** -0.5), r=[bk], w=[sqt])
                P.op("act", lambda e, bk=bk, j=j, zt=zt: e.activation(out=zt.ap[:, j, 0:n], in_=bk.ap[:, 0:n], func=AF.Copy), r=[bk], w=[zt])
            bk = banks[6]
            for j in range(nch):
                P.op("pe", lambda e, j=j, sqt=sqt, bk=bk, nch=nch: e.matmul(bk.ap[:, 0:n], lhsT=ones_bf.ap[:, 0:128], rhs=sqt.ap[:, j, 0:n],
                                                                        start=(j == 0), stop=(j == nch - 1)), r=[sqt, ones_bf], w=[bk])
            P.op("act", lambda e, bk=bk: e.activation(out=rstd2.ap[:, 0:n], in_=bk.ap[:, 0:n], func=AF.Ln, bias=eps_t.ap[:, 0:1]), r=[bk, eps_t], w=[rstd2])
            P.op("act", lambda e, bk=bk: e.activation(out=rstd2.ap[:, 0:n], in_=rstd2.ap[:, 0:n], func=AF.Exp, scale=-0.5), r=[rstd2], w=[rstd2])
            for j in range(nch):
                P.op("dve", lambda e, j=j, zt=zt, outn=outn, gt=gt: e.scalar_tensor_tensor(
                    out=outn.ap[:, j, 0:n], in0=zt.ap[:, j, 0:n], scalar=gt.ap[:, j:j + 1], in1=rstd2.ap[:, 0:n],
                    op0=ALU.mult, op1=ALU.mult), r=[zt, gt, rstd2], w=[outn])
        if STOP <= 5:
            return
        bk = banks[7]
        inproj(bk, C_KR, 32, n)
        P.op("act", lambda e, bk=bk: e.activation(out=zkr.ap[0:32, 0:n], in_=bk.ap[0:32, 0:n], func=AF.Copy), r=[bk], w=[zkr])
        for h in range(8):
            def mq(bank, h=h):
                for j in range(3):
                    P.op("pe", lambda e, j=j: e.matmul(bank.ap[0:96, 0:n], lhsT=wqb.ap[:, j, h * 96:(h + 1) * 96], rhs=qan.ap[:, j, 0:n],
                                                       start=(j == 0), stop=(j == 2)), r=[wqb, qan], w=[bank])
            normrope(mq, 96, 0, ropeM_t, 0, qT_o[h, :, t0:t0 + n], n)
        if STOP <= 6:
            flush_units()
            return
        for h in range(8):
            def mk(bank, h=h):
                for j in range(2):
                    P.op("pe", lambda e, j=j: e.matmul(bank.ap[0:96, 0:n], lhsT=wkb.ap[:, j, h, :], rhs=kvn.ap[:, j, 0:n],
                                                       start=(j == 0), stop=False), r=[wkb, kvn], w=[bank])
                P.op("pe", lambda e: e.matmul(bank.ap[0:96, 0:n], lhsT=shift_t.ap[0:32, 0:96], rhs=zkr.ap[0:32, 0:n],
                                              start=False, stop=True), r=[shift_t, zkr], w=[bank])
            normrope(mk, 96, 1, ropeM_t, 0, kT_o[h, :, t0:t0 + n], n)
        if STOP <= 7:
            flush_units()
            return
        for h in range(8):
            def gq(bank, h=h):
                inproj(bank, C_GQ + h * 64, 64, n)
            normrope(gq, 64, 2, ropeG_t, 1, qT_o[8 + h, :, t0:t0 + n], n)
        for h in range(2):
            def gk(bank, h=h):
                inproj(bank, C_GK + h * 64, 64, n)
            normrope(gk, 64, 3, ropeG_t, 1, kT_o[8 + h, :, t0:t0 + n], n)
        flush_units()
        if STOP <= 8:
            return
        for ti in range(n // 128):
            v_ = vt[ti % 2]
            bm, bg = banks[6], banks[7]
            for j in range(2):
                P.op("pe", lambda e, j=j, ti=ti, bm=bm: e.matmul(bm.ap[:, 0:512], lhsT=kvn.ap[:, j, ti * 128:(ti + 1) * 128],
                                                             rhs=wkv.ap[:, j].rearrange("p h d -> p (h d)"),
                                                             start=(j == 0), stop=(j == 1)), r=[kvn, wkv], w=[bm])
            for kc in range(8):
                P.op("pe", lambda e, kc=kc, ti=ti, bg=bg: e.matmul(bg.ap[:, 0:128], lhsT=hT.ap[:, kc, ti * 128:(ti + 1) * 128],
                                                               rhs=w_a.ap[:, kc, C_GV:C_GV + 128],
                                                               start=(kc == 0), stop=(kc == 7)), r=[hT, w_a], w=[bg])
            P.op("act", lambda e, v_=v_, bm=bm: e.activation(out=v_.ap[:, 0:8, 0:64], in_=bm.ap[:, 0:512].rearrange("p (h d) -> p h d", d=64),
                                                            func=AF.Copy), r=[bm], w=[v_])
            P.op("act", lambda e, v_=v_, bg=bg: e.activation(out=v_.ap[:, 8:10, 0:64], in_=bg.ap[:, 0:128].rearrange("p (h d) -> p h d", d=64),
                                                            func=AF.Copy), r=[bg], w=[v_])
            P.dma(None, vP_o[t0 + ti * 128:t0 + (ti + 1) * 128], v_.ap, r=[v_])
    for (t0_, n_, grp_) in BLOCKS:
        do_block(t0_, n_, grp_)
    return C.finish(), C


KB = 1024


def attention_phase(C, heads, NQ, key_blocks, ones_f):
    P = C.P
    banks = C.banks
    psum_all = C.psum
    m0 = P.mark()
    NBUF = 4
    HW = min(1024, NQ)
    assert NQ == HW
    SW = min(512, HW)
    NS = HW // SW
    nob = NS
    kbuf = [P.sb(f"kbuf{i}", [KB], BF16) for i in range(NBUF)]
    vbuf = [P.sb(f"vbuf{i}", [KB // 128, 65], BF16) for i in range(NBUF)]
    qbuf = [P.sb(f"qbuf{i}", [NQ], BF16) for i in range(2)]
    NPT = 4
    pT = [P.sb(f"pT{i}", [HW], BF16) for i in range(NPT)]
    rs = P.sb("rs", [NQ], F32)
    bcs = P.sb("bcs", [NQ], F32)
    obuf = [P.sb(f"obuf{i}", [NQ], BF16) for i in range(2)]
    Ob = banks[0:nob]
    Sb = []
    bi_ = nob
    while bi_ + NS <= 8:
        Sb.append(list(range(bi_, bi_ + NS)))
        bi_ += NS
    Sb = Sb[:3]
    LA = len(Sb) - 1
    nblk_total = 0
    sidx = [0]
    pidx = [0]
    dk = 96
    for hi, hd in enumerate(heads):
        qb = qbuf[hi % 2]
        P.dma("sp", qb.ap[0:dk, :], hd["q"], w=[qb])
        nchunks_total = sum(n // 128 for _, n in key_blocks)
        ci = 0
        its = []
        binfo = []
        for (k0, kn) in key_blocks:
            kb_ = kbuf[nblk_total % NBUF]
            vb_ = vbuf[nblk_total % NBUF]
            nblk_total += 1
            nch = kn // 128
            binfo.append(dict(k0=k0, kn=kn, kb=kb_, vb=vb_, nch=nch, loaded=False))
            for c in range(nch):
                its.append(dict(kb=kb_, vb=vb_, c=c, kn=kn, nch=nch, first=(ci == 0), last=(ci == nchunks_total - 1), blk=len(binfo) - 1))
                ci += 1

        def load_upto(nb):
            for b in binfo[:nb + 1]:
                if not b["loaded"]:
                    b["loaded"] = True
                    k0, kn, nch = b["k0"], b["kn"], b["nch"]
                    P.dma("sp", b["kb"].ap[0:dk, 0:kn], hd["k"][:, k0:k0 + kn], w=[b["kb"]])
                    P.dma("pool", b["vb"].ap[:, 0:nch, :], hd["v"][k0:k0 + kn, :].rearrange("(p c) e -> p c e", c=nch), w=[b["vb"]])

        def emit_S(it):
            load_upto(it["blk"] + 2)
            sbi = Sb[sidx[0] % len(Sb)]
            sidx[0] += 1
            it["sbi"] = sbi
            for j in range(NS):
                bk = banks[sbi[j]]
                P.op("pe", lambda e, o=bk.ap[:, 0:SW], l=it["kb"].ap[0:dk, it["c"]:it["kn"]:it["nch"]], r=qb.ap[0:dk, j * SW:(j + 1) * SW]:
                     e.matmul(o, lhsT=l, rhs=r, start=True, stop=True), r=[it["kb"], qb], w=[bk])

        def emit_exp(it):
            pt = pT[pidx[0] % NPT]
            pidx[0] += 1
            it["pt"] = pt
            sbi = it["sbi"]
            s_ap = psum_all[:, sbi[0] * 512:sbi[0] * 512 + (1024 if NS == 2 else SW)]
            P.op("act", lambda e, o=pt.ap[:, 0:HW], i=s_ap, sc=hd["scale"]: e.activation(out=o, in_=i, func=AF.Exp, scale=sc),
                 r=[banks[x] for x in sbi], w=[pt])

        def emit_PV(it):
            for j in range(NS):
                ob = Ob[j]
                P.op("pe", lambda e, o=ob.ap[0:65, 0:SW], l=it["vb"].ap[:, it["c"], :], r=it["pt"].ap[:, j * SW:(j + 1) * SW],
                     f=it["first"], la=it["last"]: e.matmul(o, lhsT=l, rhs=r, start=f, stop=la), r=[it["vb"], it["pt"]], w=[ob])

        n_it = len(its)
        for i in range(min(LA, n_it)):
            emit_S(its[i])
        emit_exp(its[0])
        for i in range(n_it):
            if i + LA < n_it:
                emit_S(its[i + LA])
            if i + 1 < n_it:
                emit_exp(its[i + 1])
            emit_PV(its[i])
        for j in range(nob):
            P.op("dve", lambda e, j=j: e.reciprocal(out=rs.ap[64:65, j * SW:(j + 1) * SW], in_=Ob[j].ap[64:65, 0:SW]), r=[Ob[j]], w=[rs])
        for j in range(nob):
            sbk = banks[nob + j]
            P.op("pe", lambda e, sbk=sbk, j=j: e.matmul(sbk.ap[0:64, 0:SW], lhsT=ones_f.ap[64:65, 0:64], rhs=rs.ap[64:65, j * SW:(j + 1) * SW],
                                                    start=True, stop=True), r=[rs, ones_f], w=[sbk])
            P.op("act", lambda e, sbk=sbk, j=j: e.activation(out=bcs.ap[0:64, j * SW:(j + 1) * SW], in_=sbk.ap[0:64, 0:SW], func=AF.Copy),
                 r=[sbk], w=[bcs])
        ob_ = obuf[hi % 2]
        for j in range(nob):
            P.op("dve", lambda e, j=j, ob_=ob_: e.tensor_tensor(out=ob_.ap[0:64, j * SW:(j + 1) * SW], in0=Ob[j].ap[0:64, 0:SW],
                                                               in1=bcs.ap[0:64, j * SW:(j + 1) * SW], op=ALU.mult),
                 r=[Ob[j], bcs], w=[ob_])
        P.dma("sp", hd["o"], ob_.ap[0:64, :], r=[ob_])
    P.release(m0)


def build_attn():
    C = Ctx()
    qT = C.inp("qT", [NQH, 96, TT], BF16)
    kTf = C.inp("kTf", [NKH, 96, NK], BF16)
    vPf = C.inp("vPf", [NKH, NK, 65], BF16)
    hT_d = C.inp("hT", [D, TT], BF16)
    xT = C.inp("xT", [D, TT])
    yh = C.inp("yh", [512, TL + 30])
    yc = C.inp("yc", [512, CTX + 30])
    modv = C.inp("modv", [128, 6, 8, 2])
    w_in = C.inp("w_in", [D, IN_COLS])
    bgc = C.inp("bgc", [128, 24])
    cwc = C.inp("cwc", [128, 4, 31])
    cvec = C.inp("cvec", [128, 4, 3])
    w_conv_out = C.inp("w_conv_out", [512, D])
    w_mla_o = C.inp("w_mla_o", [512, D])
    w_gqa_o = C.inp("w_gqa_o", [512, D])
    w_out = C.inp("w_out", [D, D])
    xm_o = C.out("xmT", [D, TT])
    h2_o = C.out("h2T", [D, TT], BF16)
    oT_d = C.scratch("oT_d", [NQH * 64, TT], BF16)
    P = C.start()
    eps_t = C.eps_t
    banks = C.banks
    ones_f = P.sb("ones_f", [64], F32)
    P.op("pool", lambda e: e.memset(ones_f.ap, 1.0), w=[ones_f])
    lat_blocks = [(0, CTX)] + [(CTX + b * KB, KB) for b in range(SEQ // KB)]
    heads_l, heads_c = [], []
    for h in range(NQH):
        kvh = h if h < 8 else 8 + (h - 8) // 4
        sc = MLA_SCALE if h < 8 else GQA_SCALE
        for qh in range(TL // 1024):
            heads_l.append(dict(q=qT[h, :, qh * 1024:(qh + 1) * 1024], k=kTf[kvh], v=vPf[kvh],
                                o=oT_d[h * 64:(h + 1) * 64, qh * 1024:(qh + 1) * 1024], scale=sc))
        heads_c.append(dict(q=qT[h, :, TL:TT], k=kTf[kvh], v=vPf[kvh], o=oT_d[h * 64:(h + 1) * 64, TL:TT], scale=sc))
    attention_phase(C, heads_c, CTX, [(0, CTX)], ones_f)
    attention_phase(C, heads_l, 1024, lat_blocks, ones_f)
    P.barrier()
    C.cv_engs = ("pool", "act", "dve")
    C.nst = 2
    NB = 256
    ones_bf = P.sb("ones_bf", [128], BF16)
    P.op("pool", lambda e: e.memset(ones_bf.ap, 1.0), w=[ones_bf])
    ones512 = P.sb("ones512", [128], BF16)
    P.op("pool", lambda e: e.memset(ones512.ap, 1.0 / 512.0), w=[ones512])
    modc = C.load_f32("modc", modv, [6, 8, 2])
    G1 = T(modc.ap[:, 2], "G1")
    A2 = T(modc.ap[:, 3], "A2")
    B2 = T(modc.ap[:, 4], "B2")
    bg_t = C.load_f32("bg_t", bgc, [24])
    cw_t = C.load_f32("cw_t", cwc, [4, 31])
    cv_t = C.load_f32("cv_t", cvec, [4, 3])
    wiv = wview(w_in)
    w_g = C.load_bf16("w_g", wiv[:, :, C_GATE:IN_COLS], 8, 3072)
    w_co = C.load_bf16("w_co", wview(w_conv_out), 4, D)
    w_mo = C.load_bf16("w_mo", wview(w_mla_o), 4, D)
    w_go = C.load_bf16("w_go", wview(w_gqa_o), 4, D)
    w_o = C.load_bf16("w_o", wview(w_out), 8, D)
    xs = P.sb("xs", [8, NB], F32)
    hT = P.sb("hT", [8, NB], BF16)
    om = P.sb("om", [4, NB], BF16)
    og = P.sb("og", [4, NB], BF16)
    FB = []
    for i in range(2):
        FB.append(dict(ytile=P.sb(f"ytile{i}", [4, NB + 30], F32), scrF=P.sb(f"scrF{i}", [8, NB], F32), scrB=P.sb(f"scrB{i}", [8, NB], BF16),
                       mean_s=P.sb(f"mean_s{i}", [NB], F32), var_s=P.sb(f"var_s{i}", [NB], F32), rstd_c=P.sb(f"rstd_c{i}", [NB], F32),
                       ys=P.sb(f"ys{i}", [4, NB], BF16)))
    gate = [P.sb(f"gate{i}", [NB], F32) for i in range(3)]
    mtmp = [P.sb(f"mtmp{i}", [NB], F32) for i in range(2)]
    macc = P.sb("macc", [NB], F32)
    merged = P.sb("merged", [8, NB], BF16)
    xm = P.sb("xm", [8, NB], F32)
    h2 = P.sb("h2", [8, NB], BF16)
    rstd = P.sb("rstd", [NB], F32)
    sqb2 = P.sb("sqb2", [8, NB], BF16)
    ntmp = [P.sb(f"ntmp{i}", [NB], F32) for i in range(3)]
    oT3 = oT_d.rearrange("(b j p) t -> b p j t", b=2, p=128)
    MBLOCKS = [(i * NB, NB, 0) for i in range(TL // NB)] + [(TL, CTX, 1)]

    def front_pieces(bi):
        t0, n, grp = MBLOCKS[bi]
        fb = FB[bi % 2]
        ytile, scrF, scrB, mean_s, var_s, rstd_c, ys = fb["ytile"], fb["scrF"], fb["scrB"], fb["mean_s"], fb["var_s"], fb["rstd_c"], fb["ys"]
        pieces = []

        def load():
            ysrc = yh if grp == 0 else yc
            y0 = t0 if grp == 0 else 0
            P.dma("sp", ytile.ap[:, :, 0:n + 30], wview(ysrc)[:, :, y0:y0 + n + 30], w=[ytile])

        def conv_part(ch, k0, k1):
            def f():
                if ch == 0 and k0 == 0:
                    load()
                for k in range(k0, k1):
                    if k == 0:
                        P.op("dve", lambda e: e.tensor_scalar(out=scrF.ap[:, ch, 0:n], in0=ytile.ap[:, ch, 0:n], scalar1=cw_t.ap[:, ch, 0:1],
                                                              scalar2=cv_t.ap[:, ch, 0:1], op0=ALU.mult, op1=ALU.add),
                             r=[ytile, cw_t, cv_t], w=[scrF])
                    else:
                        P.op("dve", lambda e, k=k: e.scalar_tensor_tensor(out=scrF.ap[:, ch, 0:n], in0=ytile.ap[:, ch, k:k + n],
                                                                          scalar=cw_t.ap[:, ch, k:k + 1], in1=scrF.ap[:, ch, 0:n],
                                                                          op0=ALU.mult, op1=ALU.add), r=[ytile, cw_t, scrF], w=[scrF])
            return f
        for ch in range(4):
            pieces.append(conv_part(ch, 0, 16))
            pieces.append(conv_part(ch, 16, 31))

        def ln():
            P.op("act", lambda e: e.activation(out=scrB.ap[:, 0:4, 0:n], in_=scrF.ap[:, 0:4, 0:n], func=AF.Copy), r=[scrF], w=[scrB])
            P.op("act", lambda e: e.activation(out=scrB.ap[:, 4:8, 0:n], in_=scrF.ap[:, 0:4, 0:n], func=AF.Square, scale=512.0 ** -0.5),
                 r=[scrF], w=[scrB])
            bm, bv = banks[0], banks[1]
            for ch in range(4):
                P.op("pe", lambda e, ch=ch: e.matmul(bm.ap[:, 0:n], lhsT=ones512.ap[:, 0:128], rhs=scrB.ap[:, ch, 0:n], start=(ch == 0), stop=(ch == 3)),
                     r=[scrB, ones512], w=[bm])
            for ch in range(4):
                P.op("pe", lambda e, ch=ch: e.matmul(bv.ap[:, 0:n], lhsT=ones_bf.ap[:, 0:128], rhs=scrB.ap[:, 4 + ch, 0:n], start=(ch == 0), stop=(ch == 3)),
                     r=[scrB, ones_bf], w=[bv])
            P.op("act", lambda e: e.activation(out=mean_s.ap[:, 0:n], in_=bm.ap[:, 0:n], func=AF.Copy), r=[bm], w=[mean_s])
            P.op("dve", lambda e: e.tensor_tensor(out=var_s.ap[:, 0:n], in0=mean_s.ap[:, 0:n], in1=mean_s.ap[:, 0:n], op=ALU.mult), r=[mean_s], w=[var_s])
            P.op("dve", lambda e: e.tensor_tensor(out=var_s.ap[:, 0:n], in0=bv.ap[:, 0:n], in1=var_s.ap[:, 0:n], op=ALU.subtract), r=[bv, var_s], w=[var_s])
            P.op("act", lambda e: e.activation(out=rstd_c.ap[:, 0:n], in_=var_s.ap[:, 0:n], func=AF.Ln, bias=eps_t.ap[:, 0:1]), r=[var_s, eps_t], w=[rstd_c])
            P.op("act", lambda e: e.activation(out=rstd_c.ap[:, 0:n], in_=rstd_c.ap[:, 0:n], func=AF.Exp, scale=-0.5), r=[rstd_c], w=[rstd_c])
            for ch in range(4):
                P.op("dve", lambda e, ch=ch: e.tensor_tensor(out=scrF.ap[:, 4 + ch, 0:n], in0=scrF.ap[:, ch, 0:n], in1=mean_s.ap[:, 0:n], op=ALU.subtract),
                     r=[scrF, mean_s], w=[scrF])
                P.op("dve", lambda e, ch=ch: e.tensor_tensor(out=scrF.ap[:, 4 + ch, 0:n], in0=scrF.ap[:, 4 + ch, 0:n], in1=rstd_c.ap[:, 0:n], op=ALU.mult),
                     r=[scrF, rstd_c], w=[scrF])
                P.op("act", lambda e, ch=ch: e.activation(out=ys.ap[:, ch, 0:n], in_=scrF.ap[:, 4 + ch, 0:n], func=AF.Silu,
                                                          scale=cv_t.ap[:, ch, 1:2], bias=cv_t.ap[:, ch, 2:3]), r=[scrF, cv_t], w=[ys])
        pieces.append(ln)
        return pieces

    def back(bi, nxt):
        t0, n, grp = MBLOCKS[bi]
        ys = FB[bi % 2]["ys"]
        P.dma("sp", xs.ap[:, :, 0:n], wview(xT)[:, :, t0:t0 + n], w=[xs])
        P.dma("pool", hT.ap[:, :, 0:n], wview(hT_d)[:, :, t0:t0 + n], w=[hT])
        P.dma("pool", om.ap[:, :, 0:n], oT3[0][:, :, t0:t0 + n], w=[om])
        P.dma("pool", og.ap[:, :, 0:n], oT3[1][:, :, t0:t0 + n], w=[og])

        def merge_oc(oc):
            bc_, bm_, bg_ = banks[2], banks[3], banks[4]
            gb = [banks[5], banks[6], banks[7]]
            for (bk_, w_, src_) in ((bc_, w_co, ys), (bm_, w_mo, om), (bg_, w_go, og)):
                for ch in range(4):
                    P.op("pe", lambda e, ch=ch, bk_=bk_, w_=w_, src_=src_: e.matmul(bk_.ap[:, 0:n], lhsT=w_.ap[:, ch, oc * 128:(oc + 1) * 128],
                                                                                rhs=src_.ap[:, ch, 0:n], start=(ch == 0), stop=(ch == 3)),
                         r=[w_, src_], w=[bk_])
            for b in range(3):
                c0 = b * 1024 + oc * 128
                for kc in range(8):
                    P.op("pe", lambda e, kc=kc, b=b, c0=c0: e.matmul(gb[b].ap[:, 0:n], lhsT=w_g.ap[:, kc, c0:c0 + 128], rhs=hT.ap[:, kc, 0:n],
                                                                 start=(kc == 0), stop=(kc == 7)), r=[w_g, hT], w=[gb[b]])
                P.op("act", lambda e, b=b: e.activation(out=gate[b].ap[:, 0:n], in_=gb[b].ap[:, 0:n], func=AF.Sigmoid,
                                                        bias=bg_t.ap[:, b * 8 + oc:b * 8 + oc + 1]), r=[gb[b], bg_t], w=[gate[b]])
            P.op("dve", lambda e: e.tensor_tensor(out=macc.ap[:, 0:n], in0=bc_.ap[:, 0:n], in1=gate[0].ap[:, 0:n], op=ALU.mult),
                 r=[bc_, gate[0]], w=[macc])
            P.op("dve", lambda e: e.tensor_tensor(out=mtmp[0].ap[:, 0:n], in0=bm_.ap[:, 0:n], in1=gate[1].ap[:, 0:n], op=ALU.mult),
                 r=[bm_, gate[1]], w=[mtmp[0]])
            P.op("dve", lambda e: e.tensor_tensor(out=mtmp[1].ap[:, 0:n], in0=bg_.ap[:, 0:n], in1=gate[2].ap[:, 0:n], op=ALU.mult),
                 r=[bg_, gate[2]], w=[mtmp[1]])
            P.op("dve", lambda e: e.tensor_tensor(out=macc.ap[:, 0:n], in0=macc.ap[:, 0:n], in1=mtmp[0].ap[:, 0:n], op=ALU.add),
                 r=[macc, mtmp[0]], w=[macc])
            P.op("dve", lambda e: e.tensor_tensor(out=merged.ap[:, oc, 0:n], in0=macc.ap[:, 0:n], in1=mtmp[1].ap[:, 0:n], op=ALU.add),
                 r=[macc, mtmp[1]], w=[merged])

        for oc_ in range(8):
            merge_oc(oc_)
            if nxt:
                nxt.pop(0)()
        while nxt:
            nxt.pop(0)()

        def outproj_oc(oc):
            bk = banks[oc % 2]
            for kc in range(8):
                P.op("pe", lambda e, kc=kc: e.matmul(bk.ap[:, 0:n], lhsT=w_o.ap[:, kc, oc * 128:(oc + 1) * 128], rhs=merged.ap[:, kc, 0:n],
                                                     start=(kc == 0), stop=(kc == 7)), r=[w_o, merged], w=[bk])
            P.op("dve", lambda e: e.scalar_tensor_tensor(out=xm.ap[:, oc, 0:n], in0=bk.ap[:, 0:n], scalar=G1.ap[:, oc, grp:grp + 1],
                                                         in1=xs.ap[:, oc, 0:n], op0=ALU.mult, op1=ALU.add),
                 r=[bk, G1, xs], w=[xm])

        for oc_ in range(8):
            outproj_oc(oc_)
        P.dma(None, wview(xm_o)[:, :, t0:t0 + n], xm.ap[:, :, 0:n], r=[xm])
        emit_norm_mod(C, xm, n, ones_bf, A2, B2, grp, h2, ntmp, sqb2, banks[2], rstd)
        P.dma(None, wview(h2_o)[:, :, t0:t0 + n], h2.ap[:, :, 0:n], r=[h2])

    for f_ in front_pieces(0):
        f_()
    for bi in range(len(MBLOCKS)):
        nxt = front_pieces(bi + 1) if bi + 1 < len(MBLOCKS) else []
        back(bi, nxt)
    return C.finish(), C


def build_ffn():
    C = Ctx()
    xm_d = C.inp("xmT", [D, TT])
    h2l = C.inp("h2l", [D, TL + 2], BF16)
    h2c = C.inp("h2c", [D, CTX + 2], BF16)
    modv = C.inp("modv", [128, 6, 8, 2])
    w_up = C.inp("w_up", [D, 2 * D_FF])
    w_down = C.inp("w_down", [D_FF, D])
    fwc = C.inp("fwc", [128, 44, 4])
    x_o = C.out("xT_out", [D, TT])
    P = C.start()
    C.cv_engs = ("pool", "act", "dve")
    C.nst = 2
    banks = C.banks
    modc = C.load_f32("modc", modv, [6, 8, 2])
    G2 = T(modc.ap[:, 5], "G2")
    fw_t = C.load_f32("fw_t", fwc, [44, 4])
    wu = C.load_bf16("wu", wview(w_up), 8, 2 * D_FF)
    wd = C.load_bf16("wd", wview(w_down), 22, D)
    h2 = P.sb("h2", [8, 514], BF16)
    xmc = [P.sb(f"xmc{i}", [512], F32) for i in range(2)]
    act_t = P.sb("act_t", [22, 512], BF16)
    ta = [P.sb(f"ta{i}", [512], F32) for i in range(2)]
    tg = [P.sb(f"tg{i}", [512], F32) for i in range(2)]
    sgt = [P.sb(f"sgt{i}", [512], F32) for i in range(2)]
    cnt = [0]

    def conv_chunk(cc, n, dst):
        i = cnt[0]
        cnt[0] += 1
        bk = banks[2 + 2 * (i % 3)]
        bh = banks[3 + 2 * (i % 3)]
        for kc in range(8):
            P.op("pe", lambda e, kc=kc: e.matmul(bk.ap[:, 0:n], lhsT=wu.ap[:, kc, cc * 128:(cc + 1) * 128], rhs=h2.ap[:, kc, 1:n + 1],
                                                 start=(kc == 0), stop=(kc == 7)), r=[wu, h2], w=[bk])
        for kc in range(8):
            P.op("pe", lambda e, kc=kc: e.matmul(bh.ap[:, 0:2], lhsT=wu.ap[:, kc, cc * 128:(cc + 1) * 128], rhs=h2.ap[:, kc, 0:n + 2:n + 1],
                                                 start=(kc == 0), stop=(kc == 7)), r=[wu, h2], w=[bh])
        P.op("act", lambda e: e.activation(out=dst.ap[:, 0:n], in_=bk.ap[:, 0:n], func=AF.Identity, scale=fw_t.ap[:, cc, 1:2],
                                           bias=fw_t.ap[:, cc, 3:4]), r=[bk, fw_t], w=[dst])
        P.op("dve", lambda e: e.scalar_tensor_tensor(out=dst.ap[:, 1:n], in0=bk.ap[:, 0:n - 1], scalar=fw_t.ap[:, cc, 0:1], in1=dst.ap[:, 1:n],
                                                     op0=ALU.mult, op1=ALU.add), r=[bk, fw_t, dst], w=[dst])
        P.op("dve", lambda e: e.scalar_tensor_tensor(out=dst.ap[:, 0:n - 1], in0=bk.ap[:, 1:n], scalar=fw_t.ap[:, cc, 2:3], in1=dst.ap[:, 0:n - 1],
                                                     op0=ALU.mult, op1=ALU.add), r=[bk, fw_t, dst], w=[dst])
        P.op("dve", lambda e: e.scalar_tensor_tensor(out=dst.ap[:, 0:1], in0=bh.ap[:, 0:1], scalar=fw_t.ap[:, cc, 0:1], in1=dst.ap[:, 0:1],
                                                     op0=ALU.mult, op1=ALU.add), r=[bh, fw_t, dst], w=[dst])
        P.op("dve", lambda e: e.scalar_tensor_tensor(out=dst.ap[:, n - 1:n], in0=bh.ap[:, 1:2], scalar=fw_t.ap[:, cc, 2:3], in1=dst.ap[:, n - 1:n],
                                                     op0=ALU.mult, op1=ALU.add), r=[bh, fw_t, dst], w=[dst])

    def do_block(t0, n, grp):
        hsrc = h2l if grp == 0 else h2c
        h0 = t0 if grp == 0 else 0
        P.dma("sp", h2.ap[:, :, 0:n + 2], wview(hsrc)[:, :, h0:h0 + n + 2], w=[h2])
        for j in range(22):
            a_, g_, s_ = ta[j % 2], tg[j % 2], sgt[j % 2]
            conv_chunk(22 + j, n, g_)
            conv_chunk(j, n, a_)
            P.op("act", lambda e, g_=g_, s_=s_: e.activation(out=s_.ap[:, 0:n], in_=g_.ap[:, 0:n], func=AF.Silu), r=[g_], w=[s_])
            P.op("pool", lambda e, j=j, a_=a_, s_=s_: e.tensor_tensor(out=act_t.ap[:, j, 0:n], in0=a_.ap[:, 0:n], in1=s_.ap[:, 0:n], op=ALU.mult),
                 r=[a_, s_], w=[act_t])

        def down_oc(oc):
            bk = banks[oc % 2]
            xc = xmc[oc % 2]
            P.dma("pool", xc.ap[:, 0:n], xm_d[oc * 128:(oc + 1) * 128, t0:t0 + n], w=[xc])
            for j in range(22):
                P.op("pe", lambda e, j=j: e.matmul(bk.ap[:, 0:n], lhsT=wd.ap[:, j, oc * 128:(oc + 1) * 128], rhs=act_t.ap[:, j, 0:n],
                                                   start=(j == 0), stop=(j == 21)), r=[wd, act_t], w=[bk])
            P.op("dve", lambda e: e.scalar_tensor_tensor(out=xc.ap[:, 0:n], in0=bk.ap[:, 0:n], scalar=G2.ap[:, oc, grp:grp + 1],
                                                         in1=xc.ap[:, 0:n], op0=ALU.mult, op1=ALU.add),
                 r=[bk, G2, xc], w=[xc])
            P.dma("sp", x_o[oc * 128:(oc + 1) * 128, t0:t0 + n], xc.ap[:, 0:n], r=[xc])

        for oc_ in range(8):
            down_oc(oc_)

    for (t0_, n_, grp_) in BLOCKS:
        do_block(t0_, n_, grp_)
    return C.finish(), C


def _cols(v, n=128):
    v = np.asarray(v, np.float32)
    return np.ascontiguousarray(v.reshape(-1, n).T)


def _rope_tables():
    tabs = []
    tok = np.arange(SEQ)
    row = (tok // 64).astype(np.float32)
    col = (tok % 64).astype(np.float32)

    def mk(rot_dim, off):
        nf = rot_dim // 4
        inv = (1.0 / (10000.0 ** (np.arange(nf, dtype=np.float32) / nf))).astype(np.float32)
        ar = row[:, None] * inv
        ac = col[:, None] * inv
        ang = np.concatenate([ar, ar, ac, ac], axis=-1)
        cos = np.ones((96, SEQ), np.float32)
        sin = np.zeros((96, SEQ), np.float32)
        cos[off:off + rot_dim] = np.cos(ang).T
        sin[off:off + rot_dim] = np.sin(ang).T
        return cos, sin

    cm, sm = mk(32, 64)
    cg, sg = mk(64, 0)
    for c in range(NCORES):
        sl = slice(c * TL, (c + 1) * TL)
        M = np.zeros((96, 2, TT), np.float32)
        G = np.zeros((96, 2, TT), np.float32)
        M[:, 0, :TL] = cm[:, sl]; M[:, 1, :TL] = sm[:, sl]; M[:, 0, TL:] = 1.0
        G[:, 0, :TL] = cg[:, sl]; G[:, 1, :TL] = sg[:, sl]; G[:, 0, TL:] = 1.0
        tabs.append((M, G))
    rot = np.zeros((96, 2, 96), np.float32)
    for i in range(8):
        rot[72 + i, 0, 64 + i] = -1.0; rot[64 + i, 0, 72 + i] = 1.0
        rot[88 + i, 0, 80 + i] = -1.0; rot[80 + i, 0, 88 + i] = 1.0
    for i in range(16):
        rot[16 + i, 1, i] = -1.0; rot[i, 1, 16 + i] = 1.0
        rot[48 + i, 1, 32 + i] = -1.0; rot[32 + i, 1, 48 + i] = 1.0
    shift = np.zeros((32, 96), np.float32)
    for i in range(32):
        shift[i, 64 + i] = 1.0
    return tabs, rot.astype(NPBF), shift.astype(NPBF)


_PROGS = {}


def _prog(name):
    if name not in _PROGS:
        _PROGS[name] = {"pre": build_pre, "attn": build_attn, "ffn": build_ffn}[name]()[0]
    return _PROGS[name]


def _run(name, in_maps):
    res = run_bass_kernel_spmd(_prog(name), in_maps, core_ids=list(range(NCORES)))
    return res.results


def kernel(**inp):
    f32 = np.float32
    x = np.asarray(inp["x"], f32)[0]
    ctx = np.asarray(inp["ctx"], f32)[0]
    tabs, rot, shift = _rope_tables()
    c2 = np.ascontiguousarray(np.stack([_cols(inp["c"][0]), _cols(inp["c_ctx"])], axis=-1))
    xT = [np.ascontiguousarray(np.concatenate([x[c * TL:(c + 1) * TL], ctx], axis=0).T) for c in range(NCORES)]
    for l in range(DEPTH):
        hg = np.zeros((96, 4), f32)
        hg[:, 0] = inp["g_mla_q"][l]; hg[:, 1] = inp["g_mla_k"][l]
        hg[:64, 2] = inp["g_gqa_q"][l]; hg[:64, 3] = inp["g_gqa_k"][l]
        w_in_l = np.ascontiguousarray(inp["w_in"][l], dtype=f32)
        pre_in = []
        for c in range(NCORES):
            pre_in.append(dict(
                xT=xT[c], c2=c2, w_mod=np.ascontiguousarray(inp["w_mod"][l], dtype=f32), b_modc=_cols(inp["b_mod"][l]),
                gn1c=_cols(inp["g_norm1"][l]), gn2c=_cols(inp["g_norm2"][l]), w_in=w_in_l,
                gqac=_cols(inp["g_q_a"][l]), gkvac=_cols(inp["g_kv_a"][l]),
                w_q_b=np.ascontiguousarray(inp["w_q_b"][l], dtype=f32), w_kv_b=np.ascontiguousarray(inp["w_kv_b"][l], dtype=f32),
                hgain=hg, ropeM=tabs[c][0], ropeG=tabs[c][1], rotm=rot, shiftm=shift))
        r1 = _run("pre", pre_in)
        kTf = np.concatenate([r1[0]["kT"][:, :, TL:]] + [r1[c]["kT"][:, :, :TL] for c in range(NCORES)], axis=2)
        vPf = np.concatenate([r1[0]["vP"][TL:]] + [r1[c]["vP"][:TL] for c in range(NCORES)], axis=0)
        vPf = np.ascontiguousarray(vPf.transpose(1, 0, 2))
        kTf = np.ascontiguousarray(kTf)
        yfull = np.concatenate([r1[c]["y"][:, :TL] for c in range(NCORES)], axis=1)
        ypad = np.concatenate([np.zeros((512, 15), f32), yfull, np.zeros((512, 15), f32)], axis=1)
        cvec = np.ascontiguousarray(np.stack([_cols(inp["conv_dw_b"][l]), _cols(inp["conv_ln_g"][l]), _cols(inp["conv_ln_b"][l])], axis=-1))
        cwc = np.ascontiguousarray(np.asarray(inp["conv_dw_w"][l], f32).T.reshape(4, 128, 31).transpose(1, 0, 2))
        at_in = []
        for c in range(NCORES):
            yc = np.concatenate([np.zeros((512, 15), f32), r1[c]["y"][:, TL:], np.zeros((512, 15), f32)], axis=1)
            at_in.append(dict(
                qT=r1[c]["qT"], kTf=kTf, vPf=vPf, hT=r1[c]["hT"], xT=xT[c],
                yh=np.ascontiguousarray(ypad[:, c * TL:c * TL + TL + 30]), yc=np.ascontiguousarray(yc),
                modv=r1[c]["modv"], w_in=w_in_l, bgc=_cols(inp["b_gate"][l]), cwc=cwc, cvec=cvec,
                w_conv_out=np.ascontiguousarray(inp["w_conv_out"][l], dtype=f32), w_mla_o=np.ascontiguousarray(inp["w_mla_o"][l], dtype=f32),
                w_gqa_o=np.ascontiguousarray(inp["w_gqa_o"][l], dtype=f32), w_out=np.ascontiguousarray(inp["w_out"][l], dtype=f32)))
        r2 = _run("attn", at_in)
        h2full = np.concatenate([r2[c]["h2T"][:, :TL] for c in range(NCORES)], axis=1)
        zc = np.zeros((D, 1), NPBF)
        h2pad = np.concatenate([zc, h2full, zc], axis=1)
        fw = np.asarray(inp["ffn_dw_w"][l], f32)
        fwc = np.ascontiguousarray(np.stack([_cols(fw[0]), _cols(fw[1]), _cols(fw[2]), _cols(inp["ffn_dw_b"][l])], axis=-1))
        ff_in = []
        for c in range(NCORES):
            h2c = np.concatenate([zc, r2[c]["h2T"][:, TL:], zc], axis=1)
            ff_in.append(dict(
                xmT=r2[c]["xmT"], h2l=np.ascontiguousarray(h2pad[:, c * TL:c * TL + TL + 2]), h2c=np.ascontiguousarray(h2c),
                modv=r1[c]["modv"], w_up=np.ascontiguousarray(inp["w_up"][l], dtype=f32),
                w_down=np.ascontiguousarray(inp["w_down"][l], dtype=f32), fwc=fwc))
        r3 = _run("ffn", ff_in)
        xT = [r3[c]["xT_out"] for c in range(NCORES)]
    out = np.concatenate([xT[c][:, :TL].T for c in range(NCORES)], axis=0)[None]
    return np.ascontiguousarray(out.astype(np.float32))
```

```python
import numpy as np
import ml_dtypes
from contextlib import ExitStack
import concourse.bass as bass
import concourse.mybir as mybir
from concourse.bass_utils import run_bass_kernel_spmd

F32 = mybir.dt.float32
BF16 = mybir.dt.bfloat16
U8 = mybir.dt.uint8
AF = mybir.ActivationFunctionType
ALU = mybir.AluOpType
NPBF = ml_dtypes.bfloat16

NCORES = 8
D = 1024
SEQ = 16384
TL = SEQ // NCORES
CTX = 256
TT = TL + CTX
NK = CTX + SEQ
DEPTH = 4
EPS = 1e-6
BLOCKS = [(0, 512, 0), (512, 512, 0), (1024, 512, 0), (1536, 512, 0), (2048, 256, 1)]
NQH = 16
NKH = 10
MLA_SCALE = 96 ** -0.5
GQA_SCALE = 64 ** -0.5
C_GLU, C_QA, C_KVA, C_KR, C_GQ, C_GK, C_GV, C_GATE = 0, 1024, 1408, 1664, 1696, 2208, 2336, 2464
IN_COLS = 5536
D_FF = 2816

ENGS = ("pe", "act", "dve", "pool", "sp")
SEM_LIM = 30000
DMA_K = 12


class T:
    __slots__ = ("ap", "name")

    def __init__(self, ap, name):
        self.ap = ap
        self.name = name


class Prog:
    def __init__(self, nc, arena, arena_bytes):
        self.nc = nc
        self.ops = []
        self.last_w = {}
        self.readers = {}
        self.arena_bytes = arena_bytes
        self.arena = arena
        self.off = 0
        self.ndma = {e: 0 for e in ENGS}
        self.last_op = {e: None for e in ENGS}
        self.peak = 0
        self.dmaq = 0

    def sb(self, name, shape, dtype, parts=128):
        esz = 4 if dtype == F32 else 2
        n = int(np.prod(shape))
        nbytes = n * esz
        start = (self.off + 63) // 64 * 64
        assert start + nbytes <= self.arena_bytes, f"SBUF arena overflow {name} {start + nbytes}"
        self.off = start + nbytes
        self.peak = max(self.peak, self.off)
        ap = self.arena[0:parts, start:start + nbytes].bitcast(dtype)
        if len(shape) == 2:
            ap = ap.rearrange("p (a b) -> p a b", b=shape[1])
        elif len(shape) == 3:
            ap = ap.rearrange("p (a b c) -> p a b c", b=shape[1], c=shape[2])
        return T(ap, name)

    def mark(self):
        return self.off

    def release(self, mark):
        self.off = mark

    def _deps(self, idx, r, w):
        deps = set()
        for k in r:
            lw = self.last_w.get(k)
            if lw is not None:
                deps.add(lw)
        for k in w:
            lw = self.last_w.get(k)
            if lw is not None:
                deps.add(lw)
            for rd in self.readers.get(k, ()):
                deps.add(rd)
        for k in r:
            self.readers.setdefault(k, []).append(idx)
        for k in w:
            self.last_w[k] = idx
            self.readers[k] = []
        deps.discard(idx)
        return deps

    def op(self, eng, fn, r=(), w=()):
        idx = len(self.ops)
        deps = self._deps(idx, r, w)
        self.ops.append(dict(eng=eng, fn=fn, deps=deps, dma=False))
        self.last_op[eng] = idx
        return idx

    def dma(self, q, out, in_, r=(), w=()):
        if q is None:
            q = ("sp", "pool")[self.dmaq % 2]
            self.dmaq += 1
        idx = len(self.ops)
        deps = self._deps(idx, r, w)
        j = self.ndma[q]
        self.ndma[q] += 1
        self.ops.append(dict(eng=q, fn=lambda e: e.dma_start(out=out, in_=in_), deps=deps, dma=True, j=j))
        self.last_op[q] = idx
        return idx

    def barrier(self):
        lasts = {e: v for e, v in self.last_op.items() if v is not None}
        dmas = []
        for q in ENGS:
            cnt = 0
            for i in range(len(self.ops) - 1, -1, -1):
                o = self.ops[i]
                if o["dma"] and o["eng"] == q:
                    dmas.append(i)
                    cnt += 1
                    if cnt >= DMA_K:
                        break
        for e in ENGS:
            idx = len(self.ops)
            deps = set(v for ee, v in lasts.items() if ee != e and not self.ops[v]["dma"]) | set(dmas)
            self.ops.append(dict(eng=e, fn=None, deps=deps, dma=False))
            self.last_op[e] = idx
        self.last_w.clear()
        self.readers.clear()

    def emit(self, block, sems):
        ops = self.ops
        signaled = set()
        for o in ops:
            for d in o["deps"]:
                if not ops[d]["dma"]:
                    if ops[d]["eng"] == "pe" and o["eng"] == "pe":
                        continue
                    signaled.add(d)
        cnt = {e: 0 for e in ENGS}
        for i, o in enumerate(ops):
            if i in signaled:
                o["sig"] = cnt[o["eng"]]
                cnt[o["eng"]] += 1
        sem_iter = iter(sems)
        eng_sems = {}
        for e in ENGS:
            n_ep = cnt[e] // SEM_LIM + 1
            eng_sems[e] = [next(sem_iter) for _ in range(n_ep)]
        dma_sems = {}
        for q in ENGS:
            if self.ndma[q]:
                dma_sems[q] = [next(sem_iter) for _ in range(DMA_K)]
        per_eng = {e: [] for e in ENGS}
        for i, o in enumerate(ops):
            per_eng[o["eng"]].append(i)

        def run(e_name, e):
            waited = {x: -1 for x in ENGS}
            dwaited = set()
            for i in per_eng[e_name]:
                o = ops[i]
                need = {}
                for d in sorted(o["deps"]):
                    po = ops[d]
                    if po["dma"]:
                        if d not in dwaited:
                            dwaited.add(d)
                            q = po["eng"]
                            j = po["j"]
                            e.wait_ge(dma_sems[q][j % DMA_K], 16 * (j // DMA_K + 1))
                    else:
                        if po["eng"] == "pe" and e_name == "pe":
                            continue
                        s = po["sig"]
                        if s > waited[po["eng"]]:
                            need[po["eng"]] = max(need.get(po["eng"], -1), s)
                for pe_, s in need.items():
                    waited[pe_] = s
                    e.wait_ge(eng_sems[pe_][s // SEM_LIM], s % SEM_LIM + 1)
                if o["dma"]:
                    j = o["j"]
                    if j >= DMA_K:
                        e.wait_ge(dma_sems[e_name][j % DMA_K], 16 * (j // DMA_K))
                    ins = o["fn"](e)
                    ins.then_inc(dma_sems[e_name][j % DMA_K], 16)
                else:
                    if o["fn"] is None:
                        ins = e.nop() if "sig" in o else None
                    else:
                        ins = o["fn"](e)
                    if "sig" in o:
                        s = o["sig"]
                        ins.then_inc(eng_sems[e_name][s // SEM_LIM], 1)

        @block.tensor
        def _(e):
            run("pe", e)

        @block.scalar
        def _(e):
            run("act", e)

        @block.vector
        def _(e):
            run("dve", e)

        @block.gpsimd
        def _(e):
            run("pool", e)

        @block.sync
        def _(e):
            run("sp", e)


class Ctx:
    def __init__(self):
        self.nc = bass.Bass("TRN2", target_bir_lowering=False)
        self.es = ExitStack()
        self.ins = {}
        self.outs = {}

    def inp(self, name, shape, dt=F32):
        t = self.nc.dram_tensor(name, list(shape), dt, kind="ExternalInput").ap()
        self.ins[name] = t
        return t

    def out(self, name, shape, dt=F32):
        t = self.nc.dram_tensor(name, list(shape), dt, kind="ExternalOutput").ap()
        self.outs[name] = t
        return t

    def scratch(self, name, shape, dt):
        return self.nc.dram_tensor(name, list(shape), dt).ap()

    def start(self, arena_kb=204):
        nc, es = self.nc, self.es
        arena = es.enter_context(nc.sbuf_tensor("arena", [128, arena_kb * 1024], U8))
        psum = es.enter_context(nc.psum_tensor("psum", [128, 4096], F32))
        self.sems = [es.enter_context(nc.semaphore(f"s{i}")) for i in range(70)]
        self.block = es.enter_context(nc.Block())
        self.P = Prog(nc, arena[:], arena_kb * 1024)
        self.psum = psum[:]
        self.banks = [T(psum[:, i * 512:(i + 1) * 512], f"bank{i}") for i in range(8)]
        self.stage = None
        self.nstage = 0
        self.eps_t = self.P.sb("eps_t", [1], F32)
        self.P.op("pool", lambda e: e.memset(self.eps_t.ap, EPS), w=[self.eps_t])
        self.ncv = 0
        return self.P

    def finish(self):
        self.P.barrier()
        self.P.emit(self.block, self.sems)
        self.es.close()
        return self.nc

    def load_f32(self, name, src, shape, parts=128):
        t = self.P.sb(name, shape, F32)
        self.P.dma(None, t.ap[0:parts], src, w=[t])
        return t

    def load_bf16(self, name, src3, kc, ncols, parts=128, dst=None, dst_ap=None):
        P = self.P
        if self.stage is None:
            self.stage = [P.sb(f"stage{i}", [2048], F32) for i in range(self.nst)]
        if dst is None:
            dst = P.sb(name, [kc, ncols], BF16)
            dst_ap = dst.ap
        for k in range(kc):
            for c0 in range(0, ncols, 2048):
                cn = min(2048, ncols - c0)
                st = self.stage[self.nstage % self.nst]
                self.nstage += 1
                P.dma(None, st.ap[0:parts, 0:cn], src3[:, k, c0:c0 + cn], w=[st])
                eng = ("pool", "act", "dve")[self.ncv % 3] if self.cv_engs is None else self.cv_engs[self.ncv % len(self.cv_engs)]
                self.ncv += 1
                o = dst_ap[0:parts, k, c0:c0 + cn]
                i_ = st.ap[0:parts, 0:cn]
                if eng == "act":
                    P.op("act", lambda e, o=o, i_=i_: e.activation(out=o, in_=i_, func=AF.Copy), r=[st], w=[dst])
                elif eng == "pool":
                    P.op("pool", lambda e, o=o, i_=i_: e.tensor_copy(out=o, in_=i_), r=[st], w=[dst])
                else:
                    P.op("dve", lambda e, o=o, i_=i_: e.tensor_copy(out=o, in_=i_), r=[st], w=[dst])
        return dst

    cv_engs = None
    nst = 3


def wview(ap2d):
    return ap2d.rearrange("(kc p) n -> p kc n", p=128)


def emit_norm_mod(C, xs, n, ones_bf, Acol, Bcol, grp, hT, tmpA, sqbuf, bank_mean, rstd):
    P = C.P
    P.op("act", lambda e: e.activation(out=sqbuf.ap[:, :, 0:n], in_=xs.ap[:, :, 0:n], func=AF.Square, scale=1.0 / 32.0),
         r=[xs], w=[sqbuf])
    for kc in range(8):
        P.op("pe", lambda e, kc=kc: e.matmul(bank_mean.ap[:, 0:n], lhsT=ones_bf.ap[:, 0:128], rhs=sqbuf.ap[:, kc, 0:n],
                                             start=(kc == 0), stop=(kc == 7)), r=[sqbuf, ones_bf], w=[bank_mean])
    P.op("act", lambda e: e.activation(out=rstd.ap[:, 0:n], in_=bank_mean.ap[:, 0:n], func=AF.Ln, bias=C.eps_t.ap[:, 0:1]), r=[bank_mean, C.eps_t], w=[rstd])
    P.op("act", lambda e: e.activation(out=rstd.ap[:, 0:n], in_=rstd.ap[:, 0:n], func=AF.Exp, scale=-0.5), r=[rstd], w=[rstd])
    for kc in range(8):
        tA = tmpA[kc % len(tmpA)]
        P.op("dve", lambda e, kc=kc, tA=tA: e.scalar_tensor_tensor(out=tA.ap[:, 0:n], in0=xs.ap[:, kc, 0:n],
                                                                   scalar=Acol.ap[:, kc, grp:grp + 1], in1=rstd.ap[:, 0:n],
                                                                   op0=ALU.mult, op1=ALU.mult), r=[xs, Acol, rstd], w=[tA])
        P.op("act", lambda e, kc=kc, tA=tA: e.activation(out=hT.ap[:, kc, 0:n], in_=tA.ap[:, 0:n], func=AF.Identity,
                                                         bias=Bcol.ap[:, kc, grp:grp + 1]), r=[tA, Bcol], w=[hT])


def build_pre():
    C = Ctx()
    xT = C.inp("xT", [D, TT])
    c2 = C.inp("c2", [128, 8, 2])
    w_mod = C.inp("w_mod", [D, 6 * D])
    b_modc = C.inp("b_modc", [128, 48])
    gn1c = C.inp("gn1c", [128, 8])
    gn2c = C.inp("gn2c", [128, 8])
    w_in = C.inp("w_in", [D, IN_COLS])
    gqac = C.inp("gqac", [128, 3])
    gkvac = C.inp("gkvac", [128, 2])
    w_q_b = C.inp("w_q_b", [384, 768])
    w_kv_b = C.inp("w_kv_b", [256, 1024])
    hgain = C.inp("hgain", [96, 4])
    ropeM = C.inp("ropeM", [96, 2, TT])
    ropeG = C.inp("ropeG", [96, 2, TT])
    rotm = C.inp("rotm", [96, 2, 96], BF16)
    shiftm = C.inp("shiftm", [32, 96], BF16)
    modv_o = C.out("modv", [128, 6, 8, 2])
    hT_o = C.out("hT", [D, TT], BF16)
    qT_o = C.out("qT", [NQH, 96, TT], BF16)
    kT_o = C.out("kT", [NKH, 96, TT], BF16)
    vP_o = C.out("vP", [TT, NKH, 65], BF16)
    y_o = C.out("y", [512, TT])
    P = C.start()
    eps_t = C.eps_t
    C.cv_engs = ("pool", "act")
    C.nst = 2
    banks = C.banks

    ones_bf = P.sb("ones_bf", [128], BF16)
    P.op("pool", lambda e: e.memset(ones_bf.ap, 1.0), w=[ones_bf])
    rot_t = P.sb("rot_t", [2, 96], BF16)
    P.dma(None, rot_t.ap[0:96], rotm, w=[rot_t])
    shift_t = P.sb("shift_t", [96], BF16)
    P.dma(None, shift_t.ap[0:32], shiftm, w=[shift_t])
    hgain_t = C.load_f32("hgain_t", hgain, [4], parts=96)
    gqa_t = C.load_f32("gqa_t", gqac, [3])
    gkva_t = C.load_f32("gkva_t", gkvac, [2])
    gn1_t = C.load_f32("gn1_t", gn1c, [8])
    gn2_t = C.load_f32("gn2_t", gn2c, [8])
    bmod_t = C.load_f32("bmod_t", b_modc, [48])

    c2_t = C.load_f32("c2_t", c2, [8, 2])
    sc_t = P.sb("sc_t", [8, 2], F32)
    P.op("act", lambda e: e.activation(out=sc_t.ap, in_=c2_t.ap, func=AF.Silu), r=[c2_t], w=[sc_t])
    modv = P.sb("modv", [6, 8, 2], F32)
    m0 = P.mark()
    wm = [P.sb(f"wm{i}", [8, 512], F32) for i in range(2)]
    wmv = wview(w_mod)
    for g in range(12):
        wt = wm[g % 2]
        P.dma(None, wt.ap, wmv[:, :, g * 512:(g + 1) * 512], w=[wt])
        for s in range(4):
            j = g * 4 + s
            bk = banks[j % 2]
            for kc in range(8):
                P.op("pe", lambda e, bk=bk, wt=wt, kc=kc, s=s: e.matmul(bk.ap[:, 0:2], lhsT=wt.ap[:, kc, s * 128:(s + 1) * 128],
                                                                     rhs=sc_t.ap[:, kc, :], start=(kc == 0), stop=(kc == 7)),
                     r=[wt, sc_t], w=[bk])
            P.op("dve", lambda e, bk=bk, j=j: e.tensor_scalar(out=modv.ap[:, j // 8, j % 8, :], in0=bk.ap[:, 0:2],
                                                            scalar1=bmod_t.ap[:, j:j + 1], scalar2=None, op0=ALU.add),
                 r=[bk, bmod_t], w=[modv])
    P.barrier()
    P.release(m0)
    modc = P.sb("modc", [6, 8, 2], F32)
    for (dst, src, gn) in ((0, 1, gn1_t), (3, 4, gn2_t)):
        for cnd in range(2):
            P.op("dve", lambda e, dst=dst, src=src, gn=gn, cnd=cnd: e.scalar_tensor_tensor(
                out=modc.ap[:, dst, :, cnd], in0=modv.ap[:, src, :, cnd], scalar=1.0, in1=gn.ap,
                op0=ALU.add, op1=ALU.mult), r=[modv, gn], w=[modc])
    for (dst, src) in ((1, 0), (2, 2), (4, 3), (5, 5)):
        P.op("dve", lambda e, dst=dst, src=src: e.tensor_copy(out=modc.ap[:, dst], in_=modv.ap[:, src]), r=[modv], w=[modc])
    P.dma(None, modv_o, modc.ap, r=[modc])
    import os
    STOP = int(os.environ.get('PRE_STOP', '9'))
    if STOP <= 1:
        return C.finish(), C
    Acol = T(modc.ap[:, 0], "A1")
    Bcol = T(modc.ap[:, 1], "B1")

    wiv = wview(w_in)
    w_a = C.load_bf16("w_a", wiv[:, :, 0:C_GATE], 8, C_GATE)
    wqb = C.load_bf16("wqb", wview(w_q_b), 3, 768)
    wkb = P.sb("wkb", [2, 8, 96], BF16)
    P.op("pool", lambda e: e.memset(wkb.ap, 0.0), w=[wkb])
    wkv = P.sb("wkv", [2, 8, 64], BF16)
    wkbv = wview(w_kv_b).rearrange("p kc (h t) -> p kc h t", t=128)
    stg = C.stage
    for kc in range(2):
        st = stg[C.nstage % C.nst]
        C.nstage += 1
        P.dma(None, st.ap[:, 0:1024], wview(w_kv_b)[:, kc, :], w=[st])
        sv = st.ap[:, 0:1024].rearrange("p (h t) -> p h t", t=128)
        P.op("pool", lambda e, kc=kc, sv=sv: e.tensor_copy(out=wkb.ap[:, kc, :, 0:64], in_=sv[:, :, 0:64]), r=[st], w=[wkb])
        P.op("pool", lambda e, kc=kc, sv=sv: e.tensor_copy(out=wkv.ap[:, kc, :, :], in_=sv[:, :, 64:128]), r=[st], w=[wkv])

    if STOP <= 2:
        return C.finish(), C
    xs = P.sb("xs", [8, 512], F32)
    sqb = P.sb("sqb", [8, 512], BF16)
    tmpA = [P.sb(f"tmpA{i}", [512], F32) for i in range(3)]
    hT = P.sb("hT", [8, 512], BF16)
    rstd = P.sb("rstd", [512], F32)
    sg = [P.sb(f"sg{i}", [512], F32) for i in range(2)]
    yt = [P.sb(f"yt{i}", [512], F32) for i in range(2)]
    zq = P.sb("zq", [3, 512], F32)
    sqa = P.sb("sqa", [3, 512], BF16)
    qan = P.sb("qan", [3, 512], BF16)
    zkv = P.sb("zkv", [2, 512], F32)
    sqkv = P.sb("sqkv", [2, 512], BF16)
    kvn = P.sb("kvn", [2, 512], BF16)
    zkr = P.sb("zkr", [512], BF16)
    rstd2 = P.sb("rstd2", [512], F32)
    ropeM_t = P.sb("ropeM_t", [2, 512], F32)
    ropeG_t = P.sb("ropeG_t", [2, 512], F32)
    vt = [P.sb(f"vt{i}", [NKH, 65], BF16) for i in range(2)]
    for v_ in vt:
        P.op("pool", lambda e, v_=v_: e.memset(v_.ap, 1.0), w=[v_])
    NU = 4
    u_sq = [P.sb(f"u_sq{i}", [512], BF16) for i in range(NU)]
    u_zc = [P.sb(f"u_zc{i}", [512], F32) for i in range(NU)]
    u_rs = [P.sb(f"u_rs{i}", [512], F32) for i in range(NU)]
    u_xn = [P.sb(f"u_xn{i}", [512], BF16) for i in range(NU)]
    u_t1 = [P.sb(f"u_t1{i}", [512], F32) for i in range(NU)]
    u_t2 = [P.sb(f"u_t2{i}", [512], F32) for i in range(NU)]
    u_o = [P.sb(f"u_o{i}", [512], BF16) for i in range(NU)]
    u_og = [P.sb(f"u_og{i}", [512], BF16) for i in range(NU)]
    for o_ in u_og:
        P.op("pool", lambda e, o_=o_: e.memset(o_.ap, 0.0), w=[o_])
    ucnt = [0]

    def inproj(bank, col0, m, n):
        for kc in range(8):
            P.op("pe", lambda e, kc=kc: e.matmul(bank.ap[0:m, 0:n], lhsT=w_a.ap[:, kc, col0:col0 + m], rhs=hT.ap[:, kc, 0:n],
                                                 start=(kc == 0), stop=(kc == 7)), r=[w_a, hT], w=[bank])

    def normrope(main_fn, d, gcol, rope_t, ridx, out_dram, n):
        pending.append((main_fn, d, gcol, rope_t, ridx, out_dram, n))

    def flush_units():
        units = list(pending)
        del pending[:]
        st = []
        for (main_fn, d, gcol, rope_t, ridx, out_dram, n) in units:
            u = ucnt[0] % NU
            ucnt[0] += 1
            st.append(dict(main_fn=main_fn, d=d, gcol=gcol, rope_t=rope_t, ridx=ridx, out_dram=out_dram, n=n, bA=banks[2 * u], bB=banks[2 * u + 1],
                           sq=u_sq[u], zc=u_zc[u], rs=u_rs[u], xn=u_xn[u], t1=u_t1[u], t2=u_t2[u], o=(u_o[u] if d == 96 else u_og[u])))

        def stage_m(x):
            x["main_fn"](x["bA"])

        def stage_a(x):
            d, n, bA, bB, sq, zc = x["d"], x["n"], x["bA"], x["bB"], x["sq"], x["zc"]
            P.op("act", lambda e: e.activation(out=sq.ap[0:d, 0:n], in_=bA.ap[0:d, 0:n], func=AF.Square, scale=float(d) ** -0.5),
                 r=[bA], w=[sq])
            P.op("act", lambda e: e.activation(out=zc.ap[0:d, 0:n], in_=bA.ap[0:d, 0:n], func=AF.Copy), r=[bA], w=[zc])
            P.op("pe", lambda e: e.matmul(bB.ap[0:96, 0:n], lhsT=ones_bf.ap[0:d, 0:96], rhs=sq.ap[0:d, 0:n], start=True, stop=True),
                 r=[sq, ones_bf], w=[bB])

        def stage_b(x):
            d, n, bA, bB, zc, rs_, xn = x["d"], x["n"], x["bA"], x["bB"], x["zc"], x["rs"], x["xn"]
            gcol, ridx = x["gcol"], x["ridx"]
            P.op("act", lambda e: e.activation(out=rs_.ap[0:d, 0:n], in_=bB.ap[0:d, 0:n], func=AF.Ln, bias=eps_t.ap[0:d, 0:1]), r=[bB, eps_t], w=[rs_])
            P.op("act", lambda e: e.activation(out=rs_.ap[0:d, 0:n], in_=rs_.ap[0:d, 0:n], func=AF.Exp, scale=-0.5), r=[rs_], w=[rs_])
            P.op("dve", lambda e: e.scalar_tensor_tensor(out=xn.ap[0:d, 0:n], in0=zc.ap[0:d, 0:n], scalar=hgain_t.ap[0:d, gcol:gcol + 1],
                                                         in1=rs_.ap[0:d, 0:n], op0=ALU.mult, op1=ALU.mult),
                 r=[zc, rs_, hgain_t], w=[xn])
            P.op("pe", lambda e: e.matmul(bA.ap[0:96, 0:n], lhsT=rot_t.ap[0:d, ridx, 0:96], rhs=xn.ap[0:d, 0:n], start=True, stop=True),
                 r=[xn, rot_t], w=[bA])

        def stage_c(x):
            d, n, bA, xn, t1, t2, o_, rope_t = x["d"], x["n"], x["bA"], x["xn"], x["t1"], x["t2"], x["o"], x["rope_t"]
            P.op("pool", lambda e: e.tensor_tensor(out=t1.ap[0:d, 0:n], in0=xn.ap[0:d, 0:n], in1=rope_t.ap[0:d, 0, 0:n], op=ALU.mult),
                 r=[xn, rope_t], w=[t1])
            P.op("dve", lambda e: e.tensor_tensor(out=t2.ap[0:d, 0:n], in0=bA.ap[0:d, 0:n], in1=rope_t.ap[0:d, 1, 0:n], op=ALU.mult),
                 r=[bA, rope_t], w=[t2])
            P.op("pool", lambda e: e.tensor_tensor(out=o_.ap[0:d, 0:n], in0=t1.ap[0:d, 0:n], in1=t2.ap[0:d, 0:n], op=ALU.add),
                 r=[t1, t2], w=[o_])
            P.dma(None, x["out_dram"], o_.ap[0:96, 0:n], r=[o_])

        nU = len(st)
        for i in range(-1, nU + 2):
            if 0 <= i + 1 < nU:
                stage_m(st[i + 1])
            if 0 <= i < nU:
                stage_a(st[i])
            if 0 <= i - 1 < nU:
                stage_b(st[i - 1])
            if 0 <= i - 2 < nU:
                stage_c(st[i - 2])

    pending = []

    def do_block(t0, n, grp):
        P.dma("sp", xs.ap[:, :, 0:n], wview(xT)[:, :, t0:t0 + n], w=[xs])
        P.dma("pool", ropeM_t.ap[0:96, :, 0:n], ropeM[:, :, t0:t0 + n], w=[ropeM_t])
        P.dma("pool", ropeG_t.ap[0:96, :, 0:n], ropeG[:, :, t0:t0 + n], w=[ropeG_t])
        emit_norm_mod(C, xs, n, ones_bf, Acol, Bcol, grp, hT, tmpA, sqb, banks[6], rstd)
        P.dma(None, wview(hT_o)[:, :, t0:t0 + n], hT.ap[:, :, 0:n], r=[hT])
        if STOP <= 3:
            return
        for j in range(4):
            bg, ba = banks[6], banks[7]
            inproj(bg, C_GLU + 512 + j * 128, 128, n)
            s_ = sg[j % 2]
            P.op("act", lambda e, s_=s_, bg=bg: e.activation(out=s_.ap[:, 0:n], in_=bg.ap[:, 0:n], func=AF.Sigmoid), r=[bg], w=[s_])
            inproj(ba, C_GLU + j * 128, 128, n)
            y_ = yt[j % 2]
            P.op("dve", lambda e, y_=y_, ba=ba, s_=s_: e.tensor_tensor(out=y_.ap[:, 0:n], in0=ba.ap[:, 0:n], in1=s_.ap[:, 0:n], op=ALU.mult),
                 r=[ba, s_], w=[y_])
            P.dma(None, y_o[j * 128:(j + 1) * 128, t0:t0 + n], y_.ap[:, 0:n], r=[y_])
        if STOP <= 4:
            return
        for (zt, sqt, outn, col0, nch, dim, gt) in ((zq, sqa, qan, C_QA, 3, 384, gqa_t), (zkv, sqkv, kvn, C_KVA, 2, 256, gkva_t)):
            for j in range(nch):
                bk = banks[6 + (j % 2)]
                inproj(bk, col0 + j * 128, 128, n)
                P.op("act", lambda e, bk=bk, j=j, sqt=sqt, dim=dim: e.activation(out=sqt.ap[:, j, 0:n], in_=bk.ap[:, 0:n], func=AF.Square,
                                                                             scale=float(dim) ** -0.5), r=[bk], w=[sqt])
                P.op("act", lambda e, bk=bk, j=j, zt=zt: e.activation(out=zt.ap[:, j, 0:n], in_=bk.ap[:, 0:n], func=AF.Copy), r=[bk], w=[zt])
            bk = banks[6]
            for j in range(nch):
                P.op("pe", lambda e, j=j, sqt=sqt, bk=bk, nch=nch: e.matmul(bk.ap[:, 0:n], lhsT=ones_bf.ap[:, 0:128], rhs=sqt.ap[:, j, 0:n],
                                                                        start=(j == 0), stop=(j == nch - 1)), r=[sqt, ones_bf], w=[bk])
            P.op("act", lambda e, bk=bk: e.activation(out=rstd2.ap[:, 0:n], in_=bk.ap[:, 0:n], func=AF.Ln, bias=eps_t.ap[:, 0:1]), r=[bk, eps_t], w=[rstd2])
            P.op("act", lambda e, bk=bk: e.activation(out=rstd2.ap[:, 0:n], in_=rstd2.ap[:, 0:n], func=AF.Exp, scale=-0.5), r=[rstd2], w=[rstd2])
            for j in range(nch):
                P.op("dve", lambda e, j=j, zt=zt, outn=outn, gt=gt: e.scalar_tensor_tensor(
                    out=outn.ap[:, j, 0:n], in0=zt.ap[:, j, 0:n], scalar=gt.ap[:, j:j + 1], in1=rstd2.ap[:, 0:n],
                    op0=ALU.mult, op1=ALU.mult), r=[zt, gt, rstd2], w=[outn])
        if STOP <= 5:
            return
        bk = banks[7]
        inproj(bk, C_KR, 32, n)
        P.op("act", lambda e, bk=bk: e.activation(out=zkr.ap[0:32, 0:n], in_=bk.ap[0:32, 0:n], func=AF.Copy), r=[bk], w=[zkr])
        for h in range(8):
            def mq(bank, h=h):
                for j in range(3):
                    P.op("pe", lambda e, j=j: e.matmul(bank.ap[0:96, 0:n], lhsT=wqb.ap[:, j, h * 96:(h + 1) * 96], rhs=qan.ap[:, j, 0:n],
                                                       start=(j == 0), stop=(j == 2)), r=[wqb, qan], w=[bank])
            normrope(mq, 96, 0, ropeM_t, 0, qT_o[h, :, t0:t0 + n], n)
        if STOP <= 6:
            flush_units()
            return
        for h in range(8):
            def mk(bank, h=h):
                for j in range(2):
                    P.op("pe", lambda e, j=j: e.matmul(bank.ap[0:96, 0:n], lhsT=wkb.ap[:, j, h, :], rhs=kvn.ap[:, j, 0:n],
                                                       start=(j == 0), stop=False), r=[wkb, kvn], w=[bank])
                P.op("pe", lambda e: e.matmul(bank.ap[0:96, 0:n], lhsT=shift_t.ap[0:32, 0:96], rhs=zkr.ap[0:32, 0:n],
                                              start=False, stop=True), r=[shift_t, zkr], w=[bank])
            normrope(mk, 96, 1, ropeM_t, 0, kT_o[h, :, t0:t0 + n], n)
        if STOP <= 7:
            flush_units()
            return
        for h in range(8):
            def gq(bank, h=h):
                inproj(bank, C_GQ + h * 64, 64, n)
            normrope(gq, 64, 2, ropeG_t, 1, qT_o[8 + h, :, t0:t0 + n], n)
        for h in range(2):
            def gk(bank, h=h):
                inproj(bank, C_GK + h * 64, 64, n)
            normrope(gk, 64, 3, ropeG_t, 1, kT_o[8 + h, :, t0:t0 + n], n)
        flush_units()
        if STOP <= 8:
            return
        for ti in range(n // 128):
            v_ = vt[ti % 2]
            bm, bg = banks[6], banks[7]
            for j in range(2):
                P.op("pe", lambda e, j=j, ti=ti, bm=bm: e.matmul(bm.ap[:, 0:512], lhsT=kvn.ap[:, j, ti * 128:(ti + 1) * 128],
                                                             rhs=wkv.ap[:, j].rearrange("p h d -> p (h d)"),
                                                             start=(j == 0), stop=(j == 1)), r=[kvn, wkv], w=[bm])
            for kc in range(8):
                P.op("pe", lambda e, kc=kc, ti=ti, bg=bg: e.matmul(bg.ap[:, 0:128], lhsT=hT.ap[:, kc, ti * 128:(ti + 1) * 128],
                                                               rhs=w_a.ap[:, kc, C_GV:C_GV + 128],
                                                               start=(kc == 0), stop=(kc == 7)), r=[hT, w_a], w=[bg])
            P.op("act", lambda e, v_=v_, bm=bm: e.activation(out=v_.ap[:, 0:8, 0:64], in_=bm.ap[:, 0:512].rearrange("p (h d) -> p h d", d=64),
                                                            func=AF.Copy), r=[bm], w=[v_])
            P.op("act", lambda e, v_=v_, bg=bg: e.activation(out=v_.ap[:, 8:10, 0:64], in_=bg.ap[:, 0:128].rearrange("p (h d) -> p h d", d=64),
                                                            func=AF.Copy), r=[bg], w=[v_])
            P.dma(None, vP_o[t0 + ti * 128:t0 + (ti + 1) * 128], v_.ap, r=[v_])
    for (t0_, n_, grp_) in BLOCKS:
        do_block(t0_, n_, grp_)
    return C.finish(), C


KB = 1024


def attention_phase(C, heads, NQ, key_blocks, ones_f):
    P = C.P
    banks = C.banks
    psum_all = C.psum
    m0 = P.mark()
    NBUF = 4
    HW = min(1024, NQ)
    assert NQ == HW
    SW = min(512, HW)
    NS = HW // SW
    nob = NS
    kbuf = [P.sb(f"kbuf{i}", [KB], BF16) for i in range(NBUF)]
    vbuf = [P.sb(f"vbuf{i}", [KB // 128, 65], BF16) for i in range(NBUF)]
    qbuf = [P.sb(f"qbuf{i}", [NQ], BF16) for i in range(2)]
    NPT = 4
    pT = [P.sb(f"pT{i}", [HW], BF16) for i in range(NPT)]
    rs = P.sb("rs", [NQ], F32)
    bcs = P.sb("bcs", [NQ], F32)
    obuf = [P.sb(f"obuf{i}", [NQ], BF16) for i in range(2)]
    Ob = banks[0:nob]
    Sb = []
    bi_ = nob
    while bi_ + NS <= 8:
        Sb.append(list(range(bi_, bi_ + NS)))
        bi_ += NS
    Sb = Sb[:3]
    LA = len(Sb) - 1
    nblk_total = 0
    sidx = [0]
    pidx = [0]
    dk = 96
    for hi, hd in enumerate(heads):
        qb = qbuf[hi % 2]
        P.dma("sp", qb.ap[0:dk, :], hd["q"], w=[qb])
        nchunks_total = sum(n // 128 for _, n in key_blocks)
        ci = 0
        its = []
        binfo = []
        for (k0, kn) in key_blocks:
            kb_ = kbuf[nblk_total % NBUF]
            vb_ = vbuf[nblk_total % NBUF]
            nblk_total += 1
            nch = kn // 128
            binfo.append(dict(k0=k0, kn=kn, kb=kb_, vb=vb_, nch=nch, loaded=False))
            for c in range(nch):
                its.append(dict(kb=kb_, vb=vb_, c=c, kn=kn, nch=nch, first=(ci == 0), last=(ci == nchunks_total - 1), blk=len(binfo) - 1))
                ci += 1

        def load_upto(nb):
            for b in binfo[:nb + 1]:
                if not b["loaded"]:
                    b["loaded"] = True
                    k0, kn, nch = b["k0"], b["kn"], b["nch"]
                    P.dma("sp", b["kb"].ap[0:dk, 0:kn], hd["k"][:, k0:k0 + kn], w=[b["kb"]])
                    P.dma("pool", b["vb"].ap[:, 0:nch, :], hd["v"][k0:k0 + kn, :].rearrange("(p c) e -> p c e", c=nch), w=[b["vb"]])

        def emit_S(it):
            load_upto(it["blk"] + 2)
            sbi = Sb[sidx[0] % len(Sb)]
            sidx[0] += 1
            it["sbi"] = sbi
            for j in range(NS):
                bk = banks[sbi[j]]
                P.op("pe", lambda e, o=bk.ap[:, 0:SW], l=it["kb"].ap[0:dk, it["c"]:it["kn"]:it["nch"]], r=qb.ap[0:dk, j * SW:(j + 1) * SW]:
                     e.matmul(o, lhsT=l, rhs=r, start=True, stop=True), r=[it["kb"], qb], w=[bk])

        def emit_exp(it):
            pt = pT[pidx[0] % NPT]
            pidx[0] += 1
            it["pt"] = pt
            sbi = it["sbi"]
            s_ap = psum_all[:, sbi[0] * 512:sbi[0] * 512 + (1024 if NS == 2 else SW)]
            P.op("act", lambda e, o=pt.ap[:, 0:HW], i=s_ap, sc=hd["scale"]: e.activation(out=o, in_=i, func=AF.Exp, scale=sc),
                 r=[banks[x] for x in sbi], w=[pt])

        def emit_PV(it):
            for j in range(NS):
                ob = Ob[j]
                P.op("pe", lambda e, o=ob.ap[0:65, 0:SW], l=it["vb"].ap[:, it["c"], :], r=it["pt"].ap[:, j * SW:(j + 1) * SW],
                     f=it["first"], la=it["last"]: e.matmul(o, lhsT=l, rhs=r, start=f, stop=la), r=[it["vb"], it["pt"]], w=[ob])

        n_it = len(its)
        for i in range(min(LA, n_it)):
            emit_S(its[i])
        emit_exp(its[0])
        for i in range(n_it):
            if i + LA < n_it:
                emit_S(its[i + LA])
            if i + 1 < n_it:
                emit_exp(its[i + 1])
            emit_PV(its[i])
        for j in range(nob):
            P.op("dve", lambda e, j=j: e.reciprocal(out=rs.ap[64:65, j * SW:(j + 1) * SW], in_=Ob[j].ap[64:65, 0:SW]), r=[Ob[j]], w=[rs])
        for j in range(nob):
            sbk = banks[nob + j]
            P.op("pe", lambda e, sbk=sbk, j=j: e.matmul(sbk.ap[0:64, 0:SW], lhsT=ones_f.ap[64:65, 0:64], rhs=rs.ap[64:65, j * SW:(j + 1) * SW],
                                                    start=True, stop=True), r=[rs, ones_f], w=[sbk])
            P.op("act", lambda e, sbk=sbk, j=j: e.activation(out=bcs.ap[0:64, j * SW:(j + 1) * SW], in_=sbk.ap[0:64, 0:SW], func=AF.Copy),
                 r=[sbk], w=[bcs])
        ob_ = obuf[hi % 2]
        for j in range(nob):
            P.op("dve", lambda e, j=j, ob_=ob_: e.tensor_tensor(out=ob_.ap[0:64, j * SW:(j + 1) * SW], in0=Ob[j].ap[0:64, 0:SW],
                                                               in1=bcs.ap[0:64, j * SW:(j + 1) * SW], op=ALU.mult),
                 r=[Ob[j], bcs], w=[ob_])
        P.dma("sp", hd["o"], ob_.ap[0:64, :], r=[ob_])
    P.release(m0)


def build_attn():
    C = Ctx()
    qT = C.inp("qT", [NQH, 96, TT], BF16)
    kTf = C.inp("kTf", [NKH, 96, NK], BF16)
    vPf = C.inp("vPf", [NKH, NK, 65], BF16)
    hT_d = C.inp("hT", [D, TT], BF16)
    xT = C.inp("xT", [D, TT])
    yh = C.inp("yh", [512, TL + 30])
    yc = C.inp("yc", [512, CTX + 30])
    modv = C.inp("modv", [128, 6, 8, 2])
    w_in = C.inp("w_in", [D, IN_COLS])
    bgc = C.inp("bgc", [128, 24])
    cwc = C.inp("cwc", [128, 4, 31])
    cvec = C.inp("cvec", [128, 4, 3])
    w_conv_out = C.inp("w_conv_out", [512, D])
    w_mla_o = C.inp("w_mla_o", [512, D])
    w_gqa_o = C.inp("w_gqa_o", [512, D])
    w_out = C.inp("w_out", [D, D])
    xm_o = C.out("xmT", [D, TT])
    h2_o = C.out("h2T", [D, TT], BF16)
    oT_d = C.scratch("oT_d", [NQH * 64, TT], BF16)
    P = C.start()
    eps_t = C.eps_t
    banks = C.banks
    ones_f = P.sb("ones_f", [64], F32)
    P.op("pool", lambda e: e.memset(ones_f.ap, 1.0), w=[ones_f])
    lat_blocks = [(0, CTX)] + [(CTX + b * KB, KB) for b in range(SEQ // KB)]
    heads_l, heads_c = [], []
    for h in range(NQH):
        kvh = h if h < 8 else 8 + (h - 8) // 4
        sc = MLA_SCALE if h < 8 else GQA_SCALE
        for qh in range(TL // 1024):
            heads_l.append(dict(q=qT[h, :, qh * 1024:(qh + 1) * 1024], k=kTf[kvh], v=vPf[kvh],
                                o=oT_d[h * 64:(h + 1) * 64, qh * 1024:(qh + 1) * 1024], scale=sc))
        heads_c.append(dict(q=qT[h, :, TL:TT], k=kTf[kvh], v=vPf[kvh], o=oT_d[h * 64:(h + 1) * 64, TL:TT], scale=sc))
    attention_phase(C, heads_c, CTX, [(0, CTX)], ones_f)
    attention_phase(C, heads_l, 1024, lat_blocks, ones_f)
    P.barrier()
    C.cv_engs = ("pool", "act", "dve")
    C.nst = 2
    NB = 256
    ones_bf = P.sb("ones_bf", [128], BF16)
    P.op("pool", lambda e: e.memset(ones_bf.ap, 1.0), w=[ones_bf])
    ones512 = P.sb("ones512", [128], BF16)
    P.op("pool", lambda e: e.memset(ones512.ap, 1.0 / 512.0), w=[ones512])
    modc = C.load_f32("modc", modv, [6, 8, 2])
    G1 = T(modc.ap[:, 2], "G1")
    A2 = T(modc.ap[:, 3], "A2")
    B2 = T(modc.ap[:, 4], "B2")
    bg_t = C.load_f32("bg_t", bgc, [24])
    cw_t = C.load_f32("cw_t", cwc, [4, 31])
    cv_t = C.load_f32("cv_t", cvec, [4, 3])
    wiv = wview(w_in)
    w_g = C.load_bf16("w_g", wiv[:, :, C_GATE:IN_COLS], 8, 3072)
    w_co = C.load_bf16("w_co", wview(w_conv_out), 4, D)
    w_mo = C.load_bf16("w_mo", wview(w_mla_o), 4, D)
    w_go = C.load_bf16("w_go", wview(w_gqa_o), 4, D)
    w_o = C.load_bf16("w_o", wview(w_out), 8, D)
    xs = P.sb("xs", [8, NB], F32)
    hT = P.sb("hT", [8, NB], BF16)
    om = P.sb("om", [4, NB], BF16)
    og = P.sb("og", [4, NB], BF16)
    FB = []
    for i in range(2):
        FB.append(dict(ytile=P.sb(f"ytile{i}", [4, NB + 30], F32), scrF=P.sb(f"scrF{i}", [8, NB], F32), scrB=P.sb(f"scrB{i}", [8, NB], BF16),
                       mean_s=P.sb(f"mean_s{i}", [NB], F32), var_s=P.sb(f"var_s{i}", [NB], F32), rstd_c=P.sb(f"rstd_c{i}", [NB], F32),
                       ys=P.sb(f"ys{i}", [4, NB], BF16)))
    gate2 = [[P.sb(f"gate{p}_{i}", [NB], F32) for i in range(3)] for p in range(2)]
    ring = [0]
    mtmp = [P.sb(f"mtmp{i}", [NB], F32) for i in range(2)]
    macc = P.sb("macc", [NB], F32)
    merged = P.sb("merged", [8, NB], BF16)
    xm = P.sb("xm", [8, NB], F32)
    h2 = P.sb("h2", [8, NB], BF16)
    rstd = P.sb("rstd", [NB], F32)
    sqb2 = P.sb("sqb2", [8, NB], BF16)
    ntmp = [P.sb(f"ntmp{i}", [NB], F32) for i in range(3)]
    oT3 = oT_d.rearrange("(b j p) t -> b p j t", b=2, p=128)
    MBLOCKS = [(i * NB, NB, 0) for i in range(TL // NB)] + [(TL, CTX, 1)]

    def front_pieces(bi):
        t0, n, grp = MBLOCKS[bi]
        fb = FB[bi % 2]
        ytile, scrF, scrB, mean_s, var_s, rstd_c, ys = fb["ytile"], fb["scrF"], fb["scrB"], fb["mean_s"], fb["var_s"], fb["rstd_c"], fb["ys"]
        pieces = []

        def load():
            ysrc = yh if grp == 0 else yc
            y0 = t0 if grp == 0 else 0
            P.dma("sp", ytile.ap[:, :, 0:n + 30], wview(ysrc)[:, :, y0:y0 + n + 30], w=[ytile])

        def conv_part(ch, k0, k1):
            def f():
                if ch == 0 and k0 == 0:
                    load()
                for k in range(k0, k1):
                    if k == 0:
                        P.op("dve", lambda e: e.tensor_scalar(out=scrF.ap[:, ch, 0:n], in0=ytile.ap[:, ch, 0:n], scalar1=cw_t.ap[:, ch, 0:1],
                                                              scalar2=cv_t.ap[:, ch, 0:1], op0=ALU.mult, op1=ALU.add),
                             r=[ytile, cw_t, cv_t], w=[scrF])
                    else:
                        P.op("dve", lambda e, k=k: e.scalar_tensor_tensor(out=scrF.ap[:, ch, 0:n], in0=ytile.ap[:, ch, k:k + n],
                                                                          scalar=cw_t.ap[:, ch, k:k + 1], in1=scrF.ap[:, ch, 0:n],
                                                                          op0=ALU.mult, op1=ALU.add), r=[ytile, cw_t, scrF], w=[scrF])
            return f
        for ch in range(4):
            pieces.append(conv_part(ch, 0, 16))
            pieces.append(conv_part(ch, 16, 31))

        def ln():
            P.op("act", lambda e: e.activation(out=scrB.ap[:, 0:4, 0:n], in_=scrF.ap[:, 0:4, 0:n], func=AF.Copy), r=[scrF], w=[scrB])
            P.op("act", lambda e: e.activation(out=scrB.ap[:, 4:8, 0:n], in_=scrF.ap[:, 0:4, 0:n], func=AF.Square, scale=512.0 ** -0.5),
                 r=[scrF], w=[scrB])
            bm, bv = banks[0], banks[1]
            for ch in range(4):
                P.op("pe", lambda e, ch=ch: e.matmul(bm.ap[:, 0:n], lhsT=ones512.ap[:, 0:128], rhs=scrB.ap[:, ch, 0:n], start=(ch == 0), stop=(ch == 3)),
                     r=[scrB, ones512], w=[bm])
            for ch in range(4):
                P.op("pe", lambda e, ch=ch: e.matmul(bv.ap[:, 0:n], lhsT=ones_bf.ap[:, 0:128], rhs=scrB.ap[:, 4 + ch, 0:n], start=(ch == 0), stop=(ch == 3)),
                     r=[scrB, ones_bf], w=[bv])
            P.op("act", lambda e: e.activation(out=mean_s.ap[:, 0:n], in_=bm.ap[:, 0:n], func=AF.Copy), r=[bm], w=[mean_s])
            P.op("dve", lambda e: e.tensor_tensor(out=var_s.ap[:, 0:n], in0=mean_s.ap[:, 0:n], in1=mean_s.ap[:, 0:n], op=ALU.mult), r=[mean_s], w=[var_s])
            P.op("dve", lambda e: e.tensor_tensor(out=var_s.ap[:, 0:n], in0=bv.ap[:, 0:n], in1=var_s.ap[:, 0:n], op=ALU.subtract), r=[bv, var_s], w=[var_s])
            P.op("act", lambda e: e.activation(out=rstd_c.ap[:, 0:n], in_=var_s.ap[:, 0:n], func=AF.Ln, bias=eps_t.ap[:, 0:1]), r=[var_s, eps_t], w=[rstd_c])
            P.op("act", lambda e: e.activation(out=rstd_c.ap[:, 0:n], in_=rstd_c.ap[:, 0:n], func=AF.Exp, scale=-0.5), r=[rstd_c], w=[rstd_c])
            for ch in range(4):
                P.op("dve", lambda e, ch=ch: e.tensor_tensor(out=scrF.ap[:, 4 + ch, 0:n], in0=scrF.ap[:, ch, 0:n], in1=mean_s.ap[:, 0:n], op=ALU.subtract),
                     r=[scrF, mean_s], w=[scrF])
                P.op("dve", lambda e, ch=ch: e.tensor_tensor(out=scrF.ap[:, 4 + ch, 0:n], in0=scrF.ap[:, 4 + ch, 0:n], in1=rstd_c.ap[:, 0:n], op=ALU.mult),
                     r=[scrF, rstd_c], w=[scrF])
                P.op("act", lambda e, ch=ch: e.activation(out=ys.ap[:, ch, 0:n], in_=scrF.ap[:, 4 + ch, 0:n], func=AF.Silu,
                                                          scale=cv_t.ap[:, ch, 1:2], bias=cv_t.ap[:, ch, 2:3]), r=[scrF, cv_t], w=[ys])
        pieces.append(ln)
        return pieces

    st0, st1 = C.stage[0], C.stage[1]
    st1b = st1.ap.bitcast(BF16)
    BS = [dict(xs_k=xs, in_k=hT, xs=xs.ap, hT=hT.ap, om=om.ap, og=og.ap),
          dict(xs_k=st0, in_k=st1, xs=st0.ap[:, 0:8 * NB].rearrange("p (a b) -> p a b", b=NB),
               hT=st1b[:, 0:8 * NB].rearrange("p (a b) -> p a b", b=NB),
               om=st1b[:, 8 * NB:12 * NB].rearrange("p (a b) -> p a b", b=NB),
               og=st1b[:, 12 * NB:16 * NB].rearrange("p (a b) -> p a b", b=NB))]

    def load_inputs(bi):
        t0, n, grp = MBLOCKS[bi]
        bs = BS[bi % 2]
        P.dma("sp", bs["xs"][:, :, 0:n], wview(xT)[:, :, t0:t0 + n], w=[bs["xs_k"]])
        P.dma("pool", bs["hT"][:, :, 0:n], wview(hT_d)[:, :, t0:t0 + n], w=[bs["in_k"]])
        P.dma("pool", bs["om"][:, :, 0:n], oT3[0][:, :, t0:t0 + n], w=[bs["in_k"]])
        P.dma("pool", bs["og"][:, :, 0:n], oT3[1][:, :, t0:t0 + n], w=[bs["in_k"]])

    def back(bi, nxt):
        t0, n, grp = MBLOCKS[bi]
        ys = FB[bi % 2]["ys"]
        if bi + 1 < len(MBLOCKS):
            load_inputs(bi + 1)
        bs = BS[bi % 2]
        xs_k, in_k = bs["xs_k"], bs["in_k"]
        xs_ap, hT_ap, om_ap, og_ap = bs["xs"], bs["hT"], bs["om"], bs["og"]

        def merge_oc(oc):
            gt = gate2[oc % 2]
            for b, (w_, src_, sk_) in enumerate(((w_co, ys.ap, ys), (w_mo, om_ap, in_k), (w_go, og_ap, in_k))):
                gbk = banks[2 + ring[0] % 6]
                ring[0] += 1
                c0 = b * 1024 + oc * 128
                for kc in range(8):
                    P.op("pe", lambda e, kc=kc, gbk=gbk, c0=c0: e.matmul(gbk.ap[:, 0:n], lhsT=w_g.ap[:, kc, c0:c0 + 128], rhs=hT_ap[:, kc, 0:n],
                                                                     start=(kc == 0), stop=(kc == 7)), r=[w_g, in_k], w=[gbk])
                P.op("act", lambda e, b=b, gbk=gbk: e.activation(out=gt[b].ap[:, 0:n], in_=gbk.ap[:, 0:n], func=AF.Sigmoid,
                                                                 bias=bg_t.ap[:, b * 8 + oc:b * 8 + oc + 1]), r=[gbk, bg_t], w=[gt[b]])
                bbk = banks[2 + ring[0] % 6]
                ring[0] += 1
                for ch in range(4):
                    P.op("pe", lambda e, ch=ch, bbk=bbk, w_=w_, src_=src_: e.matmul(bbk.ap[:, 0:n], lhsT=w_.ap[:, ch, oc * 128:(oc + 1) * 128],
                                                                                rhs=src_[:, ch, 0:n], start=(ch == 0), stop=(ch == 3)),
                         r=[w_, sk_], w=[bbk])
                dst = macc if b == 0 else mtmp[b - 1]
                P.op("dve", lambda e, b=b, bbk=bbk, dst=dst: e.tensor_tensor(out=dst.ap[:, 0:n], in0=bbk.ap[:, 0:n], in1=gt[b].ap[:, 0:n], op=ALU.mult),
                     r=[bbk, gt[b]], w=[dst])
            P.op("dve", lambda e: e.tensor_tensor(out=macc.ap[:, 0:n], in0=macc.ap[:, 0:n], in1=mtmp[0].ap[:, 0:n], op=ALU.add),
                 r=[macc, mtmp[0]], w=[macc])
            P.op("dve", lambda e: e.tensor_tensor(out=merged.ap[:, oc, 0:n], in0=macc.ap[:, 0:n], in1=mtmp[1].ap[:, 0:n], op=ALU.add),
                 r=[macc, mtmp[1]], w=[merged])

        for oc_ in range(8):
            merge_oc(oc_)
            if nxt:
                nxt.pop(0)()
        while nxt:
            nxt.pop(0)()

        def outproj_oc(oc):
            bk = banks[oc % 2]
            for kc in range(8):
                P.op("pe", lambda e, kc=kc: e.matmul(bk.ap[:, 0:n], lhsT=w_o.ap[:, kc, oc * 128:(oc + 1) * 128], rhs=merged.ap[:, kc, 0:n],
                                                     start=(kc == 0), stop=(kc == 7)), r=[w_o, merged], w=[bk])
            P.op("dve", lambda e: e.scalar_tensor_tensor(out=xm.ap[:, oc, 0:n], in0=bk.ap[:, 0:n], scalar=G1.ap[:, oc, grp:grp + 1],
                                                         in1=xs_ap[:, oc, 0:n], op0=ALU.mult, op1=ALU.add),
                 r=[bk, G1, xs_k], w=[xm])

        for oc_ in range(8):
            outproj_oc(oc_)
        P.dma(None, wview(xm_o)[:, :, t0:t0 + n], xm.ap[:, :, 0:n], r=[xm])
        emit_norm_mod(C, xm, n, ones_bf, A2, B2, grp, h2, ntmp, sqb2, banks[2], rstd)
        P.dma(None, wview(h2_o)[:, :, t0:t0 + n], h2.ap[:, :, 0:n], r=[h2])

    load_inputs(0)
    for f_ in front_pieces(0):
        f_()
    for bi in range(len(MBLOCKS)):
        nxt = front_pieces(bi + 1) if bi + 1 < len(MBLOCKS) else []
        back(bi, nxt)
    return C.finish(), C


def build_ffn():
    C = Ctx()
    xm_d = C.inp("xmT", [D, TT])
    h2l = C.inp("h2l", [D, TL + 2], BF16)
    h2c = C.inp("h2c", [D, CTX + 2], BF16)
    modv = C.inp("modv", [128, 6, 8, 2])
    w_up = C.inp("w_up", [D, 2 * D_FF])
    w_down = C.inp("w_down", [D_FF, D])
    fwc = C.inp("fwc", [128, 44, 4])
    x_o = C.out("xT_out", [D, TT])
    P = C.start()
    C.cv_engs = ("pool", "act", "dve")
    C.nst = 2
    banks = C.banks
    modc = C.load_f32("modc", modv, [6, 8, 2])
    G2 = T(modc.ap[:, 5], "G2")
    fw_t = C.load_f32("fw_t", fwc, [44, 4])
    wu = C.load_bf16("wu", wview(w_up), 8, 2 * D_FF)
    wd = C.load_bf16("wd", wview(w_down), 22, D)
    h2 = P.sb("h2", [8, 514], BF16)
    xmc = [P.sb(f"xmc{i}", [512], F32) for i in range(2)]
    act_t = P.sb("act_t", [22, 512], BF16)
    ta = [P.sb(f"ta{i}", [512], F32) for i in range(2)]
    tg = [P.sb(f"tg{i}", [512], F32) for i in range(2)]
    sgt = [P.sb(f"sgt{i}", [512], F32) for i in range(2)]
    cnt = [0]

    def conv_chunk(cc, n, dst):
        i = cnt[0]
        cnt[0] += 1
        bk = banks[2 + 2 * (i % 3)]
        bh = banks[3 + 2 * (i % 3)]
        for kc in range(8):
            P.op("pe", lambda e, kc=kc: e.matmul(bk.ap[:, 0:n], lhsT=wu.ap[:, kc, cc * 128:(cc + 1) * 128], rhs=h2.ap[:, kc, 1:n + 1],
                                                 start=(kc == 0), stop=(kc == 7)), r=[wu, h2], w=[bk])
        for kc in range(8):
            P.op("pe", lambda e, kc=kc: e.matmul(bh.ap[:, 0:2], lhsT=wu.ap[:, kc, cc * 128:(cc + 1) * 128], rhs=h2.ap[:, kc, 0:n + 2:n + 1],
                                                 start=(kc == 0), stop=(kc == 7)), r=[wu, h2], w=[bh])
        P.op("act", lambda e: e.activation(out=dst.ap[:, 0:n], in_=bk.ap[:, 0:n], func=AF.Identity, scale=fw_t.ap[:, cc, 1:2],
                                           bias=fw_t.ap[:, cc, 3:4]), r=[bk, fw_t], w=[dst])
        P.op("dve", lambda e: e.scalar_tensor_tensor(out=dst.ap[:, 1:n], in0=bk.ap[:, 0:n - 1], scalar=fw_t.ap[:, cc, 0:1], in1=dst.ap[:, 1:n],
                                                     op0=ALU.mult, op1=ALU.add), r=[bk, fw_t, dst], w=[dst])
        P.op("dve", lambda e: e.scalar_tensor_tensor(out=dst.ap[:, 0:n - 1], in0=bk.ap[:, 1:n], scalar=fw_t.ap[:, cc, 2:3], in1=dst.ap[:, 0:n - 1],
                                                     op0=ALU.mult, op1=ALU.add), r=[bk, fw_t, dst], w=[dst])
        P.op("dve", lambda e: e.scalar_tensor_tensor(out=dst.ap[:, 0:1], in0=bh.ap[:, 0:1], scalar=fw_t.ap[:, cc, 0:1], in1=dst.ap[:, 0:1],
                                                     op0=ALU.mult, op1=ALU.add), r=[bh, fw_t, dst], w=[dst])
        P.op("dve", lambda e: e.scalar_tensor_tensor(out=dst.ap[:, n - 1:n], in0=bh.ap[:, 1:2], scalar=fw_t.ap[:, cc, 2:3], in1=dst.ap[:, n - 1:n],
                                                     op0=ALU.mult, op1=ALU.add), r=[bh, fw_t, dst], w=[dst])

    def do_block(t0, n, grp):
        hsrc = h2l if grp == 0 else h2c
        h0 = t0 if grp == 0 else 0
        P.dma("sp", h2.ap[:, :, 0:n + 2], wview(hsrc)[:, :, h0:h0 + n + 2], w=[h2])
        for j in range(22):
            a_, g_, s_ = ta[j % 2], tg[j % 2], sgt[j % 2]
            conv_chunk(22 + j, n, g_)
            conv_chunk(j, n, a_)
            P.op("act", lambda e, g_=g_, s_=s_: e.activation(out=s_.ap[:, 0:n], in_=g_.ap[:, 0:n], func=AF.Silu), r=[g_], w=[s_])
            P.op("pool", lambda e, j=j, a_=a_, s_=s_: e.tensor_tensor(out=act_t.ap[:, j, 0:n], in0=a_.ap[:, 0:n], in1=s_.ap[:, 0:n], op=ALU.mult),
                 r=[a_, s_], w=[act_t])

        def down_oc(oc):
            bk = banks[oc % 2]
            xc = xmc[oc % 2]
            P.dma("pool", xc.ap[:, 0:n], xm_d[oc * 128:(oc + 1) * 128, t0:t0 + n], w=[xc])
            for j in range(22):
                P.op("pe", lambda e, j=j: e.matmul(bk.ap[:, 0:n], lhsT=wd.ap[:, j, oc * 128:(oc + 1) * 128], rhs=act_t.ap[:, j, 0:n],
                                                   start=(j == 0), stop=(j == 21)), r=[wd, act_t], w=[bk])
            P.op("dve", lambda e: e.scalar_tensor_tensor(out=xc.ap[:, 0:n], in0=bk.ap[:, 0:n], scalar=G2.ap[:, oc, grp:grp + 1],
                                                         in1=xc.ap[:, 0:n], op0=ALU.mult, op1=ALU.add),
                 r=[bk, G2, xc], w=[xc])
            P.dma("sp", x_o[oc * 128:(oc + 1) * 128, t0:t0 + n], xc.ap[:, 0:n], r=[xc])

        for oc_ in range(8):
            down_oc(oc_)

    for (t0_, n_, grp_) in BLOCKS:
        do_block(t0_, n_, grp_)
    return C.finish(), C


def _cols(v, n=128):
    v = np.asarray(v, np.float32)
    return np.ascontiguousarray(v.reshape(-1, n).T)


def _rope_tables():
    tabs = []
    tok = np.arange(SEQ)
    row = (tok // 64).astype(np.float32)
    col = (tok % 64).astype(np.float32)

    def mk(rot_dim, off):
        nf = rot_dim // 4
        inv = (1.0 / (10000.0 ** (np.arange(nf, dtype=np.float32) / nf))).astype(np.float32)
        ar = row[:, None] * inv
        ac = col[:, None] * inv
        ang = np.concatenate([ar, ar, ac, ac], axis=-1)
        cos = np.ones((96, SEQ), np.float32)
        sin = np.zeros((96, SEQ), np.float32)
        cos[off:off + rot_dim] = np.cos(ang).T
        sin[off:off + rot_dim] = np.sin(ang).T
        return cos, sin

    cm, sm = mk(32, 64)
    cg, sg = mk(64, 0)
    for c in range(NCORES):
        sl = slice(c * TL, (c + 1) * TL)
        M = np.zeros((96, 2, TT), np.float32)
        G = np.zeros((96, 2, TT), np.float32)
        M[:, 0, :TL] = cm[:, sl]; M[:, 1, :TL] = sm[:, sl]; M[:, 0, TL:] = 1.0
        G[:, 0, :TL] = cg[:, sl]; G[:, 1, :TL] = sg[:, sl]; G[:, 0, TL:] = 1.0
        tabs.append((M, G))
    rot = np.zeros((96, 2, 96), np.float32)
    for i in range(8):
        rot[72 + i, 0, 64 + i] = -1.0; rot[64 + i, 0, 72 + i] = 1.0
        rot[88 + i, 0, 80 + i] = -1.0; rot[80 + i, 0, 88 + i] = 1.0
    for i in range(16):
        rot[16 + i, 1, i] = -1.0; rot[i, 1, 16 + i] = 1.0
        rot[48 + i, 1, 32 + i] = -1.0; rot[32 + i, 1, 48 + i] = 1.0
    shift = np.zeros((32, 96), np.float32)
    for i in range(32):
        shift[i, 64 + i] = 1.0
    return tabs, rot.astype(NPBF), shift.astype(NPBF)


_PROGS = {}


def _prog(name):
    if name not in _PROGS:
        _PROGS[name] = {"pre": build_pre, "attn": build_attn, "ffn": build_ffn}[name]()[0]
    return _PROGS[name]


def _run(name, in_maps):
    res = run_bass_kernel_spmd(_prog(name), in_maps, core_ids=list(range(NCORES)))
    return res.results


def kernel(**inp):
    f32 = np.float32
    x = np.asarray(inp["x"], f32)[0]
    ctx = np.asarray(inp["ctx"], f32)[0]
    tabs, rot, shift = _rope_tables()
    c2 = np.ascontiguousarray(np.stack([_cols(inp["c"][0]), _cols(inp["c_ctx"])], axis=-1))
    xT = [np.ascontiguousarray(np.concatenate([x[c * TL:(c + 1) * TL], ctx], axis=0).T) for c in range(NCORES)]
    for l in range(DEPTH):
        hg = np.zeros((96, 4), f32)
        hg[:, 0] = inp["g_mla_q"][l]; hg[:, 1] = inp["g_mla_k"][l]
        hg[:64, 2] = inp["g_gqa_q"][l]; hg[:64, 3] = inp["g_gqa_k"][l]
        w_in_l = np.ascontiguousarray(inp["w_in"][l], dtype=f32)
        pre_in = []
        for c in range(NCORES):
            pre_in.append(dict(
                xT=xT[c], c2=c2, w_mod=np.ascontiguousarray(inp["w_mod"][l], dtype=f32), b_modc=_cols(inp["b_mod"][l]),
                gn1c=_cols(inp["g_norm1"][l]), gn2c=_cols(inp["g_norm2"][l]), w_in=w_in_l,
                gqac=_cols(inp["g_q_a"][l]), gkvac=_cols(inp["g_kv_a"][l]),
                w_q_b=np.ascontiguousarray(inp["w_q_b"][l], dtype=f32), w_kv_b=np.ascontiguousarray(inp["w_kv_b"][l], dtype=f32),
                hgain=hg, ropeM=tabs[c][0], ropeG=tabs[c][1], rotm=rot, shiftm=shift))
        r1 = _run("pre", pre_in)
        kTf = np.concatenate([r1[0]["kT"][:, :, TL:]] + [r1[c]["kT"][:, :, :TL] for c in range(NCORES)], axis=2)
        vPf = np.concatenate([r1[0]["vP"][TL:]] + [r1[c]["vP"][:TL] for c in range(NCORES)], axis=0)
        vPf = np.ascontiguousarray(vPf.transpose(1, 0, 2))
        kTf = np.ascontiguousarray(kTf)
        yfull = np.concatenate([r1[c]["y"][:, :TL] for c in range(NCORES)], axis=1)
        ypad = np.concatenate([np.zeros((512, 15), f32), yfull, np.zeros((512, 15), f32)], axis=1)
        cvec = np.ascontiguousarray(np.stack([_cols(inp["conv_dw_b"][l]), _cols(inp["conv_ln_g"][l]), _cols(inp["conv_ln_b"][l])], axis=-1))
        cwc = np.ascontiguousarray(np.asarray(inp["conv_dw_w"][l], f32).T.reshape(4, 128, 31).transpose(1, 0, 2))
        at_in = []
        for c in range(NCORES):
            yc = np.concatenate([np.zeros((512, 15), f32), r1[c]["y"][:, TL:], np.zeros((512, 15), f32)], axis=1)
            at_in.append(dict(
                qT=r1[c]["qT"], kTf=kTf, vPf=vPf, hT=r1[c]["hT"], xT=xT[c],
                yh=np.ascontiguousarray(ypad[:, c * TL:c * TL + TL + 30]), yc=np.ascontiguousarray(yc),
                modv=r1[c]["modv"], w_in=w_in_l, bgc=_cols(inp["b_gate"][l]), cwc=cwc, cvec=cvec,
                w_conv_out=np.ascontiguousarray(inp["w_conv_out"][l], dtype=f32), w_mla_o=np.ascontiguousarray(inp["w_mla_o"][l], dtype=f32),
                w_gqa_o=np.ascontiguousarray(inp["w_gqa_o"][l], dtype=f32), w_out=np.ascontiguousarray(inp["w_out"][l], dtype=f32)))
        r2 = _run("attn", at_in)
        h2full = np.concatenate([r2[c]["h2T"][:, :TL] for c in range(NCORES)], axis=1)
        zc = np.zeros((D, 1), NPBF)
        h2pad = np.concatenate([zc, h2full, zc], axis=1)
        fw = np.asarray(inp["ffn_dw_w"][l], f32)
        fwc = np.ascontiguousarray(np.stack([_cols(fw[0]), _cols(fw[1]), _cols(fw[2]), _cols(inp["ffn_dw_b"][l])], axis=-1))
        ff_in = []
        for c in range(NCORES):
            h2c = np.concatenate([zc, r2[c]["h2T"][:, TL:], zc], axis=1)
            ff_in.append(dict(
                xmT=r2[c]["xmT"], h2l=np.ascontiguousarray(h2pad[:, c * TL:c * TL + TL + 2]), h2c=np.ascontiguousarray(h2c),
                modv=r1[c]["modv"], w_up=np.ascontiguousarray(inp["w_up"][l], dtype=f32),
                w_down=np.ascontiguousarray(inp["w_down"][l], dtype=f32), fwc=fwc))
        r3 = _run("ffn", ff_in)
        xT = [r3[c]["xT_out"] for c in range(NCORES)]
    out = np.concatenate([xT[c][:, :TL].T for c in range(NCORES)], axis=0)[None]
    return np.ascontiguousarray(out.astype(np.float32))
```
